# Optimizing a Trainium2 kernel written in Bass

```python
import jax
import jax.numpy as jnp
from jax import lax
import numpy as np


D_MODEL = 2048
BATCH = 2
SEQ = 4096
DEPTH = 4

N_META = 16
BLOCK = 128
PAD = BLOCK - N_META
EPS = 1e-6
NEG_INF = -1e30
ROPE_BASE = 10000.0
HALF_STEP = 0.5

SWA_HEADS = 16
SWA_KV_HEADS = 2
SWA_HEAD_DIM = 64
SWA_GROUP = SWA_HEADS // SWA_KV_HEADS
WINDOW = 128

MLA_HEADS = 8
MLA_Q_RANK = 512
MLA_KV_RANK = 512
MLA_NOPE_DIM = 128
MLA_ROPE_DIM = 64
MLA_V_DIM = 128

RET_HEADS = 4
RET_KEY_DIM = 128
RET_VAL_DIM = 256

D_FF = 5632
N_BRANCH = 3

SWA_OUT = SWA_HEADS * SWA_HEAD_DIM
SWA_KV_W = SWA_KV_HEADS * SWA_HEAD_DIM
MLA_OUT = MLA_HEADS * MLA_V_DIM
RET_QK_W = RET_HEADS * RET_KEY_DIM
RET_OUT = RET_HEADS * RET_VAL_DIM
IN_WIDTHS = (SWA_OUT, SWA_KV_W, SWA_KV_W, MLA_Q_RANK, MLA_KV_RANK, MLA_ROPE_DIM,
             RET_QK_W, RET_QK_W, RET_OUT, RET_OUT, N_BRANCH * D_MODEL)
IN_WIDTH = sum(IN_WIDTHS)

kernel_name = 'hybrid_swa_mla_retention_macaron'


def _rms_norm(x, g):
    xf = x.astype(jnp.float32)
    y = xf * lax.rsqrt(jnp.mean(xf * xf, axis=-1, keepdims=True) + EPS)
    return (y * g.astype(jnp.float32)).astype(x.dtype)


def _rope(x, pos):
    half = x.shape[-1] // 2
    inv_freq = ROPE_BASE ** (-jnp.arange(half, dtype=jnp.float32) / half)
    ang = pos[:, None] * inv_freq[None, :]
    cos = jnp.cos(ang)[None, :, None, :]
    sin = jnp.sin(ang)[None, :, None, :]
    xf = x.astype(jnp.float32)
    x1, x2 = xf[..., :half], xf[..., half:]
    return jnp.concatenate([x1 * cos - x2 * sin, x2 * cos + x1 * sin], axis=-1).astype(x.dtype)


def _swiglu(x, w_gate, w_up, w_down):
    return (jax.nn.silu(x @ w_gate) * (x @ w_up)) @ w_down


def _sliding_window_attention(q, k, v, sinks):
    B, L = q.shape[0], q.shape[1]
    nb = L // BLOCK
    qb = q.reshape(B, nb, BLOCK, SWA_KV_HEADS, SWA_GROUP, SWA_HEAD_DIM)
    kb = k.reshape(B, nb, BLOCK, SWA_KV_HEADS, SWA_HEAD_DIM)
    vb = v.reshape(B, nb, BLOCK, SWA_KV_HEADS, SWA_HEAD_DIM)
    shift = ((0, 0), (1, 0), (0, 0), (0, 0), (0, 0))
    k_prev = jnp.pad(kb[:, :-1], shift)
    v_prev = jnp.pad(vb[:, :-1], shift)
    meta_shape = (B, nb, N_META, SWA_KV_HEADS, SWA_HEAD_DIM)
    k_meta = jnp.broadcast_to(k[:, None, PAD:BLOCK], meta_shape)
    v_meta = jnp.broadcast_to(v[:, None, PAD:BLOCK], meta_shape)
    keys = jnp.concatenate([k_meta, k_prev, kb], axis=2)
    vals = jnp.concatenate([v_meta, v_prev, vb], axis=2)
    s = jnp.einsum('bnqhgd,bnkhd->bnhgqk', qb, keys).astype(jnp.float32) * (SWA_HEAD_DIM ** -0.5)
    q_pos = jnp.arange(nb * BLOCK).reshape(nb, BLOCK)
    k_pos = jnp.arange(nb)[:, None] * BLOCK - BLOCK + jnp.arange(2 * BLOCK)[None, :]
    meta_pos = PAD + jnp.arange(N_META)
    diff_w = q_pos[:, :, None] - k_pos[:, None, :]
    win_ok = (diff_w >= 0) & (diff_w < WINDOW) & (k_pos[:, None, :] >= PAD)
    meta_ok = (q_pos[:, :, None] - meta_pos[None, None, :]) >= WINDOW
    mask = jnp.concatenate([meta_ok, win_ok], axis=-1)
    s = jnp.where(mask[None, :, None, None], s, NEG_INF)
    sink = jnp.broadcast_to(sinks.astype(jnp.float32).reshape(1, 1, SWA_KV_HEADS, SWA_GROUP, 1, 1),
                            s.shape[:-1] + (1,))
    p = jax.nn.softmax(jnp.concatenate([s, sink], axis=-1), axis=-1)[..., :-1]
    o = jnp.einsum('bnhgqk,bnkhd->bnqhgd', p.astype(v.dtype), vals)
    return o.reshape(B, L, SWA_OUT)


def _mla(c_q, c_kv, k_rope, q_a_norm, w_uq, kv_a_norm, w_ukv, qn_norm, qr_norm, kn_norm, kr_norm, pos):
    B, L = c_q.shape[0], c_q.shape[1]
    nb = L // BLOCK
    q = (_rms_norm(c_q, q_a_norm) @ w_uq).reshape(B, L, MLA_HEADS, MLA_NOPE_DIM + MLA_ROPE_DIM)
    kv = (_rms_norm(c_kv, kv_a_norm) @ w_ukv).reshape(B, L, MLA_HEADS, MLA_NOPE_DIM + MLA_V_DIM)
    q_nope = _rms_norm(q[..., :MLA_NOPE_DIM], qn_norm)
    q_rope = _rope(_rms_norm(q[..., MLA_NOPE_DIM:], qr_norm), pos)
    k_nope = _rms_norm(kv[..., :MLA_NOPE_DIM], kn_norm)
    v = kv[..., MLA_NOPE_DIM:]
    k_r = _rope(_rms_norm(k_rope, kr_norm)[:, :, None, :], pos)[:, :, 0, :]
    scale = (MLA_NOPE_DIM + MLA_ROPE_DIM) ** -0.5
    k_idx = jnp.arange(L)
    qn_b = q_nope.reshape(B, nb, BLOCK, MLA_HEADS, MLA_NOPE_DIM).transpose(1, 0, 2, 3, 4)
    qr_b = q_rope.reshape(B, nb, BLOCK, MLA_HEADS, MLA_ROPE_DIM).transpose(1, 0, 2, 3, 4)
    qi_b = k_idx.reshape(nb, BLOCK)

    def q_block(args):
        qn, qr, qi = args
        s = (jnp.einsum('bqhd,bkhd->bhqk', qn, k_nope)
             + jnp.einsum('bqhd,bkd->bhqk', qr, k_r)).astype(jnp.float32) * scale
        ok = (k_idx[None, :] <= qi[:, None]) & (k_idx[None, :] >= PAD)
        p = jax.nn.softmax(jnp.where(ok[None, None], s, NEG_INF), axis=-1)
        return jnp.einsum('bhqk,bkhd->bqhd', p.astype(v.dtype), v)

    o = lax.map(q_block, (qn_b, qr_b, qi_b))
    return o.transpose(1, 0, 2, 3, 4).reshape(B, L, MLA_OUT)


def _retention(q, k, v):
    B, L, H, dk = q.shape
    dv = v.shape[-1]
    nc = L // BLOCK
    log_gamma = jnp.log(1.0 - 2.0 ** (-5.0 - jnp.arange(H, dtype=jnp.float32)))
    idx = jnp.arange(BLOCK, dtype=jnp.float32)
    qc = q.reshape(B, nc, BLOCK, H, dk)
    kc = k.reshape(B, nc, BLOCK, H, dk)
    vc = v.reshape(B, nc, BLOCK, H, dv)
    diff = idx[:, None] - idx[None, :]
    decay = jnp.where(diff[None] >= 0,
                      jnp.exp(jnp.maximum(diff, 0.0)[None] * log_gamma[:, None, None]), 0.0)
    s = jnp.einsum('bnqhd,bnkhd->bnhqk', qc, kc) * decay
    inner = jnp.einsum('bnhqk,bnkhe->bnqhe', s, vc)
    zeta = jnp.exp((BLOCK - 1.0 - idx)[None, :] * log_gamma[:, None])
    kv_chunk = jnp.einsum('bnkhd,hk,bnkhe->nbhde', kc, zeta, vc)
    chunk_decay = jnp.exp(BLOCK * log_gamma)[None, :, None, None]

    def step(state, kv):
        return state * chunk_decay + kv, state

    _, prev = lax.scan(step, jnp.zeros((B, H, dk, dv), jnp.float32), kv_chunk)
    xi = jnp.exp((idx + 1.0)[None, :] * log_gamma[:, None])
    cross = jnp.einsum('bnqhd,nbhde,hq->bnqhe', qc, prev, xi)
    return (inner + cross).reshape(B, L, H, dv)


def _group_norm(o, g):
    B, L = o.shape[0], o.shape[1]
    mu = jnp.mean(o, axis=-1, keepdims=True)
    var = jnp.mean(jnp.square(o - mu), axis=-1, keepdims=True)
    y = ((o - mu) * lax.rsqrt(var + EPS)).reshape(B, L, -1)
    return y * g.astype(jnp.float32)


def _hybrid_mixer(hn, pos, w_in, swa_q_norm, swa_k_norm, swa_sinks,
                  mla_q_a_norm, mla_w_uq, mla_kv_a_norm, mla_w_ukv,
                  mla_qn_norm, mla_qr_norm, mla_kn_norm, mla_kr_norm,
                  ret_gn, w_br_swa, w_br_mla, w_br_ret, w_o):
    B, L, _ = hn.shape
    z = hn @ w_in
    splits = np.cumsum(np.array(IN_WIDTHS))[:-1].tolist()
    (swa_q, swa_k, swa_v, mla_cq, mla_ckv, mla_kr,
     ret_q, ret_k, ret_v, ret_g, gate_pre) = jnp.split(z, splits, axis=-1)

    qa = _rms_norm(swa_q.reshape(B, L, SWA_HEADS, SWA_HEAD_DIM), swa_q_norm)
    ka = _rms_norm(swa_k.reshape(B, L, SWA_KV_HEADS, SWA_HEAD_DIM), swa_k_norm)
    va = swa_v.reshape(B, L, SWA_KV_HEADS, SWA_HEAD_DIM)
    o_a = _sliding_window_attention(qa, ka, va, swa_sinks)

    o_b = _mla(mla_cq, mla_ckv, mla_kr, mla_q_a_norm, mla_w_uq, mla_kv_a_norm, mla_w_ukv,
               mla_qn_norm, mla_qr_norm, mla_kn_norm, mla_kr_norm, pos)

    valid = (jnp.arange(L) >= PAD).astype(jnp.float32)[None, :, None, None]
    qc = _rope(ret_q.reshape(B, L, RET_HEADS, RET_KEY_DIM), pos).astype(jnp.float32)
    kc = _rope(ret_k.reshape(B, L, RET_HEADS, RET_KEY_DIM), pos).astype(jnp.float32) * (RET_KEY_DIM ** -0.5) * valid
    vc = ret_v.reshape(B, L, RET_HEADS, RET_VAL_DIM).astype(jnp.float32)
    o_c = (_group_norm(_retention(qc, kc, vc), ret_gn) * jax.nn.silu(ret_g.astype(jnp.float32))).astype(hn.dtype)

    g = jax.nn.sigmoid(gate_pre.astype(jnp.float32)).astype(hn.dtype).reshape(B, L, N_BRANCH, D_MODEL)
    merged = (g[:, :, 0] * (o_a @ w_br_swa) + g[:, :, 1] * (o_b @ w_br_mla)
              + g[:, :, 2] * (o_c @ w_br_ret))
    return merged @ w_o


def setup_inputs(seed: int = 0) -> dict:
    key = jax.random.key(seed)
    k = jax.random.split(key, 32)
    f32 = jnp.float32

    def w(i, shape, fan_in):
        return jax.random.normal(k[i], shape, f32) * (fan_in ** -0.5)

    def gain(i, shape):
        return 1.0 + 0.02 * jax.random.normal(k[i], shape, f32)

    return {
        'x': jax.random.normal(k[0], (BATCH, SEQ, D_MODEL), f32),
        'meta_tokens': jax.random.normal(k[1], (N_META, D_MODEL), f32),
        'ffn1_norm': gain(2, (DEPTH, D_MODEL)),
        'ffn1_w_gate': w(3, (DEPTH, D_MODEL, D_FF), D_MODEL),
        'ffn1_w_up': w(4, (DEPTH, D_MODEL, D_FF), D_MODEL),
        'ffn1_w_down': w(5, (DEPTH, D_FF, D_MODEL), D_FF),
        'mix_norm': gain(6, (DEPTH, D_MODEL)),
        'w_in': w(7, (DEPTH, D_MODEL, IN_WIDTH), D_MODEL),
        'swa_q_norm': gain(8, (DEPTH, SWA_HEAD_DIM)),
        'swa_k_norm': gain(9, (DEPTH, SWA_HEAD_DIM)),
        'swa_sinks': 0.5 * jax.random.normal(k[10], (DEPTH, SWA_HEADS), f32),
        'mla_q_a_norm': gain(11, (DEPTH, MLA_Q_RANK)),
        'mla_w_uq': w(12, (DEPTH, MLA_Q_RANK, MLA_HEADS * (MLA_NOPE_DIM + MLA_ROPE_DIM)), MLA_Q_RANK),
        'mla_kv_a_norm': gain(13, (DEPTH, MLA_KV_RANK)),
        'mla_w_ukv': w(14, (DEPTH, MLA_KV_RANK, MLA_HEADS * (MLA_NOPE_DIM + MLA_V_DIM)), MLA_KV_RANK),
        'mla_qn_norm': gain(15, (DEPTH, MLA_NOPE_DIM)),
        'mla_qr_norm': gain(16, (DEPTH, MLA_ROPE_DIM)),
        'mla_kn_norm': gain(17, (DEPTH, MLA_NOPE_DIM)),
        'mla_kr_norm': gain(18, (DEPTH, MLA_ROPE_DIM)),
        'ret_gn': gain(19, (DEPTH, RET_OUT)),
        'w_br_swa': w(20, (DEPTH, SWA_OUT, D_MODEL), SWA_OUT),
        'w_br_mla': w(21, (DEPTH, MLA_OUT, D_MODEL), MLA_OUT),
        'w_br_ret': w(22, (DEPTH, RET_OUT, D_MODEL), RET_OUT),
        'w_o': w(23, (DEPTH, D_MODEL, D_MODEL), D_MODEL),
        'ffn2_norm': gain(24, (DEPTH, D_MODEL)),
        'ffn2_w_gate': w(25, (DEPTH, D_MODEL, D_FF), D_MODEL),
        'ffn2_w_up': w(26, (DEPTH, D_MODEL, D_FF), D_MODEL),
        'ffn2_w_down': w(27, (DEPTH, D_FF, D_MODEL), D_FF),
    }


def reference(x, meta_tokens, ffn1_norm, ffn1_w_gate, ffn1_w_up, ffn1_w_down, mix_norm, w_in,
              swa_q_norm, swa_k_norm, swa_sinks, mla_q_a_norm, mla_w_uq, mla_kv_a_norm, mla_w_ukv,
              mla_qn_norm, mla_qr_norm, mla_kn_norm, mla_kr_norm, ret_gn, w_br_swa, w_br_mla,
              w_br_ret, w_o, ffn2_norm, ffn2_w_gate, ffn2_w_up, ffn2_w_down):
    B = x.shape[0]
    meta = jnp.broadcast_to(meta_tokens[None].astype(x.dtype), (B, N_META, D_MODEL))
    pad = jnp.zeros((B, PAD, D_MODEL), x.dtype)
    h = jnp.concatenate([pad, meta, x], axis=1)
    L = h.shape[1]
    pos = (jnp.arange(L) - PAD).astype(jnp.float32)
    for l in range(DEPTH):
        h = h + HALF_STEP * _swiglu(_rms_norm(h, ffn1_norm[l]), ffn1_w_gate[l], ffn1_w_up[l], ffn1_w_down[l])
        h = h + _hybrid_mixer(_rms_norm(h, mix_norm[l]), pos, w_in[l], swa_q_norm[l], swa_k_norm[l],
                              swa_sinks[l], mla_q_a_norm[l], mla_w_uq[l], mla_kv_a_norm[l], mla_w_ukv[l],
                              mla_qn_norm[l], mla_qr_norm[l], mla_kn_norm[l], mla_kr_norm[l], ret_gn[l],
                              w_br_swa[l], w_br_mla[l], w_br_ret[l], w_o[l])
        h = h + HALF_STEP * _swiglu(_rms_norm(h, ffn2_norm[l]), ffn2_w_gate[l], ffn2_w_up[l], ffn2_w_down[l])
    return h[:, BLOCK:]
```

```python
import math
import contextlib
import numpy as np
import ml_dtypes
import concourse.bass as bass
import concourse.mybir as mybir
from concourse.bass_utils import run_bass_kernel_spmd

F32 = mybir.dt.float32
BF16 = mybir.dt.bfloat16
F32R = mybir.dt.float32r
AF = mybir.ActivationFunctionType
ALU = mybir.AluOpType

NEG = -30000.0
EPS = 1e-6
ROPE_BASE = 10000.0
N_META = 16
BLK = 128


class Cfg:
    def __init__(self, D=2048, DFF=5632, DEPTH=4, NB=8, SWA_H=16, SWA_KVH=2, MLA_H=8, QR=512, KVR=512,
                 RET_H=4, FG=11, GS=4, NBATCH=2):
        self.D, self.DFF, self.DEPTH, self.NB = D, DFF, DEPTH, NB
        self.SWA_H, self.SWA_KVH, self.SWA_D = SWA_H, SWA_KVH, 64
        self.MLA_H, self.QR, self.KVR = MLA_H, QR, KVR
        self.NOPE, self.ROPE, self.VD = 128, 64, 128
        self.RET_H, self.DK, self.DV = RET_H, 128, 256
        self.FG, self.GS, self.NBATCH = FG, GS, NBATCH
        self.NCORE = GS * NBATCH
        self.DC = D // 128
        self.FC = DFF // 128
        assert self.FC % FG == 0
        self.S = NB * BLK
        self.T = N_META + self.S
        self.SWA_OUT = SWA_H * 64
        self.SWA_KVW = SWA_KVH * 64
        self.MLA_OUT = MLA_H * 128
        self.RET_QK = RET_H * 128
        self.RET_OUT = RET_H * 256
        self.QRC, self.KVC = QR // 128, KVR // 128
        w = [self.SWA_OUT, self.SWA_KVW, self.SWA_KVW, QR, KVR, 64, self.RET_QK, self.RET_QK,
             self.RET_OUT, self.RET_OUT, 3 * D]
        names = ["swa_q", "swa_k", "swa_v", "cq", "ckv", "kr", "ret_q", "ret_k", "ret_v", "ret_g", "gate"]
        self.off = {}
        o = 0
        for n_, w_ in zip(names, w):
            self.off[n_] = o
            o += w_
        self.IN_W = o
        self.QB = min(4, NB)
        self.NQT = NB // self.QB
        assert NB % self.QB == 0
        nt = -(-self.T // 512)
        base = -(-self.T // nt)
        base = -(-base // 8) * 8
        self.TT = []
        s = 0
        while s < self.T:
            n = min(base, self.T - s)
            self.TT.append((s, n))
            s += n
        self.X2A_ROWS = 64 + (2 * 128 * 128 + 128 * 128) // self.S
        assert (3 * 128 * 128) % self.S == 0


FULL = Cfg()


def cf_layout(cfg):
    lay = {}
    o = 0

    def add(name, n):
        nonlocal o
        lay[name] = (o, n)
        o += n
    L = cfg.DEPTH
    add("g_ffn1", L * cfg.DC)
    add("g_mix", L * cfg.DC)
    add("g_ffn2", L * cfg.DC)
    add("g_qa", L * cfg.QRC)
    add("g_kva", L * cfg.KVC)
    add("g_qn", L)
    add("g_kn", L)
    add("g_qr", L)
    add("g_qrs", L)
    add("g_kr", L)
    add("g_krs", L)
    add("g_sq", L)
    add("g_sk", L)
    add("g_gn", L * (cfg.RET_OUT // 128))
    add("sinks", L * (cfg.SWA_H // 2))
    add("zeta", cfg.RET_H)
    add("zeta_m", cfg.RET_H)
    add("zeta_w", cfg.RET_H * cfg.NB)
    add("retc", cfg.RET_H * (1 + cfg.GS))
    add("vis", cfg.GS)
    add("selp", cfg.GS)
    add("zero", 1)
    add("ones", 128)
    add("bd64", 128)
    add("xi", cfg.RET_H * 128)
    add("xi_m", cfg.RET_H * 16)
    add("dp", cfg.RET_H * 128)
    add("dp_m", cfg.RET_H * 16)
    lay["_n"] = o
    return lay


def cb_layout(cfg):
    lay = {}
    o = 0

    def add(name, n):
        nonlocal o
        lay[name] = (o, n)
        o += n
    add("ident", 128)
    add("ones", 128)
    add("cmcur", 512)
    add("cmprev", 512)
    add("mm", cfg.QB * cfg.QB * 128)
    lay["_n"] = o
    return lay


def _gammas(cfg):
    h = np.arange(cfg.RET_H, dtype=np.float64)
    return 1.0 - 2.0 ** (-5.0 - h)


def host_consts(cfg, inputs, j):
    lay = cf_layout(cfg)
    cf = np.zeros((128, lay["_n"]), np.float32)
    L = cfg.DEPTH

    def put(name, arr):
        o, n = lay[name]
        arr = np.asarray(arr, np.float32)
        assert arr.shape[1] == n, (name, arr.shape, n)
        cf[:arr.shape[0], o:o + n] = arr

    def chunks(v):
        v = np.asarray(v, np.float32)
        Lq, W = v.shape
        return v.reshape(Lq, W // 128, 128).transpose(2, 0, 1).reshape(128, -1)
    put("g_ffn1", chunks(inputs["ffn1_norm"]))
    put("g_mix", chunks(inputs["mix_norm"]))
    put("g_ffn2", chunks(inputs["ffn2_norm"]))
    put("g_qa", chunks(inputs["mla_q_a_norm"]))
    put("g_kva", chunks(inputs["mla_kv_a_norm"]))
    put("g_qn", np.asarray(inputs["mla_qn_norm"], np.float32).T)
    put("g_kn", np.asarray(inputs["mla_kn_norm"], np.float32).T)
    qr = np.asarray(inputs["mla_qr_norm"], np.float32).T
    kr = np.asarray(inputs["mla_kr_norm"], np.float32).T
    put("g_qr", qr)
    put("g_qrs", np.concatenate([qr[32:], qr[:32]], 0))
    put("g_kr", kr)
    put("g_krs", np.concatenate([kr[32:], kr[:32]], 0))
    sq = np.asarray(inputs["swa_q_norm"], np.float32).T
    sk = np.asarray(inputs["swa_k_norm"], np.float32).T
    put("g_sq", np.concatenate([sq, sq], 0))
    put("g_sk", np.concatenate([sk, sk], 0))
    put("g_gn", chunks(inputs["ret_gn"]))
    sinks = np.asarray(inputs["swa_sinks"], np.float32)
    sk2 = np.zeros((128, L * (cfg.SWA_H // 2)), np.float32)
    for l in range(L):
        for c in range(cfg.SWA_H // 2):
            sk2[:64, l * (cfg.SWA_H // 2) + c] = sinks[l, 2 * c]
            sk2[64:, l * (cfg.SWA_H // 2) + c] = sinks[l, 2 * c + 1]
    put("sinks", sk2)
    g = _gammas(cfg)
    lg = np.log(g)
    k = np.arange(128, dtype=np.float64)
    sc = cfg.DK ** -0.5
    put("zeta", np.exp((127.0 - k)[:, None] * lg[None, :]) * sc)
    zm = np.zeros((128, cfg.RET_H))
    zm[:16] = np.exp((15.0 - np.arange(16.0))[:, None] * lg[None, :]) * sc
    put("zeta_m", zm)
    zw = np.zeros((128, cfg.RET_H, cfg.NB))
    for b_ in range(cfg.NB):
        zw[:, :, b_] = np.exp((127.0 - k)[:, None] * lg[None, :] + 128.0 * (cfg.NB - 1 - b_) * lg[None, :]) * sc
    put("zeta_w", zw.reshape(128, -1))
    G = np.exp(128.0 * lg)
    retc = np.zeros((128, cfg.RET_H, 1 + cfg.GS))
    for h in range(cfg.RET_H):
        retc[:, h, 0] = G[h] ** (cfg.NB * j)
        for i in range(cfg.GS):
            retc[:, h, 1 + i] = G[h] ** (cfg.NB * (j - 1 - i)) if i < j else 0.0
    put("retc", retc.reshape(128, -1))
    vis = np.zeros((128, cfg.GS))
    for i in range(cfg.GS):
        vis[:, i] = 0.0 if i < j else NEG
    put("vis", vis)
    selp = np.full((128, cfg.GS), NEG)
    if j > 0:
        selp[:, j - 1] = 0.0
    put("selp", selp)
    put("zero", np.zeros((128, 1)))
    put("ones", np.ones((128, 128)))
    bd = np.zeros((128, 128))
    bd[:64, :64] = 1
    bd[64:, 64:] = 1
    put("bd64", bd)
    q = np.arange(128, dtype=np.float64)
    xi = np.exp((q + 1.0)[None, :] * lg[:, None])
    put("xi", np.broadcast_to(xi.reshape(1, -1), (128, cfg.RET_H * 128)))
    qm = np.arange(16, dtype=np.float64)
    xim = np.exp((113.0 + qm)[None, :] * lg[:, None])
    put("xi_m", np.broadcast_to(xim.reshape(1, -1), (128, cfg.RET_H * 16)))
    dp = np.zeros((128, cfg.RET_H, 128))
    dpm = np.zeros((128, cfg.RET_H, 16))
    for h in range(cfg.RET_H):
        dp[:, h, :] = np.exp(-(k + 1.0) * lg[h])[:, None] * sc * (q[None, :] >= k[:, None])
        dpm[:16, h, :] = (np.exp(-(113.0 + qm) * lg[h])[:, None] * sc * (qm[None, :] >= qm[:, None]))
    put("dp", dp.reshape(128, -1))
    put("dp_m", dpm.reshape(128, -1))

    layb = cb_layout(cfg)
    cb = np.zeros((128, layb["_n"]), np.float32)

    def putb(name, arr):
        o, n = layb[name]
        assert arr.shape == (128, n), (name, arr.shape)
        cb[:, o:o + n] = arr
    putb("ident", np.eye(128))
    putb("ones", np.ones((128, 128)))
    kk = np.arange(128)[:, None]
    qq = np.arange(128)[None, :]
    cur = np.where(qq >= kk, 0.0, NEG)
    prev = np.where(kk > qq, 0.0, NEG)
    putb("cmcur", np.tile(cur, (1, 4)))
    putb("cmprev", np.tile(prev, (1, 4)))
    mm = np.zeros((128, cfg.QB, cfg.QB, 128))
    for r in range(cfg.QB):
        for bb in range(cfg.QB):
            mm[:, r, bb, :] = NEG if bb < r else (cur if bb == r else 0.0)
    putb("mm", mm.reshape(128, -1))
    cb = cb.astype(ml_dtypes.bfloat16)

    pos = np.concatenate([np.arange(16.0), 16.0 + j * cfg.S + np.arange(cfg.S, dtype=np.float64)])
    T = cfg.T

    def tables(dim):
        half = dim // 2
        inv = ROPE_BASE ** (-np.arange(half, dtype=np.float64) / half)
        ang = (pos[None, :].astype(np.float32) * inv[:, None].astype(np.float32)).astype(np.float32)
        c = np.cos(ang.astype(np.float64))
        s = np.sin(ang.astype(np.float64))
        return np.concatenate([c, c], 0), np.concatenate([-s, s], 0)
    cr, sr = tables(128)
    cm, sm = tables(64)
    rope = np.zeros((128, 4, T), np.float32)
    rope[:, 0] = cr
    rope[:, 1] = sr
    rope[:64, 2] = cm
    rope[:64, 3] = sm
    return cf, cb, rope.reshape(128, 4 * T)


ENGS = ["pe", "act", "dve", "pool", "sp"]


class Buf:
    __slots__ = ("ap", "w", "r", "name", "psum")

    def __init__(self, ap, name="", psum=False):
        self.ap = ap
        self.w = None
        self.r = {}
        self.name = name
        self.psum = psum


class Prog:
    def __init__(self, nc, stack, semreg):
        self.nc = nc
        self.stack = stack
        self.streams = {e: [] for e in ENGS}
        self.sems = semreg
        self.cnt = {}
        self.seen = {e: {} for e in ENGS}
        self.dbg = []
        for e in ENGS:
            self.newsem("done_" + e)

    def newsem(self, key):
        if key not in self.sems:
            self.sems[key] = self.stack.enter_context(self.nc.semaphore(key))
        if key not in self.cnt:
            self.cnt[key] = 0
        return key

    def _waits(self, eng, reads, writes):
        ev = {}

        def add(k, v):
            if ev.get(k, 0) < v:
                ev[k] = v
        for b in reads:
            if b.w is not None:
                add(*b.w)
            if b.psum:
                for k, v in b.r.items():
                    if k != "done_" + eng:
                        add(k, v)
        for b in writes:
            if b.w is not None:
                add(*b.w)
            for k, v in b.r.items():
                add(k, v)
        out = []
        seen = self.seen[eng]
        for k, v in ev.items():
            if eng == "pe" and k == "done_pe":
                continue
            if seen.get(k, 0) >= v:
                continue
            seen[k] = v
            out.append((k, v))
        return out

    def _finish(self, evt, reads, writes):
        k, v = evt
        for b in reads:
            if b.r.get(k, 0) < v:
                b.r[k] = v
        for b in writes:
            b.w = evt
            b.r = {}

    def op(self, eng, fn, reads=(), writes=()):
        waits = self._waits(eng, reads, writes)
        key = "done_" + eng
        self.cnt[key] += 1
        evt = (key, self.cnt[key])
        self.streams[eng].append((waits, [fn], (key, 1)))
        self.dbg.append((eng, evt, waits, [b.name for b in reads], [b.name for b in writes]))
        self._finish(evt, reads, writes)
        return evt

    def pe(self, fns, reads=(), writes=()):
        waits = self._waits("pe", reads, writes)
        key = "done_pe"
        self.cnt[key] += 1
        evt = (key, self.cnt[key])
        self.streams["pe"].append((waits, list(fns), (key, 1)))
        self.dbg.append(("pe", evt, waits, [b.name for b in reads], [b.name for b in writes]))
        self._finish(evt, reads, writes)
        return evt

    def dma(self, eng, pairs, sem, reads=(), writes=(), **kw):
        self.newsem(sem)
        waits = self._waits(eng, reads, writes)
        fns = []
        for (o, i) in pairs:
            fns.append((lambda e, o=o, i=i: e.dma_start(out=o, in_=i, **kw)))
        self.cnt[sem] += 16 * len(pairs)
        evt = (sem, self.cnt[sem])
        self.streams[eng].append((waits, fns, (sem, 16, "each")))
        self.dbg.append((eng + ":dma", evt, waits, [b.name for b in reads], [b.name for b in writes]))
        self._finish(evt, reads, writes)
        return evt

    def collective(self, fn, sem, reads=(), writes=()):
        self.newsem(sem)
        waits = self._waits("pool", reads, writes)
        self.cnt[sem] += 1
        evt = (sem, self.cnt[sem])
        self.streams["pool"].append((waits, [fn], (sem, 1)))
        self._finish(evt, reads, writes)
        return evt

    def barrier(self):
        tot = [(k, v) for k, v in self.cnt.items() if v > 0 and not (k.startswith("w") or k.startswith("cc") or k == "done_pool")]
        self.dbg.append(("BARRIER", None, tot, [], []))
        for e in ENGS:
            if e == "pool":
                continue
            waits = []
            for k, v in tot:
                if k == "done_" + e:
                    continue
                if self.seen[e].get(k, 0) >= v:
                    continue
                self.seen[e][k] = v
                waits.append((k, v))
            if waits:
                self.streams[e].append((waits, [], None))

    def wait_on(self, eng, evts):
        waits = []
        for k, v in evts:
            if self.seen[eng].get(k, 0) < v:
                self.seen[eng][k] = v
                waits.append((k, v))
        if waits:
            self.streams[eng].append((waits, [], None))

    def emit(self):
        nc = self.nc
        sems = self.sems
        streams = self.streams

        def run(e, ops):
            for waits, fns, inc in ops:
                for k, v in waits:
                    e.wait_ge(sems[k], v)
                n = len(fns)
                for idx, fn in enumerate(fns):
                    ins = fn(e)
                    if inc is None:
                        continue
                    if len(inc) == 3 or idx == n - 1:
                        ins.then_inc(sems[inc[0]], inc[1])
        with nc.Block() as block:
            @block.tensor
            def _(e):
                run(e, streams["pe"])

            @block.scalar
            def _(e):
                run(e, streams["act"])

            @block.vector
            def _(e):
                run(e, streams["dve"])

            @block.gpsimd
            def _(e):
                run(e, streams["pool"])

            @block.sync
            def _(e):
                run(e, streams["sp"])


class Arena:
    def __init__(self, ap, words):
        self.ap = ap
        self.words = words
        self.top = 0
        self.peak = 0

    def mark(self):
        return self.top

    def release(self, m):
        self.top = m

    def _alloc(self, n):
        n = (n + 1) // 2 * 2
        o = self.top
        self.top += n
        self.peak = max(self.peak, self.top)
        assert self.top <= self.words, ("arena overflow", self.top, self.words)
        return o

    def f32(self, n, dt=F32):
        o = self._alloc(n)
        v = self.ap[:, o:o + n]
        return v if dt == F32 else v.bitcast(dt)

    def bf16(self, n):
        w = (n + 1) // 2
        o = self._alloc(w)
        return self.ap[:, o:o + w].bitcast(BF16)[:, 0:n]


NSLOT = 6
AHEAD = 4


class _Stop(Exception):
    pass


def build_program(cfg):
    nc = bass.Bass("TRN2", target_bir_lowering=False)
    stack = contextlib.ExitStack()
    semreg = {}
    C = cfg
    L, D, DC, T, S = C.DEPTH, C.D, C.DC, C.T, C.S
    lay = cf_layout(C)
    layb = cb_layout(C)

    def din(name, shape, dt=F32):
        return nc.dram_tensor(name, list(shape), dt, kind="ExternalInput").ap()
    xT = din("xT", [D, S])
    metaT = din("metaT", [D, N_META])
    cf_d = din("cf", [128, lay["_n"]])
    cb_d = din("cb", [128, layb["_n"]], BF16)
    rope_d = din("rope", [128, 4 * T])
    W = {}
    for nm, shp in weight_shapes(C):
        W[nm] = din(nm, shp)
    yT = nc.dram_tensor("yT", [D, S], F32, kind="ExternalOutput").ap()
    HW = DC * T
    hsp_d = nc.dram_tensor("hspill", [128, HW], F32).ap()
    X = []
    for l in range(L):
        X.append(dict(
            x1s=nc.dram_tensor(f"x1s{l}", [C.KVR, S], BF16),
            x1d=nc.dram_tensor(f"x1d{l}", [C.GS * C.KVR, S], BF16),
            x2s=nc.dram_tensor(f"x2s{l}", [C.X2A_ROWS, S], BF16),
            x2d=nc.dram_tensor(f"x2d{l}", [C.GS * C.X2A_ROWS, S], BF16),
            x3s=nc.dram_tensor(f"x3s{l}", [128, C.RET_H * 256], F32),
            x3d=nc.dram_tensor(f"x3d{l}", [C.GS * 128, C.RET_H * 256], F32),
        ))
    groups = [[b * C.GS + i for i in range(C.GS)] for b in range(C.NBATCH)]

    HNW = DC * ((T + 1) // 2)
    SLOTW = 16 * 64
    NSINK = lay["sinks"][1]
    KW = lay["_n"] + (layb["_n"] + 1) // 2 + NSINK + 4 + 5 * 512 + 32
    NMAXC = max(C.SWA_OUT, C.MLA_OUT, C.RET_OUT) // 128
    CWORDS = max(DC * ((T + 1) // 2) + NMAXC * ((T + 1) // 2), C.FG * ((T + 1) // 2)) + 64
    RW = 128 + 128 + 3 * 512
    TOTAL = 211000 // 4 - RW
    HWA = TOTAL - HNW - NSLOT * SLOTW - KW - CWORDS
    assert HWA >= HW, ("SBUF too small", HWA, HW)
    arena_t = stack.enter_context(nc.sbuf_tensor("arena", [128, TOTAL], F32))
    psum_t = stack.enter_context(nc.psum_tensor("ps", [128, 8, 512], F32))
    r_t = stack.enter_context(nc.sbuf_tensor("f32r", [128, RW], F32R))
    o = 0
    hn_ap = arena_t[:, o:o + HNW].bitcast(BF16).rearrange("p (c t) -> p c t", c=DC)[:, :, 0:T]
    o += HNW
    H_OFF = o
    h_ap = arena_t[:, o:o + HW].rearrange("p (c t) -> p c t", c=DC)
    h_flat = arena_t[:, o:o + HW]
    o += HWA
    slot_aps = []
    for i in range(NSLOT):
        slot_aps.append(arena_t[:, o:o + SLOTW].bitcast(BF16).rearrange("p (k n) -> p k n", k=16))
        o += SLOTW
    K_OFF = o
    o += KW
    C_OFF = o
    o += CWORDS
    assert o <= TOTAL

    def generate(script):
        P = Prog(nc, stack, semreg)
        AH = Arena(arena_t[:, H_OFF:H_OFF + HWA], HWA)
        AK = Arena(arena_t[:, K_OFF:K_OFF + KW], KW)
        AC = Arena(arena_t[:, C_OFF:C_OFF + CWORDS], CWORDS)
        AR = Arena(r_t[:, 0:RW], RW)
        PS = [Buf(psum_t[:, i, :], f"ps{i}", psum=True) for i in range(8)]
        cf_ap = AK.f32(lay["_n"])
        cb_ap = AK.bf16(layb["_n"])
        onesR = AR.f32(128)
        bdR = AR.f32(128)
        esink = AK.f32(NSINK)
        eps_t = AK.f32(2)
        CONST = Buf(None, "const")

        def cfc(name, i=0, n=1, rows=128):
            o_, _ = lay[name]
            return cf_ap[0:rows, o_ + i:o_ + i + n]

        def cbc(name, i=0, n=128, rows=128):
            o_, _ = layb[name]
            return cb_ap[0:rows, o_ + i:o_ + i + n]

        P.dma("sp", [(cf_ap, cf_d), (cb_ap, cb_d)], "ld_c", writes=[CONST])
        P.op("dve", lambda e: e.tensor_copy(out=onesR, in_=cfc("ones", 0, 128)), reads=[CONST], writes=[CONST])
        P.op("dve", lambda e: e.tensor_copy(out=bdR, in_=cfc("bd64", 0, 128)), reads=[CONST], writes=[CONST])
        P.op("dve", lambda e: e.memset(eps_t, EPS), writes=[CONST])
        P.op("act", lambda e: e.activation(out=esink, in_=cfc("sinks", 0, NSINK), func=AF.Exp), reads=[CONST], writes=[CONST])
        ident = cbc("ident")
        onesb = cbc("ones")

        TT = C.TT
        NT = len(TT)
        hB = [[Buf(None, f"h{c}_{t}") for t in range(NT)] for c in range(DC)]
        hnB = [[Buf(None, f"hn{c}_{t}") for t in range(NT)] for c in range(DC)]
        allh = [hB[c][t] for c in range(DC) for t in range(NT)]

        def hsl(c, ti):
            s, n = TT[ti]
            return h_ap[:, c, s:s + n]

        def hnsl(c, ti):
            s, n = TT[ti]
            return hn_ap[:, c, s:s + n]

        xv = xT.rearrange("(c p) t -> p c t", p=128)
        mv = metaT.rearrange("(c p) t -> p c t", p=128)
        step = max(1, DC // 4)
        pairs = [(h_ap[:, :, 0:N_META], mv)]
        for c0 in range(0, DC, step):
            pairs.append((h_ap[:, c0:c0 + step, N_META:T], xv[:, c0:c0 + step, :]))
        P.dma("sp", pairs, "ld_x", writes=allh)

        class PSM:
            def __init__(self):
                self.free = list(range(8))
                self.rr = 0

            def reserve(self, n):
                got = self.free[:n]
                self.free = self.free[n:]
                return [PS[i] for i in got]

            def unreserve(self, bufs):
                for b in bufs:
                    self.free.append(PS.index(b))
                self.free.sort()

            def next(self):
                self.rr = (self.rr + 1) % len(self.free)
                return PS[self.free[self.rr]]
        psm = PSM()

        slots = [Buf(slot_aps[i], f"slot{i}") for i in range(NSLOT)]

        class WQ:
            def __init__(self):
                self.plans = []
                self.slot_of = {}
                self.free = list(range(NSLOT))
                self.next_issue = 0

            def plan(self, loader):
                self.plans.append(loader)
                return len(self.plans) - 1

            def _src(self):
                return script if script is not None else self.plans

            def _issue(self):
                idx = self.next_issue
                assert self.free, "weight slots exhausted (too many held)"
                si = self.free.pop(0)
                sb = slots[si]
                P.dma("pool", self._src()[idx](sb.ap), f"w{si}", writes=[sb])
                self.slot_of[idx] = si
                self.next_issue += 1

            def get(self, idx):
                while self.next_issue <= idx:
                    self._issue()
                src = self._src()
                while self.free and self.next_issue < len(src) and self.next_issue <= idx + AHEAD:
                    self._issue()
                return slots[self.slot_of[idx]]

            def done(self, *idxs):
                for idx in idxs:
                    self.free.append(self.slot_of.pop(idx))
        wq = WQ()

        def wcols(wd, l, kc0, nkc, col0, ncol):
            src = wd[l].rearrange("(k p) n -> p k n", p=128)[:, kc0:kc0 + nkc, col0:col0 + ncol]
            return lambda s: [(s[:, 0:nkc, 0:ncol], src)]

        def mm(out, lhsT, rhs, start, stop):
            return lambda e: e.matmul(out, lhsT=lhsT, rhs=rhs, start=start, stop=stop)

        def i_act(out, in_, func, bias=None, scale=None):
            kw = {}
            if bias is not None:
                kw["bias"] = bias
            if scale is not None:
                kw["scale"] = scale
            return lambda e: e.activation(out=out, in_=in_, func=func, **kw)

        def i_stt(out, in0, scalar, in1, op0, op1):
            return lambda e: e.scalar_tensor_tensor(out=out, in0=in0, scalar=scalar, in1=in1, op0=op0, op1=op1)

        def i_tt(out, in0, in1, op):
            return lambda e: e.tensor_tensor(out=out, in0=in0, in1=in1, op=op)

        def i_ts(out, in0, s1, op0):
            return lambda e: e.tensor_scalar(out=out, in0=in0, scalar1=s1, scalar2=None, op0=op0)

        def i_cp(out, in_):
            return lambda e: e.tensor_copy(out=out, in_=in_)

        def i_recip(out, in_):
            return lambda e: e.reciprocal(out=out, in_=in_)

        def rot(n, w, dt=F32, name="r"):
            tiles = [(AR.f32(w) if dt == F32R else AK.f32(w, dt)) for _ in range(n)]
            bufs = [Buf(tiles[i], f"{name}{i}") for i in range(n)]
            st = {"i": 0}

            def nxt():
                st["i"] = (st["i"] + 1) % n
                return bufs[st["i"]]
            return nxt
        nextsq = rot(3, 512, F32R, "sq")
        nextrs = rot(2, 512, F32, "rs")
        nexttmp = rot(3, 512, F32, "tmp")

        def emit_rstd(stat_ps, rows, n, rs, dim):
            P.op("act", i_act(rs.ap[0:rows, 0:n], stat_ps.ap[0:rows, 0:n], AF.Ln, bias=eps_t[0:rows, 0:1], scale=1.0 / dim),
                 reads=[stat_ps, CONST], writes=[rs])
            P.op("act", i_act(rs.ap[0:rows, 0:n], rs.ap[0:rows, 0:n], AF.Exp, scale=-0.5), reads=[rs], writes=[rs])

        def rmsnorm_tile(gname, l, ti):
            s, n = TT[ti]
            st = psm.next()
            for c in range(DC):
                sq = nextsq()
                P.op("act", i_act(sq.ap[:, 0:n], hsl(c, ti), AF.Square), reads=[hB[c][ti]], writes=[sq])
                P.pe([mm(st.ap[:, 0:n], onesR, sq.ap[:, 0:n], c == 0, c == DC - 1)], reads=[sq, CONST], writes=[st])
            rs = nextrs()
            emit_rstd(st, 128, n, rs, D)
            for c in range(DC):
                P.op("dve", i_stt(hnsl(c, ti), hsl(c, ti), cfc(gname, l * DC + c), rs.ap[:, 0:n], ALU.mult, ALU.mult),
                     reads=[hB[c][ti], rs, CONST], writes=[hnB[c][ti]])

        def rmsnorm_h(gname, l):
            for ti in range(NT):
                rmsnorm_tile(gname, l, ti)

        def ffn(l, wg, wu, wd, gname):
            rmsnorm_tile(gname, l, 0)
            lazy = {"todo": True}
            m = AC.mark()
            FG = C.FG
            a_ap = AC.bf16(FG * T).rearrange("p (f t) -> p f t", f=FG)
            aB = [[Buf(None, f"a{f}_{t}") for t in range(NT)] for f in range(FG)]
            plan = []
            for g in range(C.FC // FG):
                for fl in range(FG):
                    fc = g * FG + fl
                    plan.append(("gu", fl, wq.plan(wcols(wg, l, 0, DC, fc * 128, 128)), wq.plan(wcols(wu, l, 0, DC, fc * 128, 128))))
                for dc in range(DC):
                    plan.append(("dn", dc, wq.plan(wcols(wd, l, g * FG, FG, dc * 128, 128)), None))
            for kind, x_, w1, w2 in plan:
                if kind == "gu":
                    fl = x_
                    sg_ = wq.get(w1)
                    su_ = wq.get(w2)
                    for ti, (s, n) in enumerate(TT):
                        if lazy["todo"] and ti > 0:
                            rmsnorm_tile(gname, l, ti)
                        pg = psm.next()
                        pu = psm.next()
                        rb = [hnB[kc][ti] for kc in range(DC)]
                        P.pe([mm(pg.ap[:, 0:n], sg_.ap[:, kc, :], hnsl(kc, ti), kc == 0, kc == DC - 1) for kc in range(DC)], reads=[sg_] + rb, writes=[pg])
                        P.pe([mm(pu.ap[:, 0:n], su_.ap[:, kc, :], hnsl(kc, ti), kc == 0, kc == DC - 1) for kc in range(DC)], reads=[su_] + rb, writes=[pu])
                        tb = nexttmp()
                        P.op("act", i_act(tb.ap[:, 0:n], pg.ap[:, 0:n], AF.Silu), reads=[pg], writes=[tb])
                        P.op("dve", i_tt(a_ap[:, fl, s:s + n], tb.ap[:, 0:n], pu.ap[:, 0:n], ALU.mult), reads=[tb, pu], writes=[aB[fl][ti]])
                    lazy["todo"] = False
                    wq.done(w1, w2)
                else:
                    dc = x_
                    sd_ = wq.get(w1)
                    for ti, (s, n) in enumerate(TT):
                        po = psm.next()
                        P.pe([mm(po.ap[:, 0:n], sd_.ap[:, fl, :], a_ap[:, fl, s:s + n], fl == 0, fl == FG - 1) for fl in range(FG)],
                             reads=[sd_] + [aB[fl][ti] for fl in range(FG)], writes=[po])
                        P.op("dve", i_stt(hsl(dc, ti), po.ap[:, 0:n], 0.5, hsl(dc, ti), ALU.mult, ALU.add), reads=[po, hB[dc][ti]], writes=[hB[dc][ti]])
                    wq.done(w1)
            P.barrier()
            AC.release(m)

        def proj_fm(ps, n, wslot, m0, M, srcs, src_bufs):
            nk = len(srcs)
            P.pe([mm(ps.ap[0:M, 0:n], wslot.ap[:, kc, m0:m0 + M], srcs[kc], kc == 0, kc == nk - 1) for kc in range(nk)],
                 reads=[wslot] + list(src_bufs), writes=[ps])

        def hn_srcs(ti):
            return [hnsl(kc, ti) for kc in range(DC)], [hnB[kc][ti] for kc in range(DC)]

        def headnorm(ps, rows, n, onesM, dim, gain_ap, out_ap, out_buf):
            sq = nextsq()
            P.op("act", i_act(sq.ap[0:rows, 0:n], ps.ap[0:rows, 0:n], AF.Square), reads=[ps], writes=[sq])
            st = psm.next()
            P.pe([mm(st.ap[:, 0:n], onesM[0:rows, :], sq.ap[0:rows, 0:n], True, True)], reads=[sq, CONST], writes=[st])
            rs = nextrs()
            emit_rstd(st, rows, n, rs, dim)
            P.op("dve", i_stt(out_ap, ps.ap[0:rows, 0:n], gain_ap, rs.ap[0:rows, 0:n], ALU.mult, ALU.mult), reads=[ps, rs, CONST], writes=[out_buf])

        def blk(b):
            return N_META + b * BLK

        def tok_range(bi):
            return (0, N_META) if bi == 0 else (blk(bi - 1), BLK)

        def tiles_of(t0, n):
            return [ti for ti, (s, m_) in enumerate(TT) if s < t0 + n and t0 < s + m_]

        def proj_tm(ps, t0, nt, wslot, ncol, col0=0):
            rb = [hnB[kc][ti] for kc in range(DC) for ti in tiles_of(t0, nt)]
            P.pe([mm(ps.ap[0:nt, 0:ncol], hn_ap[:, kc, t0:t0 + nt], wslot.ap[:, kc, col0:col0 + ncol], kc == 0, kc == DC - 1) for kc in range(DC)],
                 reads=[wslot] + rb, writes=[ps])
        Gch = [float(x) for x in np.exp(128.0 * np.log(_gammas(C)))]

        def ck(name):
            if getattr(C, "stop", None) == name:
                raise _Stop()

        def dump(name, ap3, nch, bufs):
            if getattr(C, "stop", None) != name:
                return
            nch = min(nch, DC)
            yv_ = yT.rearrange("(c p) t -> p c t", p=128)
            ev_ = P.dma("pool", [(yv_[:, c, :], ap3[:, c, N_META:T]) for c in range(nch)], "st_y", reads=list(bufs), writes=[])
            P.wait_on("sp", [ev_])
            raise _Stop("dumped")

        def mixer(l):
            Xl = X[l]
            win = W["w_in"]
            off = C.off
            srcw = win[l].rearrange("(k p) n -> p k n", p=128)
            q4 = HW // 4
            P.dma("sp", [(hsp_d[:, q * q4:(q + 1) * q4], h_flat[:, q * q4:(q + 1) * q4]) for q in range(4)], "hsp", reads=allh, writes=[])
            rmsnorm_h("g_mix", l)
            P.barrier()
            ck("spill")
            mH = AH.mark()
            mC = AC.mark()
            merged = AC.bf16(DC * T).rearrange("p (c t) -> p c t", c=DC)
            mgB = [[Buf(None, f"mg{c}_{t}") for t in range(NT)] for c in range(DC)]
            ROPE = Buf(None, "rope")

            def load_rope(tabs):
                t_ = AH.f32(len(tabs) * T).rearrange("p (a t) -> p a t", a=len(tabs))
                P.dma("sp", [(t_[:, i, :], rope_d[:, tb * T:(tb + 1) * T]) for i, tb in enumerate(tabs)], "ld_r", writes=[ROPE])
                return t_

            def proj_rmsnorm(col0, nch, gname, gidx0, out3, outB):
                widx = [wq.plan(wcols(win, l, 0, DC, col0 + oc * 128, 128)) for oc in range(nch)]
                m_ = AH.mark()
                raw = AH.f32(nch * 512).rearrange("p (c n) -> p c n", c=nch)
                rawB = [Buf(None, "raw") for _ in range(nch)]
                for ti, (s, n) in enumerate(TT):
                    srcs, sb = hn_srcs(ti)
                    st = psm.next()
                    for oc in range(nch):
                        ws = wq.get(widx[oc])
                        ps = psm.next()
                        proj_fm(ps, n, ws, 0, 128, srcs, sb)
                        sq = nextsq()
                        P.op("act", i_act(sq.ap[:, 0:n], ps.ap[:, 0:n], AF.Square), reads=[ps], writes=[sq])
                        P.op("dve", i_ts(raw[:, oc, 0:n], ps.ap[:, 0:n], 1.0, ALU.mult), reads=[ps], writes=[rawB[oc]])
                        P.pe([mm(st.ap[:, 0:n], onesR, sq.ap[:, 0:n], oc == 0, oc == nch - 1)], reads=[sq, CONST], writes=[st])
                    rs = nextrs()
                    emit_rstd(st, 128, n, rs, nch * 128)
                    for oc in range(nch):
                        P.op("dve", i_stt(out3[:, oc, s:s + n], raw[:, oc, 0:n], cfc(gname, gidx0 + oc), rs.ap[:, 0:n], ALU.mult, ALU.mult),
                             reads=[rawB[oc], rs, CONST], writes=[outB[oc][ti]])
                wq.done(*widx)
                P.barrier()
                AH.release(m_)

            def rope_fm(wsrc3, nkc, srcs_fn, col0, rows, cos3, sin3, out_fn, out_buf_fn, gain=None, gain_s=None, norm=False):
                half = rows // 2
                wi = wq.plan(lambda s_: [(s_[:, 0:nkc, 0:rows], wsrc3[:, :, col0:col0 + rows])])
                wsi = wq.plan(lambda s_: [(s_[:, 0:nkc, 0:half], wsrc3[:, :, col0 + half:col0 + rows]),
                                          (s_[:, 0:nkc, half:rows], wsrc3[:, :, col0:col0 + half])])
                for ti, (s, n) in enumerate(TT):
                    srcs, sb = srcs_fn(ti)
                    px_ = psm.next()
                    ppx = psm.next()
                    proj_fm(px_, n, wq.get(wi), 0, rows, srcs, sb)
                    proj_fm(ppx, n, wq.get(wsi), 0, rows, srcs, sb)
                    t1 = nexttmp()
                    t2 = nexttmp()
                    cos_ap = cos3[0:rows, s:s + n]
                    sin_ap = sin3[0:rows, s:s + n]
                    if norm:
                        sq = nextsq()
                        P.op("act", i_act(sq.ap[0:rows, 0:n], px_.ap[0:rows, 0:n], AF.Square), reads=[px_], writes=[sq])
                        st = psm.next()
                        P.pe([mm(st.ap[:, 0:n], onesR[0:rows, :], sq.ap[0:rows, 0:n], True, True)], reads=[sq, CONST], writes=[st])
                        rs = nextrs()
                        emit_rstd(st, rows, n, rs, rows)
                        P.op("dve", i_stt(t1.ap[0:rows, 0:n], px_.ap[0:rows, 0:n], gain, cos_ap, ALU.mult, ALU.mult), reads=[px_, ROPE, CONST], writes=[t1])
                        P.op("dve", i_stt(t2.ap[0:rows, 0:n], ppx.ap[0:rows, 0:n], gain_s, sin_ap, ALU.mult, ALU.mult), reads=[ppx, ROPE, CONST], writes=[t2])
                        P.op("dve", i_tt(t1.ap[0:rows, 0:n], t1.ap[0:rows, 0:n], t2.ap[0:rows, 0:n], ALU.add), reads=[t1, t2], writes=[t1])
                        P.op("dve", i_tt(out_fn(s, n), t1.ap[0:rows, 0:n], rs.ap[0:rows, 0:n], ALU.mult), reads=[t1, rs], writes=[out_buf_fn(ti)])
                    else:
                        P.op("dve", i_tt(t1.ap[0:rows, 0:n], px_.ap[0:rows, 0:n], cos_ap, ALU.mult), reads=[px_, ROPE], writes=[t1])
                        P.op("dve", i_tt(t2.ap[0:rows, 0:n], ppx.ap[0:rows, 0:n], sin_ap, ALU.mult), reads=[ppx, ROPE], writes=[t2])
                        P.op("dve", i_tt(out_fn(s, n), t1.ap[0:rows, 0:n], t2.ap[0:rows, 0:n], ALU.add), reads=[t1, t2], writes=[out_buf_fn(ti)])
                wq.done(wi, wsi)

            RH = C.RET_H
            NKV = C.SWA_KVH
            VW = NKV * 64
            ckvn = AH.bf16(C.KVC * T).rearrange("p (c t) -> p c t", c=C.KVC)
            ckvnB = [[Buf(None, "ckvn") for _ in range(NT)] for _ in range(C.KVC)]
            NKALL = T + C.GS * S
            kr_all = AH.bf16(NKALL)
            krB = [Buf(None, "kr") for _ in range(NT)]
            Mst = AH.f32(RH * 256).rearrange("p (h e) -> p h e", h=RH)
            MB = [Buf(None, "M") for _ in range(RH)]
            mSWA = AH.mark()
            KK = AH.bf16(NKV * T).rearrange("p (g t) -> p g t", g=NKV)
            KKB = [[Buf(None, "KK") for _ in range(NT)] for _ in range(NKV)]
            Vs = AH.bf16((C.NB + 1) * VW).rearrange("p (b n) -> p b n", b=C.NB + 1)
            VsB = [Buf(None, "Vs") for _ in range(C.NB + 1)]
            KKp_all = AH.bf16(C.GS * NKV * 128).rearrange("p (i g n) -> p i g n", i=C.GS, g=NKV)
            Vp_all = AH.bf16(C.GS * VW).rearrange("p (i n) -> p i n", i=C.GS)
            PRV = Buf(None, "prev")
            mM1 = AH.mark()
            Lloc = AH.f32(RH * 256).rearrange("p (h e) -> p h e", h=RH)
            LB = [Buf(None, "L") for _ in range(RH)]
            rope_t = load_rope([0, 1, 2, 3])

            dump("hn", hn_ap, DC, [b_ for r_ in hnB for b_ in r_])
            ck("rope")
            proj_rmsnorm(off["ckv"], C.KVC, "g_kva", l * C.KVC, ckvn, ckvnB)
            ck("m1a")
            dump("ckvn", ckvn, C.KVC, [b_ for r_ in ckvnB for b_ in r_])
            rope_fm(srcw, DC, hn_srcs, off["kr"], 64, rope_t[:, 2, :], rope_t[:, 3, :], lambda s, n: kr_all[0:64, s:s + n], lambda ti: krB[ti],
                    gain=cfc("g_kr", l, rows=64), gain_s=cfc("g_krs", l, rows=64), norm=True)
            for g in range(NKV):
                c0 = off["swa_k"] + g * 64
                wi = wq.plan(lambda s_, c0=c0: [(s_[:, 0:DC, 0:64], srcw[:, :, c0:c0 + 64]), (s_[:, 0:DC, 64:128], srcw[:, :, c0:c0 + 64])])
                for ti, (s, n) in enumerate(TT):
                    srcs, sb = hn_srcs(ti)
                    ps = psm.next()
                    proj_fm(ps, n, wq.get(wi), 0, 128, srcs, sb)
                    headnorm(ps, 128, n, bdR, 64, cfc("g_sk", l), KK[:, g, s:s + n], KKB[g][ti])
                wq.done(wi)
            wv = wq.plan(wcols(win, l, 0, DC, off["swa_v"], VW))
            for bi in range(C.NB + 1):
                t0, nt = tok_range(bi)
                ps = psm.next()
                proj_tm(ps, t0, nt, wq.get(wv), VW)
                P.op("act", i_act(Vs[0:nt, bi, :], ps.ap[0:nt, 0:VW], AF.Copy), reads=[ps], writes=[VsB[bi]])
            wq.done(wv)

            ck("m1c")
            def ret_kv_head(h, want_q, ropeT, weighted=False):
                KT = AH.bf16(T)
                KTB = [Buf(None, "KT") for _ in range(NT)]
                rope_fm(srcw, DC, hn_srcs, off["ret_k"] + h * 128, 128, ropeT[:, 0, :], ropeT[:, 1, :], lambda s, n: KT[:, s:s + n], lambda ti: KTB[ti])
                QT = None
                QTB = None
                if want_q:
                    QT = AH.bf16(T)
                    QTB = [Buf(None, "QT") for _ in range(NT)]
                    rope_fm(srcw, DC, hn_srcs, off["ret_q"] + h * 128, 128, ropeT[:, 0, :], ropeT[:, 1, :], lambda s, n: QT[:, s:s + n], lambda ti: QTB[ti])
                Kz = AH.bf16((C.NB + 1) * 128).rearrange("p (b n) -> p b n", b=C.NB + 1)
                KzB = [Buf(None, "Kz") for _ in range(C.NB + 1)]
                Vr = AH.bf16((C.NB + 1) * 256).rearrange("p (b n) -> p b n", b=C.NB + 1)
                VrB = [Buf(None, "Vr") for _ in range(C.NB + 1)]
                wvv = wq.plan(wcols(win, l, 0, DC, off["ret_v"] + h * 256, 128))
                wvv2 = wq.plan(wcols(win, l, 0, DC, off["ret_v"] + h * 256 + 128, 128))
                for bi in range(C.NB + 1):
                    t0, nt = tok_range(bi)
                    pt = psm.next()
                    ptb = pt.ap.bitcast(BF16)
                    P.pe([lambda e, ptb=ptb, t0=t0, nt=nt, KT=KT: e.transpose(out=ptb[0:nt, 0:128], in_=KT[:, t0:t0 + nt], identity=ident)],
                         reads=[KTB[ti] for ti in tiles_of(t0, nt)] + [CONST], writes=[pt])
                    zc = cfc("zeta_m", h, rows=nt) if bi == 0 else (cfc("zeta_w", h * C.NB + bi - 1) if weighted else cfc("zeta", h))
                    P.op("dve", i_ts(Kz[0:nt, bi, :], ptb[0:nt, 0:128], zc, ALU.mult), reads=[pt, CONST], writes=[KzB[bi]])
                    for hf, wslot_i in enumerate((wvv, wvv2)):
                        pv = psm.next()
                        proj_tm(pv, t0, nt, wq.get(wslot_i), 128)
                        P.op("act", i_act(Vr[0:nt, bi, hf * 128:(hf + 1) * 128], pv.ap[0:nt, 0:128], AF.Copy), reads=[pv], writes=[VrB[bi]])
                wq.done(wvv, wvv2)
                return KT, KTB, QT, QTB, Kz, KzB, Vr, VrB

            def kv_update(state_ap, state_buf, Kz, KzB, Vr, VrB, bi, gamma_chunk, first):
                t0, nt = tok_range(bi)
                pk = psm.next()
                P.pe([mm(pk.ap[:, 0:256], Kz[0:nt, bi, :], Vr[0:nt, bi, :], True, True)], reads=[KzB[bi], VrB[bi]], writes=[pk])
                if first:
                    P.op("dve", i_cp(state_ap, pk.ap[:, 0:256]), reads=[pk], writes=[state_buf])
                else:
                    P.op("dve", i_stt(state_ap, state_ap, float(gamma_chunk), pk.ap[:, 0:256], ALU.mult, ALU.add), reads=[pk, state_buf], writes=[state_buf])
            for h in range(RH):
                m_ = AH.mark()
                KT, KTB, _, _, Kz, KzB, Vr, VrB = ret_kv_head(h, False, rope_t, weighted=True)
                kv_update(Mst[:, h, :], MB[h], Kz, KzB, Vr, VrB, 0, 0.0, True)
                pk = psm.next()
                P.pe([mm(pk.ap[:, 0:256], Kz[:, b + 1, :], Vr[:, b + 1, :], b == 0, b == C.NB - 1) for b in range(C.NB)],
                     reads=[KzB[b + 1] for b in range(C.NB)] + [VrB[b + 1] for b in range(C.NB)], writes=[pk])
                P.op("dve", i_cp(Lloc[:, h, :], pk.ap[:, 0:256]), reads=[pk], writes=[LB[h]])
                P.barrier()
                AH.release(m_)

            DX1s, DX1d, DX2s, DX2d, DX3s, DX3d = (Buf(None, "dx") for _ in range(6))
            lastb = C.NB
            P.dma("sp", [(Xl["x1s"].ap().rearrange("(c p) t -> p c t", p=128), ckvn[:, :, N_META:T])], "st_x1",
                  reads=[b_ for r_ in ckvnB for b_ in r_], writes=[DX1s])
            x2s = Xl["x2s"].ap()
            x2d = Xl["x2d"].ap()
            kkrows = 128 * 128 // S

            def x2rows(ap2, base, k):
                return ap2[base + k * kkrows:base + (k + 1) * kkrows, :].rearrange("r (q n) -> (r q) n", n=128)
            prs = [(x2s[0:64, :], kr_all[0:64, N_META:T])]
            for g in range(NKV):
                prs.append((x2rows(x2s, 64, g), KK[:, g, blk(C.NB - 1):blk(C.NB - 1) + BLK]))
            prs.append((x2rows(x2s, 64, NKV), Vs[:, lastb, :]))
            P.dma("sp", prs, "st_x2", reads=krB + [b_ for r_ in KKB for b_ in r_] + [VsB[lastb]], writes=[DX2s])
            P.dma("sp", [(Xl["x3s"].ap(), Lloc.rearrange("p h e -> p (h e)"))], "st_x3", reads=LB, writes=[DX3s])
            for ci, (s_, d_, bs, bd) in enumerate(((Xl["x1s"], Xl["x1d"], DX1s, DX1d), (Xl["x2s"], Xl["x2d"], DX2s, DX2d), (Xl["x3s"], Xl["x3d"], DX3s, DX3d))):
                P.collective(lambda e, s_=s_, d_=d_: e.collective_compute("AllGather", ALU.bypass, replica_groups=groups,
                                                                          ins=[s_.ap().opt()], outs=[d_.ap().opt()]),
                             f"cc{ci}", reads=[bs], writes=[bd])
            P.barrier()
            AH.release(mM1)
            KRG = Buf(None, "krg")
            P.dma("sp", [(kr_all[0:64, T + i * S:T + (i + 1) * S], x2d[i * C.X2A_ROWS:i * C.X2A_ROWS + 64, :]) for i in range(C.GS)],
                  "ld_g", reads=[DX2d], writes=[KRG])
            pr = []
            for i in range(C.GS):
                base = i * C.X2A_ROWS + 64
                for g in range(NKV):
                    pr.append((KKp_all[:, i, g, :], x2rows(x2d, base, g)))
                pr.append((Vp_all[:, i, :], x2rows(x2d, base, NKV)))
            P.dma("sp", pr, "ld_g2", reads=[DX2d], writes=[PRV])

            ck("xchg")

            def branch_merge(r, o3, oB, nkc, wbr):
                for dc in range(DC):
                    wg_i = wq.plan(wcols(win, l, 0, DC, off["gate"] + r * D + dc * 128, 128))
                    wb_i = wq.plan(wcols(wbr, l, 0, nkc, dc * 128, 128))
                    for ti, (s, n) in enumerate(TT):
                        srcs, sb = hn_srcs(ti)
                        pg = psm.next()
                        proj_fm(pg, n, wq.get(wg_i), 0, 128, srcs, sb)
                        py = psm.next()
                        proj_fm(py, n, wq.get(wb_i), 0, 128, [o3[:, kc, s:s + n] for kc in range(nkc)], [oB[kc][ti] for kc in range(nkc)])
                        tb = nexttmp()
                        P.op("act", i_act(tb.ap[:, 0:n], pg.ap[:, 0:n], AF.Sigmoid), reads=[pg], writes=[tb])
                        if r == 0:
                            P.op("dve", i_tt(merged[:, dc, s:s + n], tb.ap[:, 0:n], py.ap[:, 0:n], ALU.mult), reads=[tb, py], writes=[mgB[dc][ti]])
                        else:
                            P.op("dve", i_tt(tb.ap[:, 0:n], tb.ap[:, 0:n], py.ap[:, 0:n], ALU.mult), reads=[tb, py], writes=[tb])
                            P.op("dve", i_tt(merged[:, dc, s:s + n], tb.ap[:, 0:n], merged[:, dc, s:s + n], ALU.add),
                                 reads=[tb, mgB[dc][ti]], writes=[mgB[dc][ti]])
                    wq.done(wg_i, wb_i)

            mSC = AC.mark()
            NQC = C.SWA_H // 2
            o_a = AC.bf16(NQC * T).rearrange("p (c t) -> p c t", c=NQC)
            oaB = [[Buf(None, "oa") for _ in range(NT)] for _ in range(NQC)]
            Qs = AH.bf16(NQC * T).rearrange("p (c t) -> p c t", c=NQC)
            QsB = [[Buf(None, "Qs") for _ in range(NT)] for _ in range(NQC)]
            for c in range(NQC):
                wi = wq.plan(wcols(win, l, 0, DC, off["swa_q"] + c * 128, 128))
                for ti, (s, n) in enumerate(TT):
                    srcs, sb = hn_srcs(ti)
                    ps = psm.next()
                    proj_fm(ps, n, wq.get(wi), 0, 128, srcs, sb)
                    headnorm(ps, 128, n, bdR, 64, cfc("g_sq", l), Qs[:, c, s:s + n], QsB[c][ti])
                wq.done(wi)
            ptile = [AH.bf16(512) for _ in range(3)]
            ptB = [Buf(ptile[i], f"pt{i}") for i in range(3)]
            pti = {"i": 0}

            def nextpt():
                pti["i"] = (pti["i"] + 1) % 3
                return ptB[pti["i"]]
            fin_t = AH.f32(128)
            finB = Buf(fin_t, "fin")
            HPG = C.SWA_H // NKV
            quads = []
            for g in range(NKV):
                hs = list(range(g * HPG, (g + 1) * HPG))
                for q0 in range(0, len(hs), 4):
                    quads.append((g, hs[q0:q0 + 4]))
            po, pl = psm.reserve(2)

            spend = {"p": None}

            def swa_flush():
                if spend["p"] is None:
                    return
                (hp, par, vv, nk, nq, rb, pt, first, last) = spend["p"]
                spend["p"] = None
                fns = []
                for ii, (hq, hh) in enumerate(hp):
                    cs = (hq // 2) * 128
                    fns.append(mm(po.ap[par * 64:(par + 1) * 64, cs:cs + nq], vv, pt.ap[0:nk, ii * 128:ii * 128 + nq], first and ii == 0, last))
                P.pe(fns, reads=rb + [pt], writes=[po])
                fns = []
                for ii, (hq, hh) in enumerate(hp):
                    cs = (hq // 2) * 128
                    fns.append(mm(pl.ap[par * 64:(par + 1) * 64, cs:cs + nq], onesb[0:nk, 0:64], pt.ap[0:nk, ii * 128:ii * 128 + nq], first and ii == 0, last))
                P.pe(fns, reads=[pt, CONST], writes=[pl])

            def swa_block(bi):
                t0, nq = tok_range(bi)
                qtiles = tiles_of(t0, nq)
                for (g, hs) in quads:
                    mrd = [KKB[g][ti] for ti in tiles_of(0, N_META)] + [VsB[0]]
                    kts = []
                    if bi == 0:
                        kts.append((KK[:, g, 0:N_META], Vs[0:N_META, 0, g * 64:(g + 1) * 64], N_META, "cmcur", cfc("zero", 0, rows=N_META), mrd))
                    else:
                        kts.append((KK[:, g, 0:N_META], Vs[0:N_META, 0, g * 64:(g + 1) * 64], N_META, None, cfc("zero", 0, rows=N_META), mrd))
                        kts.append((KK[:, g, t0:t0 + BLK], Vs[:, bi, g * 64:(g + 1) * 64], BLK, "cmcur", cfc("zero"),
                                    [KKB[g][ti] for ti in qtiles] + [VsB[bi]]))
                        if bi >= 2:
                            p0 = blk(bi - 2)
                            kts.append((KK[:, g, p0:p0 + BLK], Vs[:, bi - 1, g * 64:(g + 1) * 64], BLK, "cmprev", cfc("zero"),
                                        [KKB[g][ti] for ti in tiles_of(p0, BLK)] + [VsB[bi - 1]]))
                        else:
                            for i in range(C.GS):
                                kts.append((KKp_all[:, i, g, :], Vp_all[:, i, g * 64:(g + 1) * 64], BLK, "cmprev", cfc("selp", i), [PRV]))
                    nh = len(hs)
                    qrd = [QsB[hh // 2][ti] for hh in hs for ti in qtiles]
                    for ki, (kT, vv, nk, mname, bias, rb) in enumerate(kts):
                        for par in range(2):
                            hp = [(hq, hh) for hq, hh in enumerate(hs) if hh % 2 == par]
                            if not hp:
                                continue
                            ncolp = len(hp) * 128
                            pss = psm.next()
                            fns = []
                            if mname is not None:
                                fns.append(mm(pss.ap[0:nk, 0:ncolp], ident[0:nk, 0:nk], cbc(mname, 0, ncolp, rows=nk), True, False))
                            for ii, (hq, hh) in enumerate(hp):
                                c = hh // 2
                                fns.append(mm(pss.ap[0:nk, ii * 128:ii * 128 + nq], kT[par * 64:(par + 1) * 64, :],
                                              Qs[par * 64:(par + 1) * 64, c, t0:t0 + nq], mname is None and ii == 0, ii == len(hp) - 1))
                            P.pe(fns, reads=rb + qrd + [CONST], writes=[pss])
                            pt = nextpt()
                            P.op("act", i_act(pt.ap[0:nk, 0:ncolp], pss.ap[0:nk, 0:ncolp], AF.Exp, bias=bias, scale=0.125), reads=[pss, CONST], writes=[pt])
                            swa_flush()
                            spend["p"] = (hp, par, vv, nk, nq, rb, pt, ki == 0, ki == len(kts) - 1)
                    swa_flush()
                    for cq in range((nh + 1) // 2):
                        c = hs[2 * cq] // 2
                        cs = cq * 128
                        P.op("dve", i_ts(fin_t[:, 0:nq], pl.ap[:, cs:cs + nq], esink[:, l * NQC + c:l * NQC + c + 1], ALU.add), reads=[pl, CONST], writes=[finB])
                        P.op("dve", i_recip(fin_t[:, 0:nq], fin_t[:, 0:nq]), reads=[finB], writes=[finB])
                        P.op("dve", i_tt(o_a[:, c, t0:t0 + nq], po.ap[:, cs:cs + nq], fin_t[:, 0:nq], ALU.mult), reads=[po, finB],
                             writes=[oaB[c][ti] for ti in qtiles])
            for bi in list(range(2, C.NB + 1)) + [0, 1]:
                swa_block(bi)
            psm.unreserve([po, pl])
            P.barrier()
            dump("o_a", o_a, NQC, [b_ for r_ in oaB for b_ in r_])
            AH.release(mSWA)
            branch_merge(0, o_a, oaB, NQC, W["w_br_swa"])
            P.barrier()
            dump("m0", merged, DC, [b_ for r_ in mgB for b_ in r_])
            AC.release(mSC)

            ck("swa")
            mR = AH.mark()
            mRC = AC.mark()
            NRC = C.RET_OUT // 128
            o_c = AC.bf16(NRC * T).rearrange("p (c t) -> p c t", c=NRC)
            ocB = [[Buf(None, "oc") for _ in range(NT)] for _ in range(NRC)]
            rope2 = load_rope([0, 1])
            st_t = AH.f32(256)
            stB = Buf(st_t, "state")
            sb16 = AH.bf16(256)
            sb16B = Buf(sb16, "state16")
            qx_t = [AH.bf16(128) for _ in range(2)]
            qxB = [Buf(qx_t[i], "qx") for i in range(2)]
            pT_t = [AH.bf16(128) for _ in range(2)]
            pTB = [Buf(pT_t[i], "pT") for i in range(2)]
            mXR = AR.mark()
            Lg = AH.f32(C.GS * 256).rearrange("p (i e) -> p i e", i=C.GS)
            LGB = Buf(None, "Lg")
            x3d = Xl["x3d"].ap()
            for h in range(RH):
                m_ = AH.mark()
                P.dma("sp", [(Lg[:, i, :], x3d[i * 128:(i + 1) * 128, h * 256:(h + 1) * 256]) for i in range(C.GS)], "ld_g3", reads=[DX3d], writes=[LGB])
                KT, KTB, QT, QTB, Kz, KzB, Vr, VrB = ret_kv_head(h, True, rope2)
                sg = AH.bf16(2 * T).rearrange("p (a t) -> p a t", a=2)
                sgB = [[Buf(None, "sg") for _ in range(NT)] for _ in range(2)]
                for a_ in range(2):
                    wi = wq.plan(wcols(win, l, 0, DC, off["ret_g"] + h * 256 + a_ * 128, 128))
                    for ti, (s, n) in enumerate(TT):
                        srcs, sb = hn_srcs(ti)
                        ps = psm.next()
                        proj_fm(ps, n, wq.get(wi), 0, 128, srcs, sb)
                        P.op("act", i_act(sg[:, a_, s:s + n], ps.ap[:, 0:n], AF.Silu), reads=[ps], writes=[sgB[a_][ti]])
                    wq.done(wi)
                xh = AH.bf16(2 * T).rearrange("p (a t) -> p a t", a=2)
                sqh = AH.bf16(2 * T).rearrange("p (a t) -> p a t", a=2)
                xhB = [Buf(None, "xh") for _ in range(NT)]
                sqhB = [Buf(None, "sqh") for _ in range(NT)]
                rc0 = h * (1 + C.GS)
                P.op("dve", i_ts(st_t, Mst[:, h, :], cfc("retc", rc0), ALU.mult), reads=[MB[h], CONST], writes=[stB])
                for i in range(C.GS):
                    P.op("dve", i_stt(st_t, Lg[:, i, :], cfc("retc", rc0 + 1 + i), st_t, ALU.mult, ALU.add), reads=[LGB, stB, CONST], writes=[stB])
                def prep(bi):
                    t0, nq = tok_range(bi)
                    qtiles = tiles_of(t0, nq)
                    it = bi % 2
                    if bi == 0:
                        xi_ap = cfc("xi_m", h * 16, 16)
                        dp_ap = cfc("dp_m", h * 16, 16, rows=16)
                    else:
                        xi_ap = cfc("xi", h * 128, 128)
                        dp_ap = cfc("dp", h * 128, 128)
                    P.op("dve", i_tt(qx_t[it][:, 0:nq], QT[:, t0:t0 + nq], xi_ap, ALU.mult), reads=[QTB[ti] for ti in qtiles] + [CONST], writes=[qxB[it]])
                    pss = psm.next()
                    P.pe([mm(pss.ap[0:nq, 0:nq], KT[:, t0:t0 + nq], qx_t[it][:, 0:nq], True, True)], reads=[KTB[ti] for ti in qtiles] + [qxB[it]], writes=[pss])
                    P.op("dve", i_tt(pT_t[it][0:nq, 0:nq], pss.ap[0:nq, 0:nq], dp_ap, ALU.mult), reads=[pss, CONST], writes=[pTB[it]])

                def main(bi):
                    t0, nq = tok_range(bi)
                    qtiles = tiles_of(t0, nq)
                    it = bi % 2
                    if bi > 0:
                        P.op("act", i_act(sb16, st_t, AF.Copy), reads=[stB], writes=[sb16B])
                    pos_ = [psm.next(), psm.next()]
                    for e2 in range(2):
                        fns = [mm(pos_[e2].ap[:, 0:nq], Vr[0:nq, bi, e2 * 128:(e2 + 1) * 128], pT_t[it][0:nq, 0:nq], True, bi == 0)]
                        rd = [VrB[bi], pTB[it]]
                        if bi > 0:
                            fns.append(mm(pos_[e2].ap[:, 0:nq], sb16[:, e2 * 128:(e2 + 1) * 128], qx_t[it][:, 0:nq], False, True))
                            rd += [sb16B, qxB[it]]
                        P.pe(fns, reads=rd, writes=[pos_[e2]])
                    for e2 in range(2):
                        P.op("act", i_act(xh[:, e2, t0:t0 + nq], pos_[e2].ap[:, 0:nq], AF.Copy), reads=[pos_[e2]], writes=[xhB[ti] for ti in qtiles])
                        P.op("act", i_act(sqh[:, e2, t0:t0 + nq], pos_[e2].ap[:, 0:nq], AF.Square), reads=[pos_[e2]], writes=[sqhB[ti] for ti in qtiles])
                    if 0 < bi < C.NB:
                        kv_update(st_t, stB, Kz, KzB, Vr, VrB, bi, Gch[h], False)
                prep(0)
                for bi in range(0, C.NB + 1):
                    if bi + 1 <= C.NB:
                        prep(bi + 1)
                    main(bi)
                for ti, (s, n) in enumerate(TT):
                    s1 = psm.next()
                    s2 = psm.next()
                    P.pe([mm(s1.ap[:, 0:n], onesb, xh[:, e2, s:s + n], e2 == 0, e2 == 1) for e2 in range(2)], reads=[xhB[ti], CONST], writes=[s1])
                    P.pe([mm(s2.ap[:, 0:n], onesb, sqh[:, e2, s:s + n], e2 == 0, e2 == 1) for e2 in range(2)], reads=[sqhB[ti], CONST], writes=[s2])
                    mean = nexttmp()
                    var = nexttmp()
                    P.op("dve", i_ts(mean.ap[:, 0:n], s1.ap[:, 0:n], 1.0 / 256, ALU.mult), reads=[s1], writes=[mean])
                    P.op("dve", i_tt(var.ap[:, 0:n], mean.ap[:, 0:n], mean.ap[:, 0:n], ALU.mult), reads=[mean], writes=[var])
                    P.op("dve", i_stt(var.ap[:, 0:n], s2.ap[:, 0:n], 1.0 / 256, var.ap[:, 0:n], ALU.mult, ALU.subtract), reads=[s2, var], writes=[var])
                    P.op("act", i_act(var.ap[:, 0:n], var.ap[:, 0:n], AF.Ln, bias=eps_t[:, 0:1], scale=1.0), reads=[var, CONST], writes=[var])
                    P.op("act", i_act(var.ap[:, 0:n], var.ap[:, 0:n], AF.Exp, scale=-0.5), reads=[var], writes=[var])
                    for e2 in range(2):
                        tmpa = nexttmp()
                        P.op("dve", i_tt(tmpa.ap[:, 0:n], xh[:, e2, s:s + n], mean.ap[:, 0:n], ALU.subtract), reads=[xhB[ti], mean], writes=[tmpa])
                        P.op("dve", i_tt(tmpa.ap[:, 0:n], tmpa.ap[:, 0:n], var.ap[:, 0:n], ALU.mult), reads=[tmpa, var], writes=[tmpa])
                        P.op("dve", i_stt(o_c[:, 2 * h + e2, s:s + n], tmpa.ap[:, 0:n], cfc("g_gn", l * NRC + 2 * h + e2), sg[:, e2, s:s + n], ALU.mult, ALU.mult),
                             reads=[tmpa, CONST, sgB[e2][ti]], writes=[ocB[2 * h + e2][ti]])
                P.barrier()
                AH.release(m_)
            P.barrier()
            AH.release(mR)
            AR.release(mXR)
            dump("o_c", o_c, NRC, [b_ for r_ in ocB for b_ in r_])
            branch_merge(2, o_c, ocB, NRC, W["w_br_ret"])
            P.barrier()
            dump("m02", merged, DC, [b_ for r_ in mgB for b_ in r_])
            AC.release(mRC)

            ck("ret")
            mM = AH.mark()
            mMC = AC.mark()
            NMC = C.MLA_OUT // 128
            MH = C.MLA_H
            o_b = AC.bf16(NMC * T).rearrange("p (c t) -> p c t", c=NMC)
            obB = [[Buf(None, "ob") for _ in range(NT)] for _ in range(NMC)]
            cqn = AH.bf16(C.QRC * T).rearrange("p (c t) -> p c t", c=C.QRC)
            cqnB = [[Buf(None, "cqn") for _ in range(NT)] for _ in range(C.QRC)]
            proj_rmsnorm(off["cq"], C.QRC, "g_qa", l * C.QRC, cqn, cqnB)
            rope3 = load_rope([2, 3])
            wuq = W["mla_w_uq"]
            wukv = W["mla_w_ukv"]
            wuq3 = wuq[l].rearrange("(k p) n -> p k n", p=128)
            qn_t = [AH.bf16(T) for _ in range(2)]
            qr_t = [AH.bf16(T) for _ in range(2)]
            qnB = [[Buf(None, "qn") for _ in range(NT)] for _ in range(2)]
            qrB = [[Buf(None, "qr") for _ in range(NT)] for _ in range(2)]
            cg_t = AH.bf16(C.KVC * S).rearrange("p (c t) -> p c t", c=C.KVC)
            cgB = Buf(None, "cg")
            Kh_t = [AH.bf16(S) for _ in range(2)]
            KhB = [Buf(None, "Kh") for _ in range(2)]
            Vh_t = [AH.bf16(S).rearrange("p (b n) -> p b n", n=128) for _ in range(2)]
            VhB = [Buf(None, "Vh") for _ in range(2)]
            Pm_t = [AH.bf16(512) for _ in range(2)]
            PmB = [Buf(Pm_t[i], "Pm") for i in range(2)]
            pmi = {"i": 0}
            wkv_t = [AH.bf16(2 * C.KVC * 128).rearrange("p (a k n) -> p a k n", a=2, k=C.KVC) for _ in range(2)]
            wkvB = [Buf(None, "wkv") for _ in range(2)]
            Khm = AH.bf16(N_META)
            Vhm = AH.bf16(128)
            KhmB = Buf(None, "Khm")
            VhmB = Buf(None, "Vhm")

            def nextpm():
                pmi["i"] = (pmi["i"] + 1) % 2
                return PmB[pmi["i"]]
            acc = psm.reserve(2 * C.NQT)
            scale = (C.NOPE + C.ROPE) ** -0.5
            x1d = Xl["x1d"].ap()
            srcs_list = ["own"] + list(range(C.GS - 1))
            items = [(h, sidx) for h in range(MH) for sidx in range(len(srcs_list))]
            QW = C.QB * BLK
            wukv3 = wukv[l].rearrange("(k p) n -> p k n", p=128)

            def mla_q(h):
                ib = h % 2
                P.dma("pool", [(wkv_t[ib][:, 0, :, :], wukv3[:, :, h * 256:h * 256 + 128]), (wkv_t[ib][:, 1, :, :], wukv3[:, :, h * 256 + 128:h * 256 + 256])],
                      f"wkv{ib}", reads=[cqnB[0][0]], writes=[wkvB[ib]])
                wn = wq.plan(wcols(wuq, l, 0, C.QRC, h * 192, 128))

                def cq_srcs(ti):
                    s, n = TT[ti]
                    return [cqn[:, kc, s:s + n] for kc in range(C.QRC)], [cqnB[kc][ti] for kc in range(C.QRC)]
                for ti, (s, n) in enumerate(TT):
                    srcs, sb = cq_srcs(ti)
                    ps = psm.next()
                    proj_fm(ps, n, wq.get(wn), 0, 128, srcs, sb)
                    headnorm(ps, 128, n, onesR, 128, cfc("g_qn", l), qn_t[ib][:, s:s + n], qnB[ib][ti])
                wq.done(wn)
                rope_fm(wuq3, C.QRC, cq_srcs, h * 192 + 128, 64, rope3[:, 0, :], rope3[:, 1, :], lambda s, n: qr_t[ib][0:64, s:s + n], lambda ti: qrB[ib][ti],
                        gain=cfc("g_qr", l, rows=64), gain_s=cfc("g_qrs", l, rows=64), norm=True)

            def mla_expand(it_idx):
                h, sidx = items[it_idx]
                src = srcs_list[sidx]
                ib = it_idx % 2
                wk = wkv_t[h % 2][:, 0]
                wv_ = wkv_t[h % 2][:, 1]
                wb = wkvB[h % 2]
                if src == "own":
                    def csl(kc, k0, n):
                        return ckvn[:, kc, N_META + k0:N_META + k0 + n]

                    def crd(k0, n):
                        return [ckvnB[kc][ti] for kc in range(C.KVC) for ti in tiles_of(N_META + k0, n)]
                else:
                    P.dma("sp", [(cg_t, x1d[src * C.KVR:(src + 1) * C.KVR, :].rearrange("(c p) t -> p c t", p=128))], "ld_cg", reads=[DX1d], writes=[cgB])

                    def csl(kc, k0, n):
                        return cg_t[:, kc, k0:k0 + n]

                    def crd(k0, n):
                        return [cgB]
                for k0 in range(0, S, 512):
                    n = min(512, S - k0)
                    ps = psm.next()
                    P.pe([mm(ps.ap[:, 0:n], wk[:, kc, :], csl(kc, k0, n), kc == 0, kc == C.KVC - 1) for kc in range(C.KVC)], reads=[wb] + crd(k0, n), writes=[ps])
                    headnorm(ps, 128, n, onesR, 128, cfc("g_kn", l), Kh_t[ib][:, k0:k0 + n], KhB[ib])
                for kb in range(C.NB):
                    ps = psm.next()
                    P.pe([mm(ps.ap[:, 0:128], csl(kc, kb * 128, 128), wv_[:, kc, :], kc == 0, kc == C.KVC - 1) for kc in range(C.KVC)],
                         reads=[wb] + crd(kb * 128, 128), writes=[ps])
                    P.op("act", i_act(Vh_t[ib][:, kb, :], ps.ap[:, 0:128], AF.Copy), reads=[ps], writes=[VhB[ib]])

            def mla_expand_meta(h):
                wk = wkv_t[h % 2][:, 0]
                wv_ = wkv_t[h % 2][:, 1]
                wb = wkvB[h % 2]
                mrd = [ckvnB[kc][ti] for kc in range(C.KVC) for ti in tiles_of(0, N_META)]
                ps = psm.next()
                P.pe([mm(ps.ap[:, 0:N_META], wk[:, kc, :], ckvn[:, kc, 0:N_META], kc == 0, kc == C.KVC - 1) for kc in range(C.KVC)], reads=[wb] + mrd, writes=[ps])
                headnorm(ps, 128, N_META, onesR, 128, cfc("g_kn", l), Khm[:, 0:N_META], KhmB)
                ps = psm.next()
                P.pe([mm(ps.ap[0:N_META, 0:128], ckvn[:, kc, 0:N_META], wv_[:, kc, :], kc == 0, kc == C.KVC - 1) for kc in range(C.KVC)], reads=[wb] + mrd, writes=[ps])
                P.op("act", i_act(Vhm[0:N_META, :], ps.ap[0:N_META, 0:128], AF.Copy), reads=[ps], writes=[VhmB])

            pend = {"p": None}

            def att_flush():
                if pend["p"] is not None:
                    (Ops, Lps, vv, nk, nq, rb, pm, first, last) = pend["p"]
                    P.pe([mm(Ops.ap[:, 0:nq], vv, pm.ap[0:nk, 0:nq], first, last)], reads=rb + [pm], writes=[Ops])
                    P.pe([mm(Lps.ap[:, 0:nq], onesb[0:nk, :], pm.ap[0:nk, 0:nq], first, last)], reads=[pm, CONST], writes=[Lps])
                    pend["p"] = None

            def att_step(h, qsl, nq, qrd, Ops, Lps, kT, krT, vv, nk, bias, mask_ap, rb, first, last):
                ibq = h % 2
                pss = psm.next()
                fns = [mm(pss.ap[0:nk, 0:nq], kT, qn_t[ibq][:, qsl:qsl + nq], True, False),
                       mm(pss.ap[0:nk, 0:nq], krT, qr_t[ibq][0:64, qsl:qsl + nq], False, mask_ap is None)]
                if mask_ap is not None:
                    fns.append(mm(pss.ap[0:nk, 0:nq], ident[0:nk, 0:nk], mask_ap, False, True))
                P.pe(fns, reads=rb + qrd + [CONST, KRG], writes=[pss])
                pm = nextpm()
                P.op("act", i_act(pm.ap[0:nk, 0:nq], pss.ap[0:nk, 0:nq], AF.Exp, bias=bias, scale=scale), reads=[pss, CONST], writes=[pm])
                att_flush()
                pend["p"] = (Ops, Lps, vv, nk, nq, rb, pm, first, last)

            def finish(h, Ops, Lps, qsl, nq):
                rl = nexttmp()
                P.op("dve", i_recip(rl.ap[:, 0:nq], Lps.ap[:, 0:nq]), reads=[Lps], writes=[rl])
                P.op("dve", i_tt(o_b[:, h, qsl:qsl + nq], Ops.ap[:, 0:nq], rl.ap[:, 0:nq], ALU.mult), reads=[Ops, rl],
                     writes=[obB[h][ti] for ti in tiles_of(qsl, nq)])

            def mla_attend(it_idx):
                h, sidx = items[it_idx]
                src = srcs_list[sidx]
                ib = it_idx % 2
                ibq = h % 2
                nsrc = len(srcs_list)
                mkrd = [KhmB, VhmB] + [krB[ti] for ti in tiles_of(0, N_META)]
                for qt in range(C.NQT):
                    qsl = blk(qt * C.QB)
                    qrd = [qnB[ibq][ti] for ti in tiles_of(qsl, QW)] + [qrB[ibq][ti] for ti in tiles_of(qsl, QW)]
                    Ops, Lps = acc[2 * qt], acc[2 * qt + 1]
                    if src == "own":
                        att_step(h, qsl, QW, qrd, Ops, Lps, Khm[:, 0:N_META], kr_all[0:64, 0:N_META], Vhm[0:N_META, :], N_META,
                                 cfc("zero", 0, rows=N_META), None, mkrd, True, False)
                        for kb in range((qt + 1) * C.QB):
                            r = kb - qt * C.QB
                            mask_ap = cbc("mm", r * C.QB * 128, QW) if r >= 0 else None
                            att_step(h, qsl, QW, qrd, Ops, Lps, Kh_t[ib][:, kb * 128:(kb + 1) * 128], kr_all[0:64, blk(kb):blk(kb) + 128],
                                     Vh_t[ib][:, kb, :], 128, cfc("zero"), mask_ap, [KhB[ib], VhB[ib]] + [krB[ti] for ti in tiles_of(blk(kb), 128)], False, False)
                    else:
                        for kb in range(C.NB):
                            lastk = (sidx == nsrc - 1) and (kb == C.NB - 1)
                            att_step(h, qsl, QW, qrd, Ops, Lps, Kh_t[ib][:, kb * 128:(kb + 1) * 128],
                                     kr_all[0:64, T + src * S + kb * 128:T + src * S + (kb + 1) * 128],
                                     Vh_t[ib][:, kb, :], 128, cfc("vis", src), None, [KhB[ib], VhB[ib]], False, lastk)
                if src == "own":
                    Om = psm.next()
                    Lm = psm.next()
                    qrd = [qnB[ibq][ti] for ti in tiles_of(0, N_META)] + [qrB[ibq][ti] for ti in tiles_of(0, N_META)]
                    att_step(h, 0, N_META, qrd, Om, Lm, Khm[:, 0:N_META], kr_all[0:64, 0:N_META], Vhm[0:N_META, :], N_META,
                             cfc("zero", 0, rows=N_META), cbc("cmcur", 0, N_META, rows=N_META), mkrd, True, True)
                    att_flush()
                    finish(h, Om, Lm, 0, N_META)

            nI = len(items)
            mla_q(0)
            mla_expand(0)
            mla_expand_meta(0)
            for it_idx in range(nI):
                h, sidx = items[it_idx]
                if it_idx + 1 < nI:
                    h2, s2 = items[it_idx + 1]
                    if s2 == 0:
                        mla_q(h2)
                    mla_expand(it_idx + 1)
                mla_attend(it_idx)
                att_flush()
                if sidx == len(srcs_list) - 1:
                    for qt in range(C.NQT):
                        finish(h, acc[2 * qt], acc[2 * qt + 1], blk(qt * C.QB), QW)
                    if it_idx + 1 < nI:
                        mla_expand_meta(h + 1)
            psm.unreserve(acc)
            P.barrier()
            AH.release(mM)
            dump("o_b", o_b, NMC, [b_ for r_ in obB for b_ in r_])
            branch_merge(1, o_b, obB, NMC, W["w_br_mla"])
            P.barrier()
            dump("merged", merged, DC, [b_ for r_ in mgB for b_ in r_])
            AC.release(mMC)

            ck("mla")
            AH.release(mH)
            P.dma("sp", [(h_flat[:, q * q4:(q + 1) * q4], hsp_d[:, q * q4:(q + 1) * q4]) for q in range(4)], "hld", reads=[], writes=allh)
            for dc in range(DC):
                wi = wq.plan(wcols(W["w_o"], l, 0, DC, dc * 128, 128))
                for ti, (s, n) in enumerate(TT):
                    po_ = psm.next()
                    proj_fm(po_, n, wq.get(wi), 0, 128, [merged[:, kc, s:s + n] for kc in range(DC)], [mgB[kc][ti] for kc in range(DC)])
                    P.op("dve", i_tt(hsl(dc, ti), po_.ap[:, 0:n], hsl(dc, ti), ALU.add), reads=[po_, hB[dc][ti]], writes=[hB[dc][ti]])
                wq.done(wi)
            P.barrier()
            AC.release(mC)

        try:
            ck("load")
            for l in range(L):
                ffn(l, W["ffn1_w_gate"], W["ffn1_w_up"], W["ffn1_w_down"], "g_ffn1")
                ck("ffn1")
                mixer(l)
                ck("mixer")
                ffn(l, W["ffn2_w_gate"], W["ffn2_w_up"], W["ffn2_w_down"], "g_ffn2")
        except _Stop as e_:
            P.barrier()
            if e_.args:
                return P, wq.plans, (AH.peak, AK.peak, AC.peak)

        yv = yT.rearrange("(c p) t -> p c t", p=128)
        step = max(1, DC // 4)
        ev = P.dma("sp", [(yv[:, c0:c0 + step, :], h_ap[:, c0:c0 + step, N_META:T]) for c0 in range(0, DC, step)], "st_y", reads=allh, writes=[])
        P.wait_on("sp", [ev])
        return P, wq.plans, (AH.peak, AK.peak, AC.peak)

    P1, script, _ = generate(None)
    P2, script2, peaks = generate(script)
    assert len(script2) == len(script)
    if getattr(cfg, "dbgdump", None):
        with open(cfg.dbgdump, "w") as f:
            for d in P2.dbg:
                f.write(repr(d) + "\n")
    P2.emit()
    return nc, stack, peaks


def weight_shapes(C):
    L, D = C.DEPTH, C.D
    return [("ffn1_w_gate", [L, D, C.DFF]), ("ffn1_w_up", [L, D, C.DFF]), ("ffn1_w_down", [L, C.DFF, D]),
            ("w_in", [L, D, C.IN_W]), ("mla_w_uq", [L, C.QR, C.MLA_H * 192]),
            ("mla_w_ukv", [L, C.KVR, C.MLA_H * 256]), ("w_br_swa", [L, C.SWA_OUT, D]),
            ("w_br_mla", [L, C.MLA_OUT, D]), ("w_br_ret", [L, C.RET_OUT, D]), ("w_o", [L, D, D]),
            ("ffn2_w_gate", [L, D, C.DFF]), ("ffn2_w_up", [L, D, C.DFF]), ("ffn2_w_down", [L, C.DFF, D])]


def run_cfg(cfg, inputs, trace=False):
    x = np.asarray(inputs["x"], np.float32)
    meta = np.asarray(inputs["meta_tokens"], np.float32)
    S = cfg.S
    wnames = [n for n, _ in weight_shapes(cfg)]
    wts = {n: np.ascontiguousarray(np.asarray(inputs[n], np.float32)) for n in wnames}
    metaT = np.ascontiguousarray(meta.T)
    consts = [host_consts(cfg, inputs, j) for j in range(cfg.GS)]
    in_maps = []
    for core in range(cfg.NCORE):
        b, j = divmod(core, cfg.GS)
        cf, cb, rope = consts[j]
        m = {"xT": np.ascontiguousarray(x[b, j * S:(j + 1) * S, :].T), "metaT": metaT, "cf": cf, "cb": cb, "rope": rope}
        m.update(wts)
        in_maps.append(m)
    nc, stack, peaks = build_program(cfg)
    with stack:
        res = run_bass_kernel_spmd(nc, in_maps, core_ids=list(range(cfg.NCORE)), trace=trace)
    out = np.zeros((cfg.NBATCH, cfg.GS * S, cfg.D), np.float32)
    for core in range(cfg.NCORE):
        b, j = divmod(core, cfg.GS)
        out[b, j * S:(j + 1) * S, :] = res.results[core]["yT"].T
    return out, res


def kernel(**inputs):
    out, _ = run_cfg(FULL, inputs)
    return out
```

```python
import math
import contextlib
import numpy as np
import ml_dtypes
import concourse.bass as bass
import concourse.mybir as mybir
from concourse.bass_utils import run_bass_kernel_spmd

F32 = mybir.dt.float32
BF16 = mybir.dt.bfloat16
F32R = mybir.dt.float32r
AF = mybir.ActivationFunctionType
ALU = mybir.AluOpType

NEG = -30000.0
EPS = 1e-6
ROPE_BASE = 10000.0
N_META = 16
BLK = 128


class Cfg:
    def __init__(self, D=2048, DFF=5632, DEPTH=4, NB=8, SWA_H=16, SWA_KVH=2, MLA_H=8, QR=512, KVR=512,
                 RET_H=4, FG=11, GS=4, NBATCH=2):
        self.D, self.DFF, self.DEPTH, self.NB = D, DFF, DEPTH, NB
        self.SWA_H, self.SWA_KVH, self.SWA_D = SWA_H, SWA_KVH, 64
        self.MLA_H, self.QR, self.KVR = MLA_H, QR, KVR
        self.NOPE, self.ROPE, self.VD = 128, 64, 128
        self.RET_H, self.DK, self.DV = RET_H, 128, 256
        self.FG, self.GS, self.NBATCH = FG, GS, NBATCH
        self.NCORE = GS * NBATCH
        self.DC = D // 128
        self.FC = DFF // 128
        assert self.FC % FG == 0
        self.S = NB * BLK
        self.T = N_META + self.S
        self.SWA_OUT = SWA_H * 64
        self.SWA_KVW = SWA_KVH * 64
        self.MLA_OUT = MLA_H * 128
        self.RET_QK = RET_H * 128
        self.RET_OUT = RET_H * 256
        self.QRC, self.KVC = QR // 128, KVR // 128
        w = [self.SWA_OUT, self.SWA_KVW, self.SWA_KVW, QR, KVR, 64, self.RET_QK, self.RET_QK,
             self.RET_OUT, self.RET_OUT, 3 * D]
        names = ["swa_q", "swa_k", "swa_v", "cq", "ckv", "kr", "ret_q", "ret_k", "ret_v", "ret_g", "gate"]
        self.off = {}
        o = 0
        for n_, w_ in zip(names, w):
            self.off[n_] = o
            o += w_
        self.IN_W = o
        self.QB = min(4, NB)
        self.NQT = NB // self.QB
        assert NB % self.QB == 0
        nt = -(-self.T // 512)
        base = -(-self.T // nt)
        base = -(-base // 8) * 8
        self.TT = []
        s = 0
        while s < self.T:
            n = min(base, self.T - s)
            self.TT.append((s, n))
            s += n
        self.X2A_ROWS = 64 + (2 * 128 * 128 + 128 * 128) // self.S
        assert (3 * 128 * 128) % self.S == 0


FULL = Cfg()


def cf_layout(cfg):
    lay = {}
    o = 0

    def add(name, n):
        nonlocal o
        lay[name] = (o, n)
        o += n
    L = cfg.DEPTH
    add("g_ffn1", L * cfg.DC)
    add("g_mix", L * cfg.DC)
    add("g_ffn2", L * cfg.DC)
    add("g_qa", L * cfg.QRC)
    add("g_kva", L * cfg.KVC)
    add("g_qn", L)
    add("g_kn", L)
    add("g_qr", L)
    add("g_qrs", L)
    add("g_kr", L)
    add("g_krs", L)
    add("g_sq", L)
    add("g_sk", L)
    add("g_gn", L * (cfg.RET_OUT // 128))
    add("sinks", L * (cfg.SWA_H // 2))
    add("zeta", cfg.RET_H)
    add("zeta_m", cfg.RET_H)
    add("zeta_w", cfg.RET_H * cfg.NB)
    add("retc", cfg.RET_H * (1 + cfg.GS))
    add("vis", cfg.GS)
    add("selp", cfg.GS)
    add("zero", 1)
    add("ones", 128)
    add("bd64", 128)
    add("xi", cfg.RET_H * 128)
    add("xi_m", cfg.RET_H * 16)
    add("dp", cfg.RET_H * 128)
    add("dp_m", cfg.RET_H * 16)
    lay["_n"] = o
    return lay


def cb_layout(cfg):
    lay = {}
    o = 0

    def add(name, n):
        nonlocal o
        lay[name] = (o, n)
        o += n
    add("ident", 128)
    add("ones", 128)
    add("cmcur", 512)
    add("cmprev", 512)
    add("mm", cfg.QB * cfg.QB * 128)
    lay["_n"] = o
    return lay


def _gammas(cfg):
    h = np.arange(cfg.RET_H, dtype=np.float64)
    return 1.0 - 2.0 ** (-5.0 - h)


def host_consts(cfg, inputs, j):
    lay = cf_layout(cfg)
    cf = np.zeros((128, lay["_n"]), np.float32)
    L = cfg.DEPTH

    def put(name, arr):
        o, n = lay[name]
        arr = np.asarray(arr, np.float32)
        assert arr.shape[1] == n, (name, arr.shape, n)
        cf[:arr.shape[0], o:o + n] = arr

    def chunks(v):
        v = np.asarray(v, np.float32)
        Lq, W = v.shape
        return v.reshape(Lq, W // 128, 128).transpose(2, 0, 1).reshape(128, -1)
    put("g_ffn1", chunks(inputs["ffn1_norm"]))
    put("g_mix", chunks(inputs["mix_norm"]))
    put("g_ffn2", chunks(inputs["ffn2_norm"]))
    put("g_qa", chunks(inputs["mla_q_a_norm"]))
    put("g_kva", chunks(inputs["mla_kv_a_norm"]))
    put("g_qn", np.asarray(inputs["mla_qn_norm"], np.float32).T)
    put("g_kn", np.asarray(inputs["mla_kn_norm"], np.float32).T)
    qr = np.asarray(inputs["mla_qr_norm"], np.float32).T
    kr = np.asarray(inputs["mla_kr_norm"], np.float32).T
    put("g_qr", qr)
    put("g_qrs", np.concatenate([qr[32:], qr[:32]], 0))
    put("g_kr", kr)
    put("g_krs", np.concatenate([kr[32:], kr[:32]], 0))
    sq = np.asarray(inputs["swa_q_norm"], np.float32).T
    sk = np.asarray(inputs["swa_k_norm"], np.float32).T
    put("g_sq", np.concatenate([sq, sq], 0))
    put("g_sk", np.concatenate([sk, sk], 0))
    put("g_gn", chunks(inputs["ret_gn"]))
    sinks = np.asarray(inputs["swa_sinks"], np.float32)
    sk2 = np.zeros((128, L * (cfg.SWA_H // 2)), np.float32)
    for l in range(L):
        for c in range(cfg.SWA_H // 2):
            sk2[:64, l * (cfg.SWA_H // 2) + c] = sinks[l, 2 * c]
            sk2[64:, l * (cfg.SWA_H // 2) + c] = sinks[l, 2 * c + 1]
    put("sinks", sk2)
    g = _gammas(cfg)
    lg = np.log(g)
    k = np.arange(128, dtype=np.float64)
    sc = cfg.DK ** -0.5
    put("zeta", np.exp((127.0 - k)[:, None] * lg[None, :]) * sc)
    zm = np.zeros((128, cfg.RET_H))
    zm[:16] = np.exp((15.0 - np.arange(16.0))[:, None] * lg[None, :]) * sc
    put("zeta_m", zm)
    zw = np.zeros((128, cfg.RET_H, cfg.NB))
    for b_ in range(cfg.NB):
        zw[:, :, b_] = np.exp((127.0 - k)[:, None] * lg[None, :] + 128.0 * (cfg.NB - 1 - b_) * lg[None, :]) * sc
    put("zeta_w", zw.reshape(128, -1))
    G = np.exp(128.0 * lg)
    retc = np.zeros((128, cfg.RET_H, 1 + cfg.GS))
    for h in range(cfg.RET_H):
        retc[:, h, 0] = G[h] ** (cfg.NB * j)
        for i in range(cfg.GS):
            retc[:, h, 1 + i] = G[h] ** (cfg.NB * (j - 1 - i)) if i < j else 0.0
    put("retc", retc.reshape(128, -1))
    vis = np.zeros((128, cfg.GS))
    for i in range(cfg.GS):
        vis[:, i] = 0.0 if i < j else NEG
    put("vis", vis)
    selp = np.full((128, cfg.GS), NEG)
    if j > 0:
        selp[:, j - 1] = 0.0
    put("selp", selp)
    put("zero", np.zeros((128, 1)))
    put("ones", np.ones((128, 128)))
    bd = np.zeros((128, 128))
    bd[:64, :64] = 1
    bd[64:, 64:] = 1
    put("bd64", bd)
    q = np.arange(128, dtype=np.float64)
    xi = np.exp((q + 1.0)[None, :] * lg[:, None])
    put("xi", np.broadcast_to(xi.reshape(1, -1), (128, cfg.RET_H * 128)))
    qm = np.arange(16, dtype=np.float64)
    xim = np.exp((113.0 + qm)[None, :] * lg[:, None])
    put("xi_m", np.broadcast_to(xim.reshape(1, -1), (128, cfg.RET_H * 16)))
    dp = np.zeros((128, cfg.RET_H, 128))
    dpm = np.zeros((128, cfg.RET_H, 16))
    for h in range(cfg.RET_H):
        dp[:, h, :] = np.exp(-(k + 1.0) * lg[h])[:, None] * sc * (q[None, :] >= k[:, None])
        dpm[:16, h, :] = (np.exp(-(113.0 + qm) * lg[h])[:, None] * sc * (qm[None, :] >= qm[:, None]))
    put("dp", dp.reshape(128, -1))
    put("dp_m", dpm.reshape(128, -1))

    layb = cb_layout(cfg)
    cb = np.zeros((128, layb["_n"]), np.float32)

    def putb(name, arr):
        o, n = layb[name]
        assert arr.shape == (128, n), (name, arr.shape)
        cb[:, o:o + n] = arr
    putb("ident", np.eye(128))
    putb("ones", np.ones((128, 128)))
    kk = np.arange(128)[:, None]
    qq = np.arange(128)[None, :]
    cur = np.where(qq >= kk, 0.0, NEG)
    prev = np.where(kk > qq, 0.0, NEG)
    putb("cmcur", np.tile(cur, (1, 4)))
    putb("cmprev", np.tile(prev, (1, 4)))
    mm = np.zeros((128, cfg.QB, cfg.QB, 128))
    for r in range(cfg.QB):
        for bb in range(cfg.QB):
            mm[:, r, bb, :] = NEG if bb < r else (cur if bb == r else 0.0)
    putb("mm", mm.reshape(128, -1))
    cb = cb.astype(ml_dtypes.bfloat16)

    pos = np.concatenate([np.arange(16.0), 16.0 + j * cfg.S + np.arange(cfg.S, dtype=np.float64)])
    T = cfg.T

    def tables(dim):
        half = dim // 2
        inv = ROPE_BASE ** (-np.arange(half, dtype=np.float64) / half)
        ang = (pos[None, :].astype(np.float32) * inv[:, None].astype(np.float32)).astype(np.float32)
        c = np.cos(ang.astype(np.float64))
        s = np.sin(ang.astype(np.float64))
        return np.concatenate([c, c], 0), np.concatenate([-s, s], 0)
    cr, sr = tables(128)
    cm, sm = tables(64)
    rope = np.zeros((128, 4, T), np.float32)
    rope[:, 0] = cr
    rope[:, 1] = sr
    rope[:64, 2] = cm
    rope[:64, 3] = sm
    return cf, cb, rope.reshape(128, 4 * T)


ENGS = ["pe", "act", "dve", "pool", "sp"]


class Buf:
    __slots__ = ("ap", "w", "r", "name", "psum")

    def __init__(self, ap, name="", psum=False):
        self.ap = ap
        self.w = None
        self.r = {}
        self.name = name
        self.psum = psum


class Prog:
    def __init__(self, nc, stack, semreg):
        self.nc = nc
        self.stack = stack
        self.streams = {e: [] for e in ENGS}
        self.sems = semreg
        self.cnt = {}
        self.seen = {e: {} for e in ENGS}
        self.dbg = []
        for e in ENGS:
            self.newsem("done_" + e)

    def newsem(self, key):
        if key not in self.sems:
            self.sems[key] = self.stack.enter_context(self.nc.semaphore(key))
        if key not in self.cnt:
            self.cnt[key] = 0
        return key

    def _waits(self, eng, reads, writes):
        ev = {}

        def add(k, v):
            if ev.get(k, 0) < v:
                ev[k] = v
        for b in reads:
            if b.w is not None:
                add(*b.w)
            if b.psum:
                for k, v in b.r.items():
                    if k != "done_" + eng:
                        add(k, v)
        for b in writes:
            if b.w is not None:
                add(*b.w)
            for k, v in b.r.items():
                add(k, v)
        out = []
        seen = self.seen[eng]
        for k, v in ev.items():
            if eng == "pe" and k == "done_pe":
                continue
            if seen.get(k, 0) >= v:
                continue
            seen[k] = v
            out.append((k, v))
        return out

    def _finish(self, evt, reads, writes):
        k, v = evt
        for b in reads:
            if b.r.get(k, 0) < v:
                b.r[k] = v
        for b in writes:
            b.w = evt
            b.r = {}

    def op(self, eng, fn, reads=(), writes=()):
        waits = self._waits(eng, reads, writes)
        key = "done_" + eng
        self.cnt[key] += 1
        evt = (key, self.cnt[key])
        self.streams[eng].append((waits, [fn], (key, 1)))
        self.dbg.append((eng, evt, waits, [b.name for b in reads], [b.name for b in writes]))
        self._finish(evt, reads, writes)
        return evt

    def pe(self, fns, reads=(), writes=()):
        waits = self._waits("pe", reads, writes)
        key = "done_pe"
        self.cnt[key] += 1
        evt = (key, self.cnt[key])
        self.streams["pe"].append((waits, list(fns), (key, 1)))
        self.dbg.append(("pe", evt, waits, [b.name for b in reads], [b.name for b in writes]))
        self._finish(evt, reads, writes)
        return evt

    def dma(self, eng, pairs, sem, reads=(), writes=(), **kw):
        self.newsem(sem)
        waits = self._waits(eng, reads, writes)
        fns = []
        for (o, i) in pairs:
            fns.append((lambda e, o=o, i=i: e.dma_start(out=o, in_=i, **kw)))
        self.cnt[sem] += 16 * len(pairs)
        evt = (sem, self.cnt[sem])
        self.streams[eng].append((waits, fns, (sem, 16, "each")))
        self.dbg.append((eng + ":dma", evt, waits, [b.name for b in reads], [b.name for b in writes]))
        self._finish(evt, reads, writes)
        return evt

    def collective(self, fn, sem, reads=(), writes=()):
        self.newsem(sem)
        waits = self._waits("pool", reads, writes)
        self.cnt[sem] += 1
        evt = (sem, self.cnt[sem])
        self.streams["pool"].append((waits, [fn], (sem, 1)))
        self._finish(evt, reads, writes)
        return evt

    def barrier(self):
        tot = [(k, v) for k, v in self.cnt.items() if v > 0 and not (k.startswith("w") or k.startswith("cc") or k == "done_pool")]
        self.dbg.append(("BARRIER", None, tot, [], []))
        for e in ENGS:
            if e == "pool":
                continue
            waits = []
            for k, v in tot:
                if k == "done_" + e:
                    continue
                if self.seen[e].get(k, 0) >= v:
                    continue
                self.seen[e][k] = v
                waits.append((k, v))
            if waits:
                self.streams[e].append((waits, [], None))

    def wait_on(self, eng, evts):
        waits = []
        for k, v in evts:
            if self.seen[eng].get(k, 0) < v:
                self.seen[eng][k] = v
                waits.append((k, v))
        if waits:
            self.streams[eng].append((waits, [], None))

    def emit(self):
        nc = self.nc
        sems = self.sems
        streams = self.streams

        def run(e, ops):
            for waits, fns, inc in ops:
                for k, v in waits:
                    e.wait_ge(sems[k], v)
                n = len(fns)
                for idx, fn in enumerate(fns):
                    ins = fn(e)
                    if inc is None:
                        continue
                    if len(inc) == 3 or idx == n - 1:
                        ins.then_inc(sems[inc[0]], inc[1])
        with nc.Block() as block:
            @block.tensor
            def _(e):
                run(e, streams["pe"])

            @block.scalar
            def _(e):
                run(e, streams["act"])

            @block.vector
            def _(e):
                run(e, streams["dve"])

            @block.gpsimd
            def _(e):
                run(e, streams["pool"])

            @block.sync
            def _(e):
                run(e, streams["sp"])


class Arena:
    def __init__(self, ap, words):
        self.ap = ap
        self.words = words
        self.top = 0
        self.peak = 0

    def mark(self):
        return self.top

    def release(self, m):
        self.top = m

    def _alloc(self, n):
        n = (n + 1) // 2 * 2
        o = self.top
        self.top += n
        self.peak = max(self.peak, self.top)
        assert self.top <= self.words, ("arena overflow", self.top, self.words)
        return o

    def f32(self, n, dt=F32):
        o = self._alloc(n)
        v = self.ap[:, o:o + n]
        return v if dt == F32 else v.bitcast(dt)

    def bf16(self, n):
        w = (n + 1) // 2
        o = self._alloc(w)
        return self.ap[:, o:o + w].bitcast(BF16)[:, 0:n]


NSLOT = 6
AHEAD = 4


class _Stop(Exception):
    pass


def build_program(cfg):
    nc = bass.Bass("TRN2", target_bir_lowering=False)
    stack = contextlib.ExitStack()
    semreg = {}
    C = cfg
    L, D, DC, T, S = C.DEPTH, C.D, C.DC, C.T, C.S
    lay = cf_layout(C)
    layb = cb_layout(C)

    def din(name, shape, dt=F32):
        return nc.dram_tensor(name, list(shape), dt, kind="ExternalInput").ap()
    xT = din("xT", [D, S])
    metaT = din("metaT", [D, N_META])
    cf_d = din("cf", [128, lay["_n"]])
    cb_d = din("cb", [128, layb["_n"]], BF16)
    rope_d = din("rope", [128, 4 * T])
    W = {}
    for nm, shp in weight_shapes(C):
        W[nm] = din(nm, shp)
    yT = nc.dram_tensor("yT", [D, S], F32, kind="ExternalOutput").ap()
    HW = DC * T
    hsp_d = nc.dram_tensor("hspill", [128, HW], F32).ap()
    X = []
    for l in range(L):
        X.append(dict(
            x1s=nc.dram_tensor(f"x1s{l}", [C.KVR, S], BF16),
            x1d=nc.dram_tensor(f"x1d{l}", [C.GS * C.KVR, S], BF16),
            x2s=nc.dram_tensor(f"x2s{l}", [C.X2A_ROWS, S], BF16),
            x2d=nc.dram_tensor(f"x2d{l}", [C.GS * C.X2A_ROWS, S], BF16),
            x3s=nc.dram_tensor(f"x3s{l}", [128, C.RET_H * 256], F32),
            x3d=nc.dram_tensor(f"x3d{l}", [C.GS * 128, C.RET_H * 256], F32),
        ))
    groups = [[b * C.GS + i for i in range(C.GS)] for b in range(C.NBATCH)]

    HNW = DC * ((T + 1) // 2)
    SLOTW = 16 * 64
    NSINK = lay["sinks"][1]
    KW = lay["_n"] + (layb["_n"] + 1) // 2 + NSINK + 4 + 5 * 512 + 32
    NMAXC = max(C.SWA_OUT, C.MLA_OUT, C.RET_OUT) // 128
    CWORDS = max(DC * ((T + 1) // 2) + NMAXC * ((T + 1) // 2), C.FG * ((T + 1) // 2)) + 64
    RW = 128 + 128 + 3 * 512
    TOTAL = 211000 // 4 - RW
    HWA = TOTAL - HNW - NSLOT * SLOTW - KW - CWORDS
    assert HWA >= HW, ("SBUF too small", HWA, HW)
    arena_t = stack.enter_context(nc.sbuf_tensor("arena", [128, TOTAL], F32))
    psum_t = stack.enter_context(nc.psum_tensor("ps", [128, 8, 512], F32))
    r_t = stack.enter_context(nc.sbuf_tensor("f32r", [128, RW], F32R))
    o = 0
    hn_ap = arena_t[:, o:o + HNW].bitcast(BF16).rearrange("p (c t) -> p c t", c=DC)[:, :, 0:T]
    o += HNW
    H_OFF = o
    h_ap = arena_t[:, o:o + HW].rearrange("p (c t) -> p c t", c=DC)
    h_flat = arena_t[:, o:o + HW]
    o += HWA
    slot_aps = []
    for i in range(NSLOT):
        slot_aps.append(arena_t[:, o:o + SLOTW].bitcast(BF16).rearrange("p (k n) -> p k n", k=16))
        o += SLOTW
    K_OFF = o
    o += KW
    C_OFF = o
    o += CWORDS
    assert o <= TOTAL

    def generate(script):
        P = Prog(nc, stack, semreg)
        AH = Arena(arena_t[:, H_OFF:H_OFF + HWA], HWA)
        AK = Arena(arena_t[:, K_OFF:K_OFF + KW], KW)
        AC = Arena(arena_t[:, C_OFF:C_OFF + CWORDS], CWORDS)
        AR = Arena(r_t[:, 0:RW], RW)
        PS = [Buf(psum_t[:, i, :], f"ps{i}", psum=True) for i in range(8)]
        cf_ap = AK.f32(lay["_n"])
        cb_ap = AK.bf16(layb["_n"])
        onesR = AR.f32(128)
        bdR = AR.f32(128)
        esink = AK.f32(NSINK)
        eps_t = AK.f32(2)
        CONST = Buf(None, "const")

        def cfc(name, i=0, n=1, rows=128):
            o_, _ = lay[name]
            return cf_ap[0:rows, o_ + i:o_ + i + n]

        def cbc(name, i=0, n=128, rows=128):
            o_, _ = layb[name]
            return cb_ap[0:rows, o_ + i:o_ + i + n]

        P.dma("sp", [(cf_ap, cf_d), (cb_ap, cb_d)], "ld_c", writes=[CONST])
        P.op("dve", lambda e: e.tensor_copy(out=onesR, in_=cfc("ones", 0, 128)), reads=[CONST], writes=[CONST])
        P.op("dve", lambda e: e.tensor_copy(out=bdR, in_=cfc("bd64", 0, 128)), reads=[CONST], writes=[CONST])
        P.op("dve", lambda e: e.memset(eps_t, EPS), writes=[CONST])
        P.op("act", lambda e: e.activation(out=esink, in_=cfc("sinks", 0, NSINK), func=AF.Exp), reads=[CONST], writes=[CONST])
        ident = cbc("ident")
        onesb = cbc("ones")

        TT = C.TT
        NT = len(TT)
        hB = [[Buf(None, f"h{c}_{t}") for t in range(NT)] for c in range(DC)]
        hnB = [[Buf(None, f"hn{c}_{t}") for t in range(NT)] for c in range(DC)]
        allh = [hB[c][t] for c in range(DC) for t in range(NT)]

        def hsl(c, ti):
            s, n = TT[ti]
            return h_ap[:, c, s:s + n]

        def hnsl(c, ti):
            s, n = TT[ti]
            return hn_ap[:, c, s:s + n]

        xv = xT.rearrange("(c p) t -> p c t", p=128)
        mv = metaT.rearrange("(c p) t -> p c t", p=128)
        step = max(1, DC // 4)
        pairs = [(h_ap[:, :, 0:N_META], mv)]
        for c0 in range(0, DC, step):
            pairs.append((h_ap[:, c0:c0 + step, N_META:T], xv[:, c0:c0 + step, :]))
        P.dma("sp", pairs, "ld_x", writes=allh)

        class PSM:
            def __init__(self):
                self.free = list(range(8))
                self.rr = 0

            def reserve(self, n):
                got = self.free[:n]
                self.free = self.free[n:]
                return [PS[i] for i in got]

            def unreserve(self, bufs):
                for b in bufs:
                    self.free.append(PS.index(b))
                self.free.sort()

            def next(self):
                self.rr = (self.rr + 1) % len(self.free)
                return PS[self.free[self.rr]]
        psm = PSM()

        slots = [Buf(slot_aps[i], f"slot{i}") for i in range(NSLOT)]

        class WQ:
            def __init__(self):
                self.plans = []
                self.slot_of = {}
                self.free = list(range(NSLOT))
                self.next_issue = 0

            def plan(self, loader):
                self.plans.append(loader)
                return len(self.plans) - 1

            def _src(self):
                return script if script is not None else self.plans

            def _issue(self):
                idx = self.next_issue
                assert self.free, "weight slots exhausted (too many held)"
                si = self.free.pop(0)
                sb = slots[si]
                P.dma("pool", self._src()[idx](sb.ap), f"w{si}", writes=[sb])
                self.slot_of[idx] = si
                self.next_issue += 1

            def get(self, idx):
                while self.next_issue <= idx:
                    self._issue()
                src = self._src()
                while self.free and self.next_issue < len(src) and self.next_issue <= idx + AHEAD:
                    self._issue()
                return slots[self.slot_of[idx]]

            def done(self, *idxs):
                for idx in idxs:
                    self.free.append(self.slot_of.pop(idx))
        wq = WQ()

        def wcols(wd, l, kc0, nkc, col0, ncol):
            src = wd[l].rearrange("(k p) n -> p k n", p=128)[:, kc0:kc0 + nkc, col0:col0 + ncol]
            return lambda s: [(s[:, 0:nkc, 0:ncol], src)]

        def mm(out, lhsT, rhs, start, stop):
            return lambda e: e.matmul(out, lhsT=lhsT, rhs=rhs, start=start, stop=stop)

        def i_act(out, in_, func, bias=None, scale=None):
            kw = {}
            if bias is not None:
                kw["bias"] = bias
            if scale is not None:
                kw["scale"] = scale
            return lambda e: e.activation(out=out, in_=in_, func=func, **kw)

        def i_stt(out, in0, scalar, in1, op0, op1):
            return lambda e: e.scalar_tensor_tensor(out=out, in0=in0, scalar=scalar, in1=in1, op0=op0, op1=op1)

        def i_tt(out, in0, in1, op):
            return lambda e: e.tensor_tensor(out=out, in0=in0, in1=in1, op=op)

        def i_ts(out, in0, s1, op0):
            return lambda e: e.tensor_scalar(out=out, in0=in0, scalar1=s1, scalar2=None, op0=op0)

        def i_cp(out, in_):
            return lambda e: e.tensor_copy(out=out, in_=in_)

        def i_recip(out, in_):
            return lambda e: e.reciprocal(out=out, in_=in_)

        def rot(n, w, dt=F32, name="r"):
            tiles = [(AR.f32(w) if dt == F32R else AK.f32(w, dt)) for _ in range(n)]
            bufs = [Buf(tiles[i], f"{name}{i}") for i in range(n)]
            st = {"i": 0}

            def nxt():
                st["i"] = (st["i"] + 1) % n
                return bufs[st["i"]]
            return nxt
        nextsq = rot(3, 512, F32R, "sq")
        nextrs = rot(2, 512, F32, "rs")
        nexttmp = rot(3, 512, F32, "tmp")

        def emit_rstd(stat_ps, rows, n, rs, dim):
            P.op("act", i_act(rs.ap[0:rows, 0:n], stat_ps.ap[0:rows, 0:n], AF.Ln, bias=eps_t[0:rows, 0:1], scale=1.0 / dim),
                 reads=[stat_ps, CONST], writes=[rs])
            P.op("act", i_act(rs.ap[0:rows, 0:n], rs.ap[0:rows, 0:n], AF.Exp, scale=-0.5), reads=[rs], writes=[rs])

        def rmsnorm_tile(gname, l, ti):
            s, n = TT[ti]
            st = psm.next()
            for c in range(DC):
                sq = nextsq()
                P.op("act", i_act(sq.ap[:, 0:n], hsl(c, ti), AF.Square), reads=[hB[c][ti]], writes=[sq])
                P.pe([mm(st.ap[:, 0:n], onesR, sq.ap[:, 0:n], c == 0, c == DC - 1)], reads=[sq, CONST], writes=[st])
            rs = nextrs()
            emit_rstd(st, 128, n, rs, D)
            for c in range(DC):
                P.op("dve", i_stt(hnsl(c, ti), hsl(c, ti), cfc(gname, l * DC + c), rs.ap[:, 0:n], ALU.mult, ALU.mult),
                     reads=[hB[c][ti], rs, CONST], writes=[hnB[c][ti]])

        def rmsnorm_h(gname, l):
            for ti in range(NT):
                rmsnorm_tile(gname, l, ti)

        def ffn(l, wg, wu, wd, gname):
            rmsnorm_tile(gname, l, 0)
            lazy = {"todo": True}
            m = AC.mark()
            FG = C.FG
            a_ap = AC.bf16(FG * T).rearrange("p (f t) -> p f t", f=FG)
            aB = [[Buf(None, f"a{f}_{t}") for t in range(NT)] for f in range(FG)]
            plan = []
            for g in range(C.FC // FG):
                for fl in range(FG):
                    fc = g * FG + fl
                    plan.append(("gu", fl, wq.plan(wcols(wg, l, 0, DC, fc * 128, 128)), wq.plan(wcols(wu, l, 0, DC, fc * 128, 128))))
                for dc in range(DC):
                    plan.append(("dn", dc, wq.plan(wcols(wd, l, g * FG, FG, dc * 128, 128)), None))
            for kind, x_, w1, w2 in plan:
                if kind == "gu":
                    fl = x_
                    sg_ = wq.get(w1)
                    su_ = wq.get(w2)
                    for ti, (s, n) in enumerate(TT):
                        if lazy["todo"] and ti > 0:
                            rmsnorm_tile(gname, l, ti)
                        pg = psm.next()
                        pu = psm.next()
                        rb = [hnB[kc][ti] for kc in range(DC)]
                        P.pe([mm(pg.ap[:, 0:n], sg_.ap[:, kc, :], hnsl(kc, ti), kc == 0, kc == DC - 1) for kc in range(DC)], reads=[sg_] + rb, writes=[pg])
                        P.pe([mm(pu.ap[:, 0:n], su_.ap[:, kc, :], hnsl(kc, ti), kc == 0, kc == DC - 1) for kc in range(DC)], reads=[su_] + rb, writes=[pu])
                        tb = nexttmp()
                        P.op("act", i_act(tb.ap[:, 0:n], pg.ap[:, 0:n], AF.Silu), reads=[pg], writes=[tb])
                        P.op("dve", i_tt(a_ap[:, fl, s:s + n], tb.ap[:, 0:n], pu.ap[:, 0:n], ALU.mult), reads=[tb, pu], writes=[aB[fl][ti]])
                    lazy["todo"] = False
                    wq.done(w1, w2)
                else:
                    dc = x_
                    sd_ = wq.get(w1)
                    for ti, (s, n) in enumerate(TT):
                        po = psm.next()
                        P.pe([mm(po.ap[:, 0:n], sd_.ap[:, fl, :], a_ap[:, fl, s:s + n], fl == 0, fl == FG - 1) for fl in range(FG)],
                             reads=[sd_] + [aB[fl][ti] for fl in range(FG)], writes=[po])
                        P.op("dve", i_stt(hsl(dc, ti), po.ap[:, 0:n], 0.5, hsl(dc, ti), ALU.mult, ALU.add), reads=[po, hB[dc][ti]], writes=[hB[dc][ti]])
                    wq.done(w1)
            P.barrier()
            AC.release(m)

        def proj_fm(ps, n, wslot, m0, M, srcs, src_bufs):
            nk = len(srcs)
            P.pe([mm(ps.ap[0:M, 0:n], wslot.ap[:, kc, m0:m0 + M], srcs[kc], kc == 0, kc == nk - 1) for kc in range(nk)],
                 reads=[wslot] + list(src_bufs), writes=[ps])

        def hn_srcs(ti):
            return [hnsl(kc, ti) for kc in range(DC)], [hnB[kc][ti] for kc in range(DC)]

        def headnorm(ps, rows, n, onesM, dim, gain_ap, out_ap, out_buf):
            sq = nextsq()
            P.op("act", i_act(sq.ap[0:rows, 0:n], ps.ap[0:rows, 0:n], AF.Square), reads=[ps], writes=[sq])
            st = psm.next()
            P.pe([mm(st.ap[:, 0:n], onesM[0:rows, :], sq.ap[0:rows, 0:n], True, True)], reads=[sq, CONST], writes=[st])
            rs = nextrs()
            emit_rstd(st, rows, n, rs, dim)
            P.op("dve", i_stt(out_ap, ps.ap[0:rows, 0:n], gain_ap, rs.ap[0:rows, 0:n], ALU.mult, ALU.mult), reads=[ps, rs, CONST], writes=[out_buf])

        def blk(b):
            return N_META + b * BLK

        def tok_range(bi):
            return (0, N_META) if bi == 0 else (blk(bi - 1), BLK)

        def tiles_of(t0, n):
            return [ti for ti, (s, m_) in enumerate(TT) if s < t0 + n and t0 < s + m_]

        def proj_tm(ps, t0, nt, wslot, ncol, col0=0):
            rb = [hnB[kc][ti] for kc in range(DC) for ti in tiles_of(t0, nt)]
            P.pe([mm(ps.ap[0:nt, 0:ncol], hn_ap[:, kc, t0:t0 + nt], wslot.ap[:, kc, col0:col0 + ncol], kc == 0, kc == DC - 1) for kc in range(DC)],
                 reads=[wslot] + rb, writes=[ps])
        Gch = [float(x) for x in np.exp(128.0 * np.log(_gammas(C)))]

        def ck(name):
            if getattr(C, "stop", None) == name:
                raise _Stop()

        def dump(name, ap3, nch, bufs):
            if getattr(C, "stop", None) != name:
                return
            nch = min(nch, DC)
            yv_ = yT.rearrange("(c p) t -> p c t", p=128)
            ev_ = P.dma("pool", [(yv_[:, c, :], ap3[:, c, N_META:T]) for c in range(nch)], "st_y", reads=list(bufs), writes=[])
            P.wait_on("sp", [ev_])
            raise _Stop("dumped")

        def mixer(l):
            Xl = X[l]
            win = W["w_in"]
            off = C.off
            srcw = win[l].rearrange("(k p) n -> p k n", p=128)
            q4 = HW // 4
            P.dma("sp", [(hsp_d[:, q * q4:(q + 1) * q4], h_flat[:, q * q4:(q + 1) * q4]) for q in range(4)], "hsp", reads=allh, writes=[])
            rmsnorm_h("g_mix", l)
            P.barrier()
            ck("spill")
            mH = AH.mark()
            mC = AC.mark()
            merged = AC.bf16(DC * T).rearrange("p (c t) -> p c t", c=DC)
            mgB = [[Buf(None, f"mg{c}_{t}") for t in range(NT)] for c in range(DC)]
            ROPE = Buf(None, "rope")

            def load_rope(tabs):
                t_ = AH.f32(len(tabs) * T).rearrange("p (a t) -> p a t", a=len(tabs))
                P.dma("sp", [(t_[:, i, :], rope_d[:, tb * T:(tb + 1) * T]) for i, tb in enumerate(tabs)], "ld_r", writes=[ROPE])
                return t_

            def proj_rmsnorm(col0, nch, gname, gidx0, out3, outB):
                widx = [wq.plan(wcols(win, l, 0, DC, col0 + oc * 128, 128)) for oc in range(nch)]
                m_ = AH.mark()
                raw = AH.f32(nch * 512).rearrange("p (c n) -> p c n", c=nch)
                rawB = [Buf(None, "raw") for _ in range(nch)]
                for ti, (s, n) in enumerate(TT):
                    srcs, sb = hn_srcs(ti)
                    st = psm.next()
                    for oc in range(nch):
                        ws = wq.get(widx[oc])
                        ps = psm.next()
                        proj_fm(ps, n, ws, 0, 128, srcs, sb)
                        sq = nextsq()
                        P.op("act", i_act(sq.ap[:, 0:n], ps.ap[:, 0:n], AF.Square), reads=[ps], writes=[sq])
                        P.op("dve", i_ts(raw[:, oc, 0:n], ps.ap[:, 0:n], 1.0, ALU.mult), reads=[ps], writes=[rawB[oc]])
                        P.pe([mm(st.ap[:, 0:n], onesR, sq.ap[:, 0:n], oc == 0, oc == nch - 1)], reads=[sq, CONST], writes=[st])
                    rs = nextrs()
                    emit_rstd(st, 128, n, rs, nch * 128)
                    for oc in range(nch):
                        P.op("dve", i_stt(out3[:, oc, s:s + n], raw[:, oc, 0:n], cfc(gname, gidx0 + oc), rs.ap[:, 0:n], ALU.mult, ALU.mult),
                             reads=[rawB[oc], rs, CONST], writes=[outB[oc][ti]])
                wq.done(*widx)
                P.barrier()
                AH.release(m_)

            def rope_fm(wsrc3, nkc, srcs_fn, col0, rows, cos3, sin3, out_fn, out_buf_fn, gain=None, gain_s=None, norm=False):
                half = rows // 2
                wi = wq.plan(lambda s_: [(s_[:, 0:nkc, 0:rows], wsrc3[:, :, col0:col0 + rows])])
                wsi = wq.plan(lambda s_: [(s_[:, 0:nkc, 0:half], wsrc3[:, :, col0 + half:col0 + rows]),
                                          (s_[:, 0:nkc, half:rows], wsrc3[:, :, col0:col0 + half])])
                for ti, (s, n) in enumerate(TT):
                    srcs, sb = srcs_fn(ti)
                    px_ = psm.next()
                    ppx = psm.next()
                    proj_fm(px_, n, wq.get(wi), 0, rows, srcs, sb)
                    proj_fm(ppx, n, wq.get(wsi), 0, rows, srcs, sb)
                    t1 = nexttmp()
                    t2 = nexttmp()
                    cos_ap = cos3[0:rows, s:s + n]
                    sin_ap = sin3[0:rows, s:s + n]
                    if norm:
                        sq = nextsq()
                        P.op("act", i_act(sq.ap[0:rows, 0:n], px_.ap[0:rows, 0:n], AF.Square), reads=[px_], writes=[sq])
                        st = psm.next()
                        P.pe([mm(st.ap[:, 0:n], onesR[0:rows, :], sq.ap[0:rows, 0:n], True, True)], reads=[sq, CONST], writes=[st])
                        rs = nextrs()
                        emit_rstd(st, rows, n, rs, rows)
                        P.op("dve", i_stt(t1.ap[0:rows, 0:n], px_.ap[0:rows, 0:n], gain, cos_ap, ALU.mult, ALU.mult), reads=[px_, ROPE, CONST], writes=[t1])
                        P.op("dve", i_stt(t2.ap[0:rows, 0:n], ppx.ap[0:rows, 0:n], gain_s, sin_ap, ALU.mult, ALU.mult), reads=[ppx, ROPE, CONST], writes=[t2])
                        P.op("dve", i_tt(t1.ap[0:rows, 0:n], t1.ap[0:rows, 0:n], t2.ap[0:rows, 0:n], ALU.add), reads=[t1, t2], writes=[t1])
                        P.op("dve", i_tt(out_fn(s, n), t1.ap[0:rows, 0:n], rs.ap[0:rows, 0:n], ALU.mult), reads=[t1, rs], writes=[out_buf_fn(ti)])
                    else:
                        P.op("dve", i_tt(t1.ap[0:rows, 0:n], px_.ap[0:rows, 0:n], cos_ap, ALU.mult), reads=[px_, ROPE], writes=[t1])
                        P.op("dve", i_tt(t2.ap[0:rows, 0:n], ppx.ap[0:rows, 0:n], sin_ap, ALU.mult), reads=[ppx, ROPE], writes=[t2])
                        P.op("dve", i_tt(out_fn(s, n), t1.ap[0:rows, 0:n], t2.ap[0:rows, 0:n], ALU.add), reads=[t1, t2], writes=[out_buf_fn(ti)])
                wq.done(wi, wsi)

            RH = C.RET_H
            NKV = C.SWA_KVH
            VW = NKV * 64
            ckvn = AH.bf16(C.KVC * T).rearrange("p (c t) -> p c t", c=C.KVC)
            ckvnB = [[Buf(None, "ckvn") for _ in range(NT)] for _ in range(C.KVC)]
            NKALL = T + C.GS * S
            kr_all = AH.bf16(NKALL)
            krB = [Buf(None, "kr") for _ in range(NT)]
            Mst = AH.f32(RH * 256).rearrange("p (h e) -> p h e", h=RH)
            MB = [Buf(None, "M") for _ in range(RH)]
            mSWA = AH.mark()
            KK = AH.bf16(NKV * T).rearrange("p (g t) -> p g t", g=NKV)
            KKB = [[Buf(None, "KK") for _ in range(NT)] for _ in range(NKV)]
            Vs = AH.bf16((C.NB + 1) * VW).rearrange("p (b n) -> p b n", b=C.NB + 1)
            VsB = [Buf(None, "Vs") for _ in range(C.NB + 1)]
            KKp_all = AH.bf16(C.GS * NKV * 128).rearrange("p (i g n) -> p i g n", i=C.GS, g=NKV)
            Vp_all = AH.bf16(C.GS * VW).rearrange("p (i n) -> p i n", i=C.GS)
            PRV = Buf(None, "prev")
            mM1 = AH.mark()
            Lloc = AH.f32(RH * 256).rearrange("p (h e) -> p h e", h=RH)
            LB = [Buf(None, "L") for _ in range(RH)]
            rope_t = load_rope([0, 1, 2, 3])

            dump("hn", hn_ap, DC, [b_ for r_ in hnB for b_ in r_])
            ck("rope")
            proj_rmsnorm(off["ckv"], C.KVC, "g_kva", l * C.KVC, ckvn, ckvnB)
            ck("m1a")
            dump("ckvn", ckvn, C.KVC, [b_ for r_ in ckvnB for b_ in r_])
            rope_fm(srcw, DC, hn_srcs, off["kr"], 64, rope_t[:, 2, :], rope_t[:, 3, :], lambda s, n: kr_all[0:64, s:s + n], lambda ti: krB[ti],
                    gain=cfc("g_kr", l, rows=64), gain_s=cfc("g_krs", l, rows=64), norm=True)
            for g in range(NKV):
                c0 = off["swa_k"] + g * 64
                wi = wq.plan(lambda s_, c0=c0: [(s_[:, 0:DC, 0:64], srcw[:, :, c0:c0 + 64]), (s_[:, 0:DC, 64:128], srcw[:, :, c0:c0 + 64])])
                for ti, (s, n) in enumerate(TT):
                    srcs, sb = hn_srcs(ti)
                    ps = psm.next()
                    proj_fm(ps, n, wq.get(wi), 0, 128, srcs, sb)
                    headnorm(ps, 128, n, bdR, 64, cfc("g_sk", l), KK[:, g, s:s + n], KKB[g][ti])
                wq.done(wi)
            wv = wq.plan(wcols(win, l, 0, DC, off["swa_v"], VW))
            for bi in range(C.NB + 1):
                t0, nt = tok_range(bi)
                ps = psm.next()
                proj_tm(ps, t0, nt, wq.get(wv), VW)
                P.op("act", i_act(Vs[0:nt, bi, :], ps.ap[0:nt, 0:VW], AF.Copy), reads=[ps], writes=[VsB[bi]])
            wq.done(wv)

            ck("m1c")
            def ret_alloc(want_q):
                KT = AH.bf16(T)
                KTB = [Buf(None, "KT") for _ in range(NT)]
                QT = AH.bf16(T) if want_q else None
                QTB = [Buf(None, "QT") for _ in range(NT)] if want_q else None
                Kz = AH.bf16((C.NB + 1) * 128).rearrange("p (b n) -> p b n", b=C.NB + 1)
                KzB = [Buf(None, "Kz") for _ in range(C.NB + 1)]
                Vr = AH.bf16((C.NB + 1) * 256).rearrange("p (b n) -> p b n", b=C.NB + 1)
                VrB = [Buf(None, "Vr") for _ in range(C.NB + 1)]
                return (KT, KTB, QT, QTB, Kz, KzB, Vr, VrB)

            def ret_kv_head(h, want_q, ropeT, weighted=False, pre=None):
                (KT, KTB, QT, QTB, Kz, KzB, Vr, VrB) = pre
                rope_fm(srcw, DC, hn_srcs, off["ret_k"] + h * 128, 128, ropeT[:, 0, :], ropeT[:, 1, :], lambda s, n: KT[:, s:s + n], lambda ti: KTB[ti])
                if want_q:
                    rope_fm(srcw, DC, hn_srcs, off["ret_q"] + h * 128, 128, ropeT[:, 0, :], ropeT[:, 1, :], lambda s, n: QT[:, s:s + n], lambda ti: QTB[ti])
                wvv = wq.plan(wcols(win, l, 0, DC, off["ret_v"] + h * 256, 128))
                wvv2 = wq.plan(wcols(win, l, 0, DC, off["ret_v"] + h * 256 + 128, 128))
                for bi in range(C.NB + 1):
                    t0, nt = tok_range(bi)
                    pt = psm.next()
                    ptb = pt.ap.bitcast(BF16)
                    P.pe([lambda e, ptb=ptb, t0=t0, nt=nt, KT=KT: e.transpose(out=ptb[0:nt, 0:128], in_=KT[:, t0:t0 + nt], identity=ident)],
                         reads=[KTB[ti] for ti in tiles_of(t0, nt)] + [CONST], writes=[pt])
                    zc = cfc("zeta_m", h, rows=nt) if bi == 0 else (cfc("zeta_w", h * C.NB + bi - 1) if weighted else cfc("zeta", h))
                    P.op("dve", i_ts(Kz[0:nt, bi, :], ptb[0:nt, 0:128], zc, ALU.mult), reads=[pt, CONST], writes=[KzB[bi]])
                    for hf, wslot_i in enumerate((wvv, wvv2)):
                        pv = psm.next()
                        proj_tm(pv, t0, nt, wq.get(wslot_i), 128)
                        P.op("act", i_act(Vr[0:nt, bi, hf * 128:(hf + 1) * 128], pv.ap[0:nt, 0:128], AF.Copy), reads=[pv], writes=[VrB[bi]])
                wq.done(wvv, wvv2)
                return KT, KTB, QT, QTB, Kz, KzB, Vr, VrB

            def kv_update(state_ap, state_buf, Kz, KzB, Vr, VrB, bi, gamma_chunk, first):
                t0, nt = tok_range(bi)
                pk = psm.next()
                P.pe([mm(pk.ap[:, 0:256], Kz[0:nt, bi, :], Vr[0:nt, bi, :], True, True)], reads=[KzB[bi], VrB[bi]], writes=[pk])
                if first:
                    P.op("dve", i_cp(state_ap, pk.ap[:, 0:256]), reads=[pk], writes=[state_buf])
                else:
                    P.op("dve", i_stt(state_ap, state_ap, float(gamma_chunk), pk.ap[:, 0:256], ALU.mult, ALU.add), reads=[pk, state_buf], writes=[state_buf])
            m_p1 = AH.mark()
            pre1 = ret_alloc(False)
            for h in range(RH):
                KT, KTB, _, _, Kz, KzB, Vr, VrB = ret_kv_head(h, False, rope_t, weighted=True, pre=pre1)
                kv_update(Mst[:, h, :], MB[h], Kz, KzB, Vr, VrB, 0, 0.0, True)
                pk = psm.next()
                P.pe([mm(pk.ap[:, 0:256], Kz[:, b + 1, :], Vr[:, b + 1, :], b == 0, b == C.NB - 1) for b in range(C.NB)],
                     reads=[KzB[b + 1] for b in range(C.NB)] + [VrB[b + 1] for b in range(C.NB)], writes=[pk])
                P.op("dve", i_cp(Lloc[:, h, :], pk.ap[:, 0:256]), reads=[pk], writes=[LB[h]])
            P.barrier()
            AH.release(m_p1)

            DX1s, DX1d, DX2s, DX2d, DX3s, DX3d = (Buf(None, "dx") for _ in range(6))
            lastb = C.NB
            P.dma("sp", [(Xl["x1s"].ap().rearrange("(c p) t -> p c t", p=128), ckvn[:, :, N_META:T])], "st_x1",
                  reads=[b_ for r_ in ckvnB for b_ in r_], writes=[DX1s])
            x2s = Xl["x2s"].ap()
            x2d = Xl["x2d"].ap()
            kkrows = 128 * 128 // S

            def x2rows(ap2, base, k):
                return ap2[base + k * kkrows:base + (k + 1) * kkrows, :].rearrange("r (q n) -> (r q) n", n=128)
            prs = [(x2s[0:64, :], kr_all[0:64, N_META:T])]
            for g in range(NKV):
                prs.append((x2rows(x2s, 64, g), KK[:, g, blk(C.NB - 1):blk(C.NB - 1) + BLK]))
            prs.append((x2rows(x2s, 64, NKV), Vs[:, lastb, :]))
            P.dma("sp", prs, "st_x2", reads=krB + [b_ for r_ in KKB for b_ in r_] + [VsB[lastb]], writes=[DX2s])
            P.dma("sp", [(Xl["x3s"].ap(), Lloc.rearrange("p h e -> p (h e)"))], "st_x3", reads=LB, writes=[DX3s])
            for ci, (s_, d_, bs, bd) in enumerate(((Xl["x1s"], Xl["x1d"], DX1s, DX1d), (Xl["x2s"], Xl["x2d"], DX2s, DX2d), (Xl["x3s"], Xl["x3d"], DX3s, DX3d))):
                P.collective(lambda e, s_=s_, d_=d_: e.collective_compute("AllGather", ALU.bypass, replica_groups=groups,
                                                                          ins=[s_.ap().opt()], outs=[d_.ap().opt()]),
                             f"cc{ci}", reads=[bs], writes=[bd])
            P.barrier()
            AH.release(mM1)
            KRG = Buf(None, "krg")
            P.dma("sp", [(kr_all[0:64, T + i * S:T + (i + 1) * S], x2d[i * C.X2A_ROWS:i * C.X2A_ROWS + 64, :]) for i in range(C.GS)],
                  "ld_g", reads=[DX2d], writes=[KRG])
            pr = []
            for i in range(C.GS):
                base = i * C.X2A_ROWS + 64
                for g in range(NKV):
                    pr.append((KKp_all[:, i, g, :], x2rows(x2d, base, g)))
                pr.append((Vp_all[:, i, :], x2rows(x2d, base, NKV)))
            P.dma("sp", pr, "ld_g2", reads=[DX2d], writes=[PRV])

            ck("xchg")

            def branch_merge(r, o3, oB, nkc, wbr):
                for dc in range(DC):
                    wg_i = wq.plan(wcols(win, l, 0, DC, off["gate"] + r * D + dc * 128, 128))
                    wb_i = wq.plan(wcols(wbr, l, 0, nkc, dc * 128, 128))
                    for ti, (s, n) in enumerate(TT):
                        srcs, sb = hn_srcs(ti)
                        pg = psm.next()
                        proj_fm(pg, n, wq.get(wg_i), 0, 128, srcs, sb)
                        py = psm.next()
                        proj_fm(py, n, wq.get(wb_i), 0, 128, [o3[:, kc, s:s + n] for kc in range(nkc)], [oB[kc][ti] for kc in range(nkc)])
                        tb = nexttmp()
                        P.op("act", i_act(tb.ap[:, 0:n], pg.ap[:, 0:n], AF.Sigmoid), reads=[pg], writes=[tb])
                        if r == 0:
                            P.op("dve", i_tt(merged[:, dc, s:s + n], tb.ap[:, 0:n], py.ap[:, 0:n], ALU.mult), reads=[tb, py], writes=[mgB[dc][ti]])
                        else:
                            P.op("dve", i_tt(tb.ap[:, 0:n], tb.ap[:, 0:n], py.ap[:, 0:n], ALU.mult), reads=[tb, py], writes=[tb])
                            P.op("dve", i_tt(merged[:, dc, s:s + n], tb.ap[:, 0:n], merged[:, dc, s:s + n], ALU.add),
                                 reads=[tb, mgB[dc][ti]], writes=[mgB[dc][ti]])
                    wq.done(wg_i, wb_i)

            mSC = AC.mark()
            NQC = C.SWA_H // 2
            o_a = AC.bf16(NQC * T).rearrange("p (c t) -> p c t", c=NQC)
            oaB = [[Buf(None, "oa") for _ in range(NT)] for _ in range(NQC)]
            Qs = AH.bf16(NQC * T).rearrange("p (c t) -> p c t", c=NQC)
            QsB = [[Buf(None, "Qs") for _ in range(NT)] for _ in range(NQC)]
            for c in range(NQC):
                wi = wq.plan(wcols(win, l, 0, DC, off["swa_q"] + c * 128, 128))
                for ti, (s, n) in enumerate(TT):
                    srcs, sb = hn_srcs(ti)
                    ps = psm.next()
                    proj_fm(ps, n, wq.get(wi), 0, 128, srcs, sb)
                    headnorm(ps, 128, n, bdR, 64, cfc("g_sq", l), Qs[:, c, s:s + n], QsB[c][ti])
                wq.done(wi)
            ptile = [AH.bf16(512) for _ in range(3)]
            ptB = [Buf(ptile[i], f"pt{i}") for i in range(3)]
            pti = {"i": 0}

            def nextpt():
                pti["i"] = (pti["i"] + 1) % 3
                return ptB[pti["i"]]
            fin_t = AH.f32(128)
            finB = Buf(fin_t, "fin")
            HPG = C.SWA_H // NKV
            quads = []
            for g in range(NKV):
                hs = list(range(g * HPG, (g + 1) * HPG))
                for q0 in range(0, len(hs), 4):
                    quads.append((g, hs[q0:q0 + 4]))
            po, pl = psm.reserve(2)

            spend = {"p": None}

            def swa_flush():
                if spend["p"] is None:
                    return
                (hp, par, vv, nk, nq, rb, pt, first, last) = spend["p"]
                spend["p"] = None
                fns = []
                for ii, (hq, hh) in enumerate(hp):
                    cs = (hq // 2) * 128
                    fns.append(mm(po.ap[par * 64:(par + 1) * 64, cs:cs + nq], vv, pt.ap[0:nk, ii * 128:ii * 128 + nq], first and ii == 0, last))
                P.pe(fns, reads=rb + [pt], writes=[po])
                fns = []
                for ii, (hq, hh) in enumerate(hp):
                    cs = (hq // 2) * 128
                    fns.append(mm(pl.ap[par * 64:(par + 1) * 64, cs:cs + nq], onesb[0:nk, 0:64], pt.ap[0:nk, ii * 128:ii * 128 + nq], first and ii == 0, last))
                P.pe(fns, reads=[pt, CONST], writes=[pl])

            def swa_block(bi):
                t0, nq = tok_range(bi)
                qtiles = tiles_of(t0, nq)
                for (g, hs) in quads:
                    mrd = [KKB[g][ti] for ti in tiles_of(0, N_META)] + [VsB[0]]
                    kts = []
                    if bi == 0:
                        kts.append((KK[:, g, 0:N_META], Vs[0:N_META, 0, g * 64:(g + 1) * 64], N_META, "cmcur", cfc("zero", 0, rows=N_META), mrd))
                    else:
                        kts.append((KK[:, g, 0:N_META], Vs[0:N_META, 0, g * 64:(g + 1) * 64], N_META, None, cfc("zero", 0, rows=N_META), mrd))
                        kts.append((KK[:, g, t0:t0 + BLK], Vs[:, bi, g * 64:(g + 1) * 64], BLK, "cmcur", cfc("zero"),
                                    [KKB[g][ti] for ti in qtiles] + [VsB[bi]]))
                        if bi >= 2:
                            p0 = blk(bi - 2)
                            kts.append((KK[:, g, p0:p0 + BLK], Vs[:, bi - 1, g * 64:(g + 1) * 64], BLK, "cmprev", cfc("zero"),
                                        [KKB[g][ti] for ti in tiles_of(p0, BLK)] + [VsB[bi - 1]]))
                        else:
                            for i in range(C.GS):
                                kts.append((KKp_all[:, i, g, :], Vp_all[:, i, g * 64:(g + 1) * 64], BLK, "cmprev", cfc("selp", i), [PRV]))
                    nh = len(hs)
                    qrd = [QsB[hh // 2][ti] for hh in hs for ti in qtiles]
                    for ki, (kT, vv, nk, mname, bias, rb) in enumerate(kts):
                        for par in range(2):
                            hp = [(hq, hh) for hq, hh in enumerate(hs) if hh % 2 == par]
                            if not hp:
                                continue
                            ncolp = len(hp) * 128
                            pss = psm.next()
                            fns = []
                            if mname is not None:
                                fns.append(mm(pss.ap[0:nk, 0:ncolp], ident[0:nk, 0:nk], cbc(mname, 0, ncolp, rows=nk), True, False))
                            for ii, (hq, hh) in enumerate(hp):
                                c = hh // 2
                                fns.append(mm(pss.ap[0:nk, ii * 128:ii * 128 + nq], kT[par * 64:(par + 1) * 64, :],
                                              Qs[par * 64:(par + 1) * 64, c, t0:t0 + nq], mname is None and ii == 0, ii == len(hp) - 1))
                            P.pe(fns, reads=rb + qrd + [CONST], writes=[pss])
                            pt = nextpt()
                            P.op("act", i_act(pt.ap[0:nk, 0:ncolp], pss.ap[0:nk, 0:ncolp], AF.Exp, bias=bias, scale=0.125), reads=[pss, CONST], writes=[pt])
                            swa_flush()
                            spend["p"] = (hp, par, vv, nk, nq, rb, pt, ki == 0, ki == len(kts) - 1)
                    swa_flush()
                    for cq in range((nh + 1) // 2):
                        c = hs[2 * cq] // 2
                        cs = cq * 128
                        P.op("dve", i_ts(fin_t[:, 0:nq], pl.ap[:, cs:cs + nq], esink[:, l * NQC + c:l * NQC + c + 1], ALU.add), reads=[pl, CONST], writes=[finB])
                        P.op("dve", i_recip(fin_t[:, 0:nq], fin_t[:, 0:nq]), reads=[finB], writes=[finB])
                        P.op("dve", i_tt(o_a[:, c, t0:t0 + nq], po.ap[:, cs:cs + nq], fin_t[:, 0:nq], ALU.mult), reads=[po, finB],
                             writes=[oaB[c][ti] for ti in qtiles])
            for bi in list(range(2, C.NB + 1)) + [0, 1]:
                swa_block(bi)
            psm.unreserve([po, pl])
            P.barrier()
            dump("o_a", o_a, NQC, [b_ for r_ in oaB for b_ in r_])
            AH.release(mSWA)
            branch_merge(0, o_a, oaB, NQC, W["w_br_swa"])
            P.barrier()
            dump("m0", merged, DC, [b_ for r_ in mgB for b_ in r_])
            AC.release(mSC)

            ck("swa")
            mR = AH.mark()
            mRC = AC.mark()
            NRC = C.RET_OUT // 128
            o_c = AC.bf16(NRC * T).rearrange("p (c t) -> p c t", c=NRC)
            ocB = [[Buf(None, "oc") for _ in range(NT)] for _ in range(NRC)]
            rope2 = load_rope([0, 1])
            st_t = AH.f32(256)
            stB = Buf(st_t, "state")
            sb16 = AH.bf16(256)
            sb16B = Buf(sb16, "state16")
            qx_t = [AH.bf16(128) for _ in range(2)]
            qxB = [Buf(qx_t[i], "qx") for i in range(2)]
            pT_t = [AH.bf16(128) for _ in range(2)]
            pTB = [Buf(pT_t[i], "pT") for i in range(2)]
            mXR = AR.mark()
            Lg = AH.f32(C.GS * 256).rearrange("p (i e) -> p i e", i=C.GS)
            LGB = Buf(None, "Lg")
            x3d = Xl["x3d"].ap()
            pre2 = ret_alloc(True)
            sg = AH.bf16(2 * T).rearrange("p (a t) -> p a t", a=2)
            sgB = [[Buf(None, "sg") for _ in range(NT)] for _ in range(2)]
            xh = AH.bf16(2 * T).rearrange("p (a t) -> p a t", a=2)
            sqh = AH.bf16(2 * T).rearrange("p (a t) -> p a t", a=2)
            xhB = [Buf(None, "xh") for _ in range(NT)]
            sqhB = [Buf(None, "sqh") for _ in range(NT)]
            for h in range(RH):
                P.dma("sp", [(Lg[:, i, :], x3d[i * 128:(i + 1) * 128, h * 256:(h + 1) * 256]) for i in range(C.GS)], "ld_g3", reads=[DX3d], writes=[LGB])
                KT, KTB, QT, QTB, Kz, KzB, Vr, VrB = ret_kv_head(h, True, rope2, pre=pre2)
                for a_ in range(2):
                    wi = wq.plan(wcols(win, l, 0, DC, off["ret_g"] + h * 256 + a_ * 128, 128))
                    for ti, (s, n) in enumerate(TT):
                        srcs, sb = hn_srcs(ti)
                        ps = psm.next()
                        proj_fm(ps, n, wq.get(wi), 0, 128, srcs, sb)
                        P.op("act", i_act(sg[:, a_, s:s + n], ps.ap[:, 0:n], AF.Silu), reads=[ps], writes=[sgB[a_][ti]])
                    wq.done(wi)
                rc0 = h * (1 + C.GS)
                P.op("dve", i_ts(st_t, Mst[:, h, :], cfc("retc", rc0), ALU.mult), reads=[MB[h], CONST], writes=[stB])
                for i in range(C.GS):
                    P.op("dve", i_stt(st_t, Lg[:, i, :], cfc("retc", rc0 + 1 + i), st_t, ALU.mult, ALU.add), reads=[LGB, stB, CONST], writes=[stB])
                def prep(bi):
                    t0, nq = tok_range(bi)
                    qtiles = tiles_of(t0, nq)
                    it = bi % 2
                    if bi == 0:
                        xi_ap = cfc("xi_m", h * 16, 16)
                        dp_ap = cfc("dp_m", h * 16, 16, rows=16)
                    else:
                        xi_ap = cfc("xi", h * 128, 128)
                        dp_ap = cfc("dp", h * 128, 128)
                    P.op("dve", i_tt(qx_t[it][:, 0:nq], QT[:, t0:t0 + nq], xi_ap, ALU.mult), reads=[QTB[ti] for ti in qtiles] + [CONST], writes=[qxB[it]])
                    pss = psm.next()
                    P.pe([mm(pss.ap[0:nq, 0:nq], KT[:, t0:t0 + nq], qx_t[it][:, 0:nq], True, True)], reads=[KTB[ti] for ti in qtiles] + [qxB[it]], writes=[pss])
                    P.op("dve", i_tt(pT_t[it][0:nq, 0:nq], pss.ap[0:nq, 0:nq], dp_ap, ALU.mult), reads=[pss, CONST], writes=[pTB[it]])

                def main(bi):
                    t0, nq = tok_range(bi)
                    qtiles = tiles_of(t0, nq)
                    it = bi % 2
                    if bi > 0:
                        P.op("act", i_act(sb16, st_t, AF.Copy), reads=[stB], writes=[sb16B])
                    pos_ = [psm.next(), psm.next()]
                    for e2 in range(2):
                        fns = [mm(pos_[e2].ap[:, 0:nq], Vr[0:nq, bi, e2 * 128:(e2 + 1) * 128], pT_t[it][0:nq, 0:nq], True, bi == 0)]
                        rd = [VrB[bi], pTB[it]]
                        if bi > 0:
                            fns.append(mm(pos_[e2].ap[:, 0:nq], sb16[:, e2 * 128:(e2 + 1) * 128], qx_t[it][:, 0:nq], False, True))
                            rd += [sb16B, qxB[it]]
                        P.pe(fns, reads=rd, writes=[pos_[e2]])
                    for e2 in range(2):
                        P.op("act", i_act(xh[:, e2, t0:t0 + nq], pos_[e2].ap[:, 0:nq], AF.Copy), reads=[pos_[e2]], writes=[xhB[ti] for ti in qtiles])
                        P.op("act", i_act(sqh[:, e2, t0:t0 + nq], pos_[e2].ap[:, 0:nq], AF.Square), reads=[pos_[e2]], writes=[sqhB[ti] for ti in qtiles])
                    if 0 < bi < C.NB:
                        kv_update(st_t, stB, Kz, KzB, Vr, VrB, bi, Gch[h], False)
                prep(0)
                for bi in range(0, C.NB + 1):
                    if bi + 1 <= C.NB:
                        prep(bi + 1)
                    main(bi)
                for ti, (s, n) in enumerate(TT):
                    s1 = psm.next()
                    s2 = psm.next()
                    P.pe([mm(s1.ap[:, 0:n], onesb, xh[:, e2, s:s + n], e2 == 0, e2 == 1) for e2 in range(2)], reads=[xhB[ti], CONST], writes=[s1])
                    P.pe([mm(s2.ap[:, 0:n], onesb, sqh[:, e2, s:s + n], e2 == 0, e2 == 1) for e2 in range(2)], reads=[sqhB[ti], CONST], writes=[s2])
                    mean = nexttmp()
                    var = nexttmp()
                    P.op("dve", i_ts(mean.ap[:, 0:n], s1.ap[:, 0:n], 1.0 / 256, ALU.mult), reads=[s1], writes=[mean])
                    P.op("dve", i_tt(var.ap[:, 0:n], mean.ap[:, 0:n], mean.ap[:, 0:n], ALU.mult), reads=[mean], writes=[var])
                    P.op("dve", i_stt(var.ap[:, 0:n], s2.ap[:, 0:n], 1.0 / 256, var.ap[:, 0:n], ALU.mult, ALU.subtract), reads=[s2, var], writes=[var])
                    P.op("act", i_act(var.ap[:, 0:n], var.ap[:, 0:n], AF.Ln, bias=eps_t[:, 0:1], scale=1.0), reads=[var, CONST], writes=[var])
                    P.op("act", i_act(var.ap[:, 0:n], var.ap[:, 0:n], AF.Exp, scale=-0.5), reads=[var], writes=[var])
                    for e2 in range(2):
                        tmpa = nexttmp()
                        P.op("dve", i_tt(tmpa.ap[:, 0:n], xh[:, e2, s:s + n], mean.ap[:, 0:n], ALU.subtract), reads=[xhB[ti], mean], writes=[tmpa])
                        P.op("dve", i_tt(tmpa.ap[:, 0:n], tmpa.ap[:, 0:n], var.ap[:, 0:n], ALU.mult), reads=[tmpa, var], writes=[tmpa])
                        P.op("dve", i_stt(o_c[:, 2 * h + e2, s:s + n], tmpa.ap[:, 0:n], cfc("g_gn", l * NRC + 2 * h + e2), sg[:, e2, s:s + n], ALU.mult, ALU.mult),
                             reads=[tmpa, CONST, sgB[e2][ti]], writes=[ocB[2 * h + e2][ti]])
            P.barrier()
            AH.release(mR)
            AR.release(mXR)
            dump("o_c", o_c, NRC, [b_ for r_ in ocB for b_ in r_])
            branch_merge(2, o_c, ocB, NRC, W["w_br_ret"])
            P.barrier()
            dump("m02", merged, DC, [b_ for r_ in mgB for b_ in r_])
            AC.release(mRC)

            ck("ret")
            mM = AH.mark()
            mMC = AC.mark()
            NMC = C.MLA_OUT // 128
            MH = C.MLA_H
            o_b = AC.bf16(NMC * T).rearrange("p (c t) -> p c t", c=NMC)
            obB = [[Buf(None, "ob") for _ in range(NT)] for _ in range(NMC)]
            cqn = AH.bf16(C.QRC * T).rearrange("p (c t) -> p c t", c=C.QRC)
            cqnB = [[Buf(None, "cqn") for _ in range(NT)] for _ in range(C.QRC)]
            proj_rmsnorm(off["cq"], C.QRC, "g_qa", l * C.QRC, cqn, cqnB)
            rope3 = load_rope([2, 3])
            wuq = W["mla_w_uq"]
            wukv = W["mla_w_ukv"]
            wuq3 = wuq[l].rearrange("(k p) n -> p k n", p=128)
            qn_t = [AH.bf16(T) for _ in range(2)]
            qr_t = [AH.bf16(T) for _ in range(2)]
            qnB = [[Buf(None, "qn") for _ in range(NT)] for _ in range(2)]
            qrB = [[Buf(None, "qr") for _ in range(NT)] for _ in range(2)]
            cg_t = AH.bf16(C.KVC * S).rearrange("p (c t) -> p c t", c=C.KVC)
            cgB = Buf(None, "cg")
            Kh_t = [AH.bf16(S) for _ in range(2)]
            KhB = [Buf(None, "Kh") for _ in range(2)]
            Vh_t = [AH.bf16(S).rearrange("p (b n) -> p b n", n=128) for _ in range(2)]
            VhB = [Buf(None, "Vh") for _ in range(2)]
            Pm_t = [AH.bf16(512) for _ in range(2)]
            PmB = [Buf(Pm_t[i], "Pm") for i in range(2)]
            pmi = {"i": 0}
            wkv_t = [AH.bf16(2 * C.KVC * 128).rearrange("p (a k n) -> p a k n", a=2, k=C.KVC) for _ in range(2)]
            wkvB = [Buf(None, "wkv") for _ in range(2)]
            Khm = AH.bf16(N_META)
            Vhm = AH.bf16(128)
            KhmB = Buf(None, "Khm")
            VhmB = Buf(None, "Vhm")

            def nextpm():
                pmi["i"] = (pmi["i"] + 1) % 2
                return PmB[pmi["i"]]
            acc = psm.reserve(2 * C.NQT)
            scale = (C.NOPE + C.ROPE) ** -0.5
            x1d = Xl["x1d"].ap()
            srcs_list = ["own"] + list(range(C.GS - 1))
            items = [(h, sidx) for h in range(MH) for sidx in range(len(srcs_list))]
            QW = C.QB * BLK
            wukv3 = wukv[l].rearrange("(k p) n -> p k n", p=128)

            def mla_q(h):
                ib = h % 2
                P.dma("pool", [(wkv_t[ib][:, 0, :, :], wukv3[:, :, h * 256:h * 256 + 128]), (wkv_t[ib][:, 1, :, :], wukv3[:, :, h * 256 + 128:h * 256 + 256])],
                      f"wkv{ib}", reads=[cqnB[0][0]], writes=[wkvB[ib]])
                wn = wq.plan(wcols(wuq, l, 0, C.QRC, h * 192, 128))

                def cq_srcs(ti):
                    s, n = TT[ti]
                    return [cqn[:, kc, s:s + n] for kc in range(C.QRC)], [cqnB[kc][ti] for kc in range(C.QRC)]
                for ti, (s, n) in enumerate(TT):
                    srcs, sb = cq_srcs(ti)
                    ps = psm.next()
                    proj_fm(ps, n, wq.get(wn), 0, 128, srcs, sb)
                    headnorm(ps, 128, n, onesR, 128, cfc("g_qn", l), qn_t[ib][:, s:s + n], qnB[ib][ti])
                wq.done(wn)
                rope_fm(wuq3, C.QRC, cq_srcs, h * 192 + 128, 64, rope3[:, 0, :], rope3[:, 1, :], lambda s, n: qr_t[ib][0:64, s:s + n], lambda ti: qrB[ib][ti],
                        gain=cfc("g_qr", l, rows=64), gain_s=cfc("g_qrs", l, rows=64), norm=True)

            def mla_expand(it_idx):
                h, sidx = items[it_idx]
                src = srcs_list[sidx]
                ib = it_idx % 2
                wk = wkv_t[h % 2][:, 0]
                wv_ = wkv_t[h % 2][:, 1]
                wb = wkvB[h % 2]
                if src == "own":
                    def csl(kc, k0, n):
                        return ckvn[:, kc, N_META + k0:N_META + k0 + n]

                    def crd(k0, n):
                        return [ckvnB[kc][ti] for kc in range(C.KVC) for ti in tiles_of(N_META + k0, n)]
                else:
                    P.dma("sp", [(cg_t, x1d[src * C.KVR:(src + 1) * C.KVR, :].rearrange("(c p) t -> p c t", p=128))], "ld_cg", reads=[DX1d], writes=[cgB])

                    def csl(kc, k0, n):
                        return cg_t[:, kc, k0:k0 + n]

                    def crd(k0, n):
                        return [cgB]
                for k0 in range(0, S, 512):
                    n = min(512, S - k0)
                    ps = psm.next()
                    P.pe([mm(ps.ap[:, 0:n], wk[:, kc, :], csl(kc, k0, n), kc == 0, kc == C.KVC - 1) for kc in range(C.KVC)], reads=[wb] + crd(k0, n), writes=[ps])
                    headnorm(ps, 128, n, onesR, 128, cfc("g_kn", l), Kh_t[ib][:, k0:k0 + n], KhB[ib])
                for kb in range(C.NB):
                    ps = psm.next()
                    P.pe([mm(ps.ap[:, 0:128], csl(kc, kb * 128, 128), wv_[:, kc, :], kc == 0, kc == C.KVC - 1) for kc in range(C.KVC)],
                         reads=[wb] + crd(kb * 128, 128), writes=[ps])
                    P.op("act", i_act(Vh_t[ib][:, kb, :], ps.ap[:, 0:128], AF.Copy), reads=[ps], writes=[VhB[ib]])

            def mla_expand_meta(h):
                wk = wkv_t[h % 2][:, 0]
                wv_ = wkv_t[h % 2][:, 1]
                wb = wkvB[h % 2]
                mrd = [ckvnB[kc][ti] for kc in range(C.KVC) for ti in tiles_of(0, N_META)]
                ps = psm.next()
                P.pe([mm(ps.ap[:, 0:N_META], wk[:, kc, :], ckvn[:, kc, 0:N_META], kc == 0, kc == C.KVC - 1) for kc in range(C.KVC)], reads=[wb] + mrd, writes=[ps])
                headnorm(ps, 128, N_META, onesR, 128, cfc("g_kn", l), Khm[:, 0:N_META], KhmB)
                ps = psm.next()
                P.pe([mm(ps.ap[0:N_META, 0:128], ckvn[:, kc, 0:N_META], wv_[:, kc, :], kc == 0, kc == C.KVC - 1) for kc in range(C.KVC)], reads=[wb] + mrd, writes=[ps])
                P.op("act", i_act(Vhm[0:N_META, :], ps.ap[0:N_META, 0:128], AF.Copy), reads=[ps], writes=[VhmB])

            pend = {"p": None}

            def att_flush():
                if pend["p"] is not None:
                    (Ops, Lps, vv, nk, nq, rb, pm, first, last) = pend["p"]
                    P.pe([mm(Ops.ap[:, 0:nq], vv, pm.ap[0:nk, 0:nq], first, last)], reads=rb + [pm], writes=[Ops])
                    P.pe([mm(Lps.ap[:, 0:nq], onesb[0:nk, :], pm.ap[0:nk, 0:nq], first, last)], reads=[pm, CONST], writes=[Lps])
                    pend["p"] = None

            def att_step(h, qsl, nq, qrd, Ops, Lps, kT, krT, vv, nk, bias, mask_ap, rb, first, last):
                ibq = h % 2
                pss = psm.next()
                fns = [mm(pss.ap[0:nk, 0:nq], kT, qn_t[ibq][:, qsl:qsl + nq], True, False),
                       mm(pss.ap[0:nk, 0:nq], krT, qr_t[ibq][0:64, qsl:qsl + nq], False, mask_ap is None)]
                if mask_ap is not None:
                    fns.append(mm(pss.ap[0:nk, 0:nq], ident[0:nk, 0:nk], mask_ap, False, True))
                P.pe(fns, reads=rb + qrd + [CONST, KRG], writes=[pss])
                pm = nextpm()
                P.op("act", i_act(pm.ap[0:nk, 0:nq], pss.ap[0:nk, 0:nq], AF.Exp, bias=bias, scale=scale), reads=[pss, CONST], writes=[pm])
                att_flush()
                pend["p"] = (Ops, Lps, vv, nk, nq, rb, pm, first, last)

            def finish(h, Ops, Lps, qsl, nq):
                rl = nexttmp()
                P.op("dve", i_recip(rl.ap[:, 0:nq], Lps.ap[:, 0:nq]), reads=[Lps], writes=[rl])
                P.op("dve", i_tt(o_b[:, h, qsl:qsl + nq], Ops.ap[:, 0:nq], rl.ap[:, 0:nq], ALU.mult), reads=[Ops, rl],
                     writes=[obB[h][ti] for ti in tiles_of(qsl, nq)])

            def mla_attend(it_idx):
                h, sidx = items[it_idx]
                src = srcs_list[sidx]
                ib = it_idx % 2
                ibq = h % 2
                nsrc = len(srcs_list)
                mkrd = [KhmB, VhmB] + [krB[ti] for ti in tiles_of(0, N_META)]
                for qt in range(C.NQT):
                    qsl = blk(qt * C.QB)
                    qrd = [qnB[ibq][ti] for ti in tiles_of(qsl, QW)] + [qrB[ibq][ti] for ti in tiles_of(qsl, QW)]
                    Ops, Lps = acc[2 * qt], acc[2 * qt + 1]
                    if src == "own":
                        att_step(h, qsl, QW, qrd, Ops, Lps, Khm[:, 0:N_META], kr_all[0:64, 0:N_META], Vhm[0:N_META, :], N_META,
                                 cfc("zero", 0, rows=N_META), None, mkrd, True, False)
                        for kb in range((qt + 1) * C.QB):
                            r = kb - qt * C.QB
                            mask_ap = cbc("mm", r * C.QB * 128, QW) if r >= 0 else None
                            att_step(h, qsl, QW, qrd, Ops, Lps, Kh_t[ib][:, kb * 128:(kb + 1) * 128], kr_all[0:64, blk(kb):blk(kb) + 128],
                                     Vh_t[ib][:, kb, :], 128, cfc("zero"), mask_ap, [KhB[ib], VhB[ib]] + [krB[ti] for ti in tiles_of(blk(kb), 128)], False, False)
                    else:
                        for kb in range(C.NB):
                            lastk = (sidx == nsrc - 1) and (kb == C.NB - 1)
                            att_step(h, qsl, QW, qrd, Ops, Lps, Kh_t[ib][:, kb * 128:(kb + 1) * 128],
                                     kr_all[0:64, T + src * S + kb * 128:T + src * S + (kb + 1) * 128],
                                     Vh_t[ib][:, kb, :], 128, cfc("vis", src), None, [KhB[ib], VhB[ib]], False, lastk)
                if src == "own":
                    Om = psm.next()
                    Lm = psm.next()
                    qrd = [qnB[ibq][ti] for ti in tiles_of(0, N_META)] + [qrB[ibq][ti] for ti in tiles_of(0, N_META)]
                    att_step(h, 0, N_META, qrd, Om, Lm, Khm[:, 0:N_META], kr_all[0:64, 0:N_META], Vhm[0:N_META, :], N_META,
                             cfc("zero", 0, rows=N_META), cbc("cmcur", 0, N_META, rows=N_META), mkrd, True, True)
                    att_flush()
                    finish(h, Om, Lm, 0, N_META)

            nI = len(items)
            mla_q(0)
            mla_expand(0)
            mla_expand_meta(0)
            for it_idx in range(nI):
                h, sidx = items[it_idx]
                if it_idx + 1 < nI:
                    h2, s2 = items[it_idx + 1]
                    if s2 == 0:
                        mla_q(h2)
                    mla_expand(it_idx + 1)
                mla_attend(it_idx)
                att_flush()
                if sidx == len(srcs_list) - 1:
                    for qt in range(C.NQT):
                        finish(h, acc[2 * qt], acc[2 * qt + 1], blk(qt * C.QB), QW)
                    if it_idx + 1 < nI:
                        mla_expand_meta(h + 1)
            psm.unreserve(acc)
            P.barrier()
            AH.release(mM)
            dump("o_b", o_b, NMC, [b_ for r_ in obB for b_ in r_])
            branch_merge(1, o_b, obB, NMC, W["w_br_mla"])
            P.barrier()
            dump("merged", merged, DC, [b_ for r_ in mgB for b_ in r_])
            AC.release(mMC)

            ck("mla")
            AH.release(mH)
            P.dma("sp", [(h_flat[:, q * q4:(q + 1) * q4], hsp_d[:, q * q4:(q + 1) * q4]) for q in range(4)], "hld", reads=[], writes=allh)
            for dc in range(DC):
                wi = wq.plan(wcols(W["w_o"], l, 0, DC, dc * 128, 128))
                for ti, (s, n) in enumerate(TT):
                    po_ = psm.next()
                    proj_fm(po_, n, wq.get(wi), 0, 128, [merged[:, kc, s:s + n] for kc in range(DC)], [mgB[kc][ti] for kc in range(DC)])
                    P.op("dve", i_tt(hsl(dc, ti), po_.ap[:, 0:n], hsl(dc, ti), ALU.add), reads=[po_, hB[dc][ti]], writes=[hB[dc][ti]])
                wq.done(wi)
            P.barrier()
            AC.release(mC)

        try:
            ck("load")
            for l in range(L):
                ffn(l, W["ffn1_w_gate"], W["ffn1_w_up"], W["ffn1_w_down"], "g_ffn1")
                ck("ffn1")
                mixer(l)
                ck("mixer")
                ffn(l, W["ffn2_w_gate"], W["ffn2_w_up"], W["ffn2_w_down"], "g_ffn2")
        except _Stop as e_:
            P.barrier()
            if e_.args:
                return P, wq.plans, (AH.peak, AK.peak, AC.peak)

        yv = yT.rearrange("(c p) t -> p c t", p=128)
        step = max(1, DC // 4)
        ev = P.dma("sp", [(yv[:, c0:c0 + step, :], h_ap[:, c0:c0 + step, N_META:T]) for c0 in range(0, DC, step)], "st_y", reads=allh, writes=[])
        P.wait_on("sp", [ev])
        return P, wq.plans, (AH.peak, AK.peak, AC.peak)

    P1, script, _ = generate(None)
    P2, script2, peaks = generate(script)
    assert len(script2) == len(script)
    if getattr(cfg, "dbgdump", None):
        with open(cfg.dbgdump, "w") as f:
            for d in P2.dbg:
                f.write(repr(d) + "\n")
    P2.emit()
    return nc, stack, peaks


def weight_shapes(C):
    L, D = C.DEPTH, C.D
    return [("ffn1_w_gate", [L, D, C.DFF]), ("ffn1_w_up", [L, D, C.DFF]), ("ffn1_w_down", [L, C.DFF, D]),
            ("w_in", [L, D, C.IN_W]), ("mla_w_uq", [L, C.QR, C.MLA_H * 192]),
            ("mla_w_ukv", [L, C.KVR, C.MLA_H * 256]), ("w_br_swa", [L, C.SWA_OUT, D]),
            ("w_br_mla", [L, C.MLA_OUT, D]), ("w_br_ret", [L, C.RET_OUT, D]), ("w_o", [L, D, D]),
            ("ffn2_w_gate", [L, D, C.DFF]), ("ffn2_w_up", [L, D, C.DFF]), ("ffn2_w_down", [L, C.DFF, D])]


def run_cfg(cfg, inputs, trace=False):
    x = np.asarray(inputs["x"], np.float32)
    meta = np.asarray(inputs["meta_tokens"], np.float32)
    S = cfg.S
    wnames = [n for n, _ in weight_shapes(cfg)]
    wts = {n: np.ascontiguousarray(np.asarray(inputs[n], np.float32)) for n in wnames}
    metaT = np.ascontiguousarray(meta.T)
    consts = [host_consts(cfg, inputs, j) for j in range(cfg.GS)]
    in_maps = []
    for core in range(cfg.NCORE):
        b, j = divmod(core, cfg.GS)
        cf, cb, rope = consts[j]
        m = {"xT": np.ascontiguousarray(x[b, j * S:(j + 1) * S, :].T), "metaT": metaT, "cf": cf, "cb": cb, "rope": rope}
        m.update(wts)
        in_maps.append(m)
    nc, stack, peaks = build_program(cfg)
    with stack:
        res = run_bass_kernel_spmd(nc, in_maps, core_ids=list(range(cfg.NCORE)), trace=trace)
    out = np.zeros((cfg.NBATCH, cfg.GS * S, cfg.D), np.float32)
    for core in range(cfg.NCORE):
        b, j = divmod(core, cfg.GS)
        out[b, j * S:(j + 1) * S, :] = res.results[core]["yT"].T
    return out, res


def kernel(**inputs):
    out, _ = run_cfg(FULL, inputs)
    return out
```

```python
import math
import contextlib
import numpy as np
import ml_dtypes
import concourse.bass as bass
import concourse.mybir as mybir
from concourse.bass_utils import run_bass_kernel_spmd

F32 = mybir.dt.float32
BF16 = mybir.dt.bfloat16
F32R = mybir.dt.float32r
AF = mybir.ActivationFunctionType
ALU = mybir.AluOpType

NEG = -30000.0
EPS = 1e-6
ROPE_BASE = 10000.0
N_META = 16
BLK = 128


class Cfg:
    def __init__(self, D=2048, DFF=5632, DEPTH=4, NB=8, SWA_H=16, SWA_KVH=2, MLA_H=8, QR=512, KVR=512,
                 RET_H=4, FG=11, GS=4, NBATCH=2):
        self.D, self.DFF, self.DEPTH, self.NB = D, DFF, DEPTH, NB
        self.SWA_H, self.SWA_KVH, self.SWA_D = SWA_H, SWA_KVH, 64
        self.MLA_H, self.QR, self.KVR = MLA_H, QR, KVR
        self.NOPE, self.ROPE, self.VD = 128, 64, 128
        self.RET_H, self.DK, self.DV = RET_H, 128, 256
        self.FG, self.GS, self.NBATCH = FG, GS, NBATCH
        self.NCORE = GS * NBATCH
        self.DC = D // 128
        self.FC = DFF // 128
        assert self.FC % FG == 0
        self.S = NB * BLK
        self.T = N_META + self.S
        self.SWA_OUT = SWA_H * 64
        self.SWA_KVW = SWA_KVH * 64
        self.MLA_OUT = MLA_H * 128
        self.RET_QK = RET_H * 128
        self.RET_OUT = RET_H * 256
        self.QRC, self.KVC = QR // 128, KVR // 128
        w = [self.SWA_OUT, self.SWA_KVW, self.SWA_KVW, QR, KVR, 64, self.RET_QK, self.RET_QK,
             self.RET_OUT, self.RET_OUT, 3 * D]
        names = ["swa_q", "swa_k", "swa_v", "cq", "ckv", "kr", "ret_q", "ret_k", "ret_v", "ret_g", "gate"]
        self.off = {}
        o = 0
        for n_, w_ in zip(names, w):
            self.off[n_] = o
            o += w_
        self.IN_W = o
        self.QB = min(4, NB)
        self.NQT = NB // self.QB
        assert NB % self.QB == 0
        nt = -(-self.T // 512)
        base = -(-self.T // nt)
        base = -(-base // 8) * 8
        self.TT = []
        s = 0
        while s < self.T:
            n = min(base, self.T - s)
            self.TT.append((s, n))
            s += n
        self.X2A_ROWS = 64 + (2 * 128 * 128 + 128 * 128) // self.S
        assert (3 * 128 * 128) % self.S == 0


FULL = Cfg()


def cf_layout(cfg):
    lay = {}
    o = 0

    def add(name, n):
        nonlocal o
        lay[name] = (o, n)
        o += n
    L = cfg.DEPTH
    add("g_ffn1", L * cfg.DC)
    add("g_mix", L * cfg.DC)
    add("g_ffn2", L * cfg.DC)
    add("g_qa", L * cfg.QRC)
    add("g_kva", L * cfg.KVC)
    add("g_qn", L)
    add("g_kn", L)
    add("g_qr", L)
    add("g_qrs", L)
    add("g_kr", L)
    add("g_krs", L)
    add("g_sq", L)
    add("g_sk", L)
    add("g_gn", L * (cfg.RET_OUT // 128))
    add("sinks", L * (cfg.SWA_H // 2))
    add("zeta", cfg.RET_H)
    add("zeta_m", cfg.RET_H)
    add("zeta_w", cfg.RET_H * cfg.NB)
    add("retc", cfg.RET_H * (1 + cfg.GS))
    add("vis", cfg.GS)
    add("selp", cfg.GS)
    add("zero", 1)
    add("ones", 128)
    add("bd64", 128)
    add("xi", cfg.RET_H * 128)
    add("xi_m", cfg.RET_H * 16)
    add("dp", cfg.RET_H * 128)
    add("dp_m", cfg.RET_H * 16)
    lay["_n"] = o
    return lay


def cb_layout(cfg):
    lay = {}
    o = 0

    def add(name, n):
        nonlocal o
        lay[name] = (o, n)
        o += n
    add("ident", 128)
    add("ones", 128)
    add("cmcur", 512)
    add("cmprev", 512)
    add("mm", cfg.QB * cfg.QB * 128)
    lay["_n"] = o
    return lay


def _gammas(cfg):
    h = np.arange(cfg.RET_H, dtype=np.float64)
    return 1.0 - 2.0 ** (-5.0 - h)


def host_consts(cfg, inputs, j):
    lay = cf_layout(cfg)
    cf = np.zeros((128, lay["_n"]), np.float32)
    L = cfg.DEPTH

    def put(name, arr):
        o, n = lay[name]
        arr = np.asarray(arr, np.float32)
        assert arr.shape[1] == n, (name, arr.shape, n)
        cf[:arr.shape[0], o:o + n] = arr

    def chunks(v):
        v = np.asarray(v, np.float32)
        Lq, W = v.shape
        return v.reshape(Lq, W // 128, 128).transpose(2, 0, 1).reshape(128, -1)
    put("g_ffn1", chunks(inputs["ffn1_norm"]))
    put("g_mix", chunks(inputs["mix_norm"]))
    put("g_ffn2", chunks(inputs["ffn2_norm"]))
    put("g_qa", chunks(inputs["mla_q_a_norm"]))
    put("g_kva", chunks(inputs["mla_kv_a_norm"]))
    put("g_qn", np.asarray(inputs["mla_qn_norm"], np.float32).T)
    put("g_kn", np.asarray(inputs["mla_kn_norm"], np.float32).T)
    qr = np.asarray(inputs["mla_qr_norm"], np.float32).T
    kr = np.asarray(inputs["mla_kr_norm"], np.float32).T
    put("g_qr", qr)
    put("g_qrs", np.concatenate([qr[32:], qr[:32]], 0))
    put("g_kr", kr)
    put("g_krs", np.concatenate([kr[32:], kr[:32]], 0))
    sq = np.asarray(inputs["swa_q_norm"], np.float32).T
    sk = np.asarray(inputs["swa_k_norm"], np.float32).T
    put("g_sq", np.concatenate([sq, sq], 0))
    put("g_sk", np.concatenate([sk, sk], 0))
    put("g_gn", chunks(inputs["ret_gn"]))
    sinks = np.asarray(inputs["swa_sinks"], np.float32)
    sk2 = np.zeros((128, L * (cfg.SWA_H // 2)), np.float32)
    for l in range(L):
        for c in range(cfg.SWA_H // 2):
            sk2[:64, l * (cfg.SWA_H // 2) + c] = sinks[l, 2 * c]
            sk2[64:, l * (cfg.SWA_H // 2) + c] = sinks[l, 2 * c + 1]
    put("sinks", sk2)
    g = _gammas(cfg)
    lg = np.log(g)
    k = np.arange(128, dtype=np.float64)
    sc = cfg.DK ** -0.5
    put("zeta", np.exp((127.0 - k)[:, None] * lg[None, :]) * sc)
    zm = np.zeros((128, cfg.RET_H))
    zm[:16] = np.exp((15.0 - np.arange(16.0))[:, None] * lg[None, :]) * sc
    put("zeta_m", zm)
    zw = np.zeros((128, cfg.RET_H, cfg.NB))
    for b_ in range(cfg.NB):
        zw[:, :, b_] = np.exp((127.0 - k)[:, None] * lg[None, :] + 128.0 * (cfg.NB - 1 - b_) * lg[None, :]) * sc
    put("zeta_w", zw.reshape(128, -1))
    G = np.exp(128.0 * lg)
    retc = np.zeros((128, cfg.RET_H, 1 + cfg.GS))
    for h in range(cfg.RET_H):
        retc[:, h, 0] = G[h] ** (cfg.NB * j)
        for i in range(cfg.GS):
            retc[:, h, 1 + i] = G[h] ** (cfg.NB * (j - 1 - i)) if i < j else 0.0
    put("retc", retc.reshape(128, -1))
    vis = np.zeros((128, cfg.GS))
    for i in range(cfg.GS):
        vis[:, i] = 0.0 if i < j else NEG
    put("vis", vis)
    selp = np.full((128, cfg.GS), NEG)
    if j > 0:
        selp[:, j - 1] = 0.0
    put("selp", selp)
    put("zero", np.zeros((128, 1)))
    put("ones", np.ones((128, 128)))
    bd = np.zeros((128, 128))
    bd[:64, :64] = 1
    bd[64:, 64:] = 1
    put("bd64", bd)
    q = np.arange(128, dtype=np.float64)
    xi = np.exp((q + 1.0)[None, :] * lg[:, None])
    put("xi", np.broadcast_to(xi.reshape(1, -1), (128, cfg.RET_H * 128)))
    qm = np.arange(16, dtype=np.float64)
    xim = np.exp((113.0 + qm)[None, :] * lg[:, None])
    put("xi_m", np.broadcast_to(xim.reshape(1, -1), (128, cfg.RET_H * 16)))
    dp = np.zeros((128, cfg.RET_H, 128))
    dpm = np.zeros((128, cfg.RET_H, 16))
    for h in range(cfg.RET_H):
        dp[:, h, :] = np.exp(-(k + 1.0) * lg[h])[:, None] * sc * (q[None, :] >= k[:, None])
        dpm[:16, h, :] = (np.exp(-(113.0 + qm) * lg[h])[:, None] * sc * (qm[None, :] >= qm[:, None]))
    put("dp", dp.reshape(128, -1))
    put("dp_m", dpm.reshape(128, -1))

    layb = cb_layout(cfg)
    cb = np.zeros((128, layb["_n"]), np.float32)

    def putb(name, arr):
        o, n = layb[name]
        assert arr.shape == (128, n), (name, arr.shape)
        cb[:, o:o + n] = arr
    putb("ident", np.eye(128))
    putb("ones", np.ones((128, 128)))
    kk = np.arange(128)[:, None]
    qq = np.arange(128)[None, :]
    cur = np.where(qq >= kk, 0.0, NEG)
    prev = np.where(kk > qq, 0.0, NEG)
    putb("cmcur", np.tile(cur, (1, 4)))
    putb("cmprev", np.tile(prev, (1, 4)))
    mm = np.zeros((128, cfg.QB, cfg.QB, 128))
    for r in range(cfg.QB):
        for bb in range(cfg.QB):
            mm[:, r, bb, :] = NEG if bb < r else (cur if bb == r else 0.0)
    putb("mm", mm.reshape(128, -1))
    cb = cb.astype(ml_dtypes.bfloat16)

    pos = np.concatenate([np.arange(16.0), 16.0 + j * cfg.S + np.arange(cfg.S, dtype=np.float64)])
    T = cfg.T

    def tables(dim):
        half = dim // 2
        inv = ROPE_BASE ** (-np.arange(half, dtype=np.float64) / half)
        ang = (pos[None, :].astype(np.float32) * inv[:, None].astype(np.float32)).astype(np.float32)
        c = np.cos(ang.astype(np.float64))
        s = np.sin(ang.astype(np.float64))
        return np.concatenate([c, c], 0), np.concatenate([-s, s], 0)
    cr, sr = tables(128)
    cm, sm = tables(64)
    rope = np.zeros((128, 4, T), np.float32)
    rope[:, 0] = cr
    rope[:, 1] = sr
    rope[:64, 2] = cm
    rope[:64, 3] = sm
    return cf, cb, rope.reshape(128, 4 * T)


ENGS = ["pe", "act", "dve", "pool", "sp"]


class Buf:
    __slots__ = ("ap", "w", "r", "name", "psum")

    def __init__(self, ap, name="", psum=False):
        self.ap = ap
        self.w = None
        self.r = {}
        self.name = name
        self.psum = psum


class Prog:
    def __init__(self, nc, stack, semreg):
        self.nc = nc
        self.stack = stack
        self.streams = {e: [] for e in ENGS}
        self.sems = semreg
        self.cnt = {}
        self.seen = {e: {} for e in ENGS}
        self.dbg = []
        for e in ENGS:
            self.newsem("done_" + e)

    def newsem(self, key):
        if key not in self.sems:
            self.sems[key] = self.stack.enter_context(self.nc.semaphore(key))
        if key not in self.cnt:
            self.cnt[key] = 0
        return key

    def _waits(self, eng, reads, writes):
        ev = {}

        def add(k, v):
            if ev.get(k, 0) < v:
                ev[k] = v
        for b in reads:
            if b.w is not None:
                add(*b.w)
            if b.psum:
                for k, v in b.r.items():
                    if k != "done_" + eng:
                        add(k, v)
        for b in writes:
            if b.w is not None:
                add(*b.w)
            for k, v in b.r.items():
                add(k, v)
        out = []
        seen = self.seen[eng]
        for k, v in ev.items():
            if eng == "pe" and k == "done_pe":
                continue
            if seen.get(k, 0) >= v:
                continue
            seen[k] = v
            out.append((k, v))
        return out

    def _finish(self, evt, reads, writes):
        k, v = evt
        for b in reads:
            if b.r.get(k, 0) < v:
                b.r[k] = v
        for b in writes:
            b.w = evt
            b.r = {}

    def op(self, eng, fn, reads=(), writes=()):
        waits = self._waits(eng, reads, writes)
        key = "done_" + eng
        self.cnt[key] += 1
        evt = (key, self.cnt[key])
        self.streams[eng].append((waits, [fn], (key, 1)))
        self.dbg.append((eng, evt, waits, [b.name for b in reads], [b.name for b in writes]))
        self._finish(evt, reads, writes)
        return evt

    def pe(self, fns, reads=(), writes=()):
        waits = self._waits("pe", reads, writes)
        key = "done_pe"
        self.cnt[key] += 1
        evt = (key, self.cnt[key])
        self.streams["pe"].append((waits, list(fns), (key, 1)))
        self.dbg.append(("pe", evt, waits, [b.name for b in reads], [b.name for b in writes]))
        self._finish(evt, reads, writes)
        return evt

    def dma(self, eng, pairs, sem, reads=(), writes=(), **kw):
        self.newsem(sem)
        waits = self._waits(eng, reads, writes)
        fns = []
        for (o, i) in pairs:
            fns.append((lambda e, o=o, i=i: e.dma_start(out=o, in_=i, **kw)))
        self.cnt[sem] += 16 * len(pairs)
        evt = (sem, self.cnt[sem])
        self.streams[eng].append((waits, fns, (sem, 16, "each")))
        self.dbg.append((eng + ":dma", evt, waits, [b.name for b in reads], [b.name for b in writes]))
        self._finish(evt, reads, writes)
        return evt

    def collective(self, fn, sem, reads=(), writes=()):
        self.newsem(sem)
        waits = self._waits("pool", reads, writes)
        self.cnt[sem] += 1
        evt = (sem, self.cnt[sem])
        self.streams["pool"].append((waits, [fn], (sem, 1)))
        self._finish(evt, reads, writes)
        return evt

    def barrier(self):
        tot = [(k, v) for k, v in self.cnt.items() if v > 0 and not (k.startswith("w") or k.startswith("cc") or k == "done_pool")]
        self.dbg.append(("BARRIER", None, tot, [], []))
        for e in ENGS:
            if e == "pool":
                continue
            waits = []
            for k, v in tot:
                if k == "done_" + e:
                    continue
                if self.seen[e].get(k, 0) >= v:
                    continue
                self.seen[e][k] = v
                waits.append((k, v))
            if waits:
                self.streams[e].append((waits, [], None))

    def wait_on(self, eng, evts):
        waits = []
        for k, v in evts:
            if self.seen[eng].get(k, 0) < v:
                self.seen[eng][k] = v
                waits.append((k, v))
        if waits:
            self.streams[eng].append((waits, [], None))

    def emit(self):
        nc = self.nc
        sems = self.sems
        streams = self.streams

        def run(e, ops):
            for waits, fns, inc in ops:
                for k, v in waits:
                    e.wait_ge(sems[k], v)
                n = len(fns)
                for idx, fn in enumerate(fns):
                    ins = fn(e)
                    if inc is None:
                        continue
                    if len(inc) == 3 or idx == n - 1:
                        ins.then_inc(sems[inc[0]], inc[1])
        with nc.Block() as block:
            @block.tensor
            def _(e):
                run(e, streams["pe"])

            @block.scalar
            def _(e):
                run(e, streams["act"])

            @block.vector
            def _(e):
                run(e, streams["dve"])

            @block.gpsimd
            def _(e):
                run(e, streams["pool"])

            @block.sync
            def _(e):
                run(e, streams["sp"])


class Arena:
    def __init__(self, ap, words):
        self.ap = ap
        self.words = words
        self.top = 0
        self.peak = 0

    def mark(self):
        return self.top

    def release(self, m):
        self.top = m

    def _alloc(self, n):
        n = (n + 1) // 2 * 2
        o = self.top
        self.top += n
        self.peak = max(self.peak, self.top)
        assert self.top <= self.words, ("arena overflow", self.top, self.words)
        return o

    def f32(self, n, dt=F32):
        o = self._alloc(n)
        v = self.ap[:, o:o + n]
        return v if dt == F32 else v.bitcast(dt)

    def bf16(self, n):
        w = (n + 1) // 2
        o = self._alloc(w)
        return self.ap[:, o:o + w].bitcast(BF16)[:, 0:n]


NSLOT = 6
AHEAD = 4


class _Stop(Exception):
    pass


def build_program(cfg):
    nc = bass.Bass("TRN2", target_bir_lowering=False)
    stack = contextlib.ExitStack()
    semreg = {}
    C = cfg
    L, D, DC, T, S = C.DEPTH, C.D, C.DC, C.T, C.S
    lay = cf_layout(C)
    layb = cb_layout(C)

    def din(name, shape, dt=F32):
        return nc.dram_tensor(name, list(shape), dt, kind="ExternalInput").ap()
    xT = din("xT", [D, S])
    metaT = din("metaT", [D, N_META])
    cf_d = din("cf", [128, lay["_n"]])
    cb_d = din("cb", [128, layb["_n"]], BF16)
    rope_d = din("rope", [128, 4 * T])
    W = {}
    for nm, shp in weight_shapes(C):
        W[nm] = din(nm, shp)
    yT = nc.dram_tensor("yT", [D, S], F32, kind="ExternalOutput").ap()
    HW = DC * T
    hsp_d = nc.dram_tensor("hspill", [128, HW], F32).ap()
    X = []
    for l in range(L):
        X.append(dict(
            x1s=nc.dram_tensor(f"x1s{l}", [C.KVR, S], BF16),
            x1d=nc.dram_tensor(f"x1d{l}", [C.GS * C.KVR, S], BF16),
            x2s=nc.dram_tensor(f"x2s{l}", [C.X2A_ROWS, S], BF16),
            x2d=nc.dram_tensor(f"x2d{l}", [C.GS * C.X2A_ROWS, S], BF16),
            x3s=nc.dram_tensor(f"x3s{l}", [128, C.RET_H * 256], F32),
            x3d=nc.dram_tensor(f"x3d{l}", [C.GS * 128, C.RET_H * 256], F32),
        ))
    groups = [[b * C.GS + i for i in range(C.GS)] for b in range(C.NBATCH)]

    HNW = DC * ((T + 1) // 2)
    SLOTW = 16 * 64
    NSINK = lay["sinks"][1]
    KW = lay["_n"] + (layb["_n"] + 1) // 2 + NSINK + 4 + 5 * 512 + 32
    NMAXC = max(C.SWA_OUT, C.MLA_OUT, C.RET_OUT) // 128
    CWORDS = max(DC * ((T + 1) // 2) + NMAXC * ((T + 1) // 2), C.FG * ((T + 1) // 2)) + 64
    RW = 128 + 128 + 3 * 512
    TOTAL = 211000 // 4 - RW
    HWA = TOTAL - HNW - NSLOT * SLOTW - KW - CWORDS
    assert HWA >= HW, ("SBUF too small", HWA, HW)
    arena_t = stack.enter_context(nc.sbuf_tensor("arena", [128, TOTAL], F32))
    psum_t = stack.enter_context(nc.psum_tensor("ps", [128, 8, 512], F32))
    r_t = stack.enter_context(nc.sbuf_tensor("f32r", [128, RW], F32R))
    o = 0
    hn_ap = arena_t[:, o:o + HNW].bitcast(BF16).rearrange("p (c t) -> p c t", c=DC)[:, :, 0:T]
    o += HNW
    H_OFF = o
    h_ap = arena_t[:, o:o + HW].rearrange("p (c t) -> p c t", c=DC)
    h_flat = arena_t[:, o:o + HW]
    o += HWA
    slot_aps = []
    for i in range(NSLOT):
        slot_aps.append(arena_t[:, o:o + SLOTW].bitcast(BF16).rearrange("p (k n) -> p k n", k=16))
        o += SLOTW
    K_OFF = o
    o += KW
    C_OFF = o
    o += CWORDS
    assert o <= TOTAL

    def generate(script):
        P = Prog(nc, stack, semreg)
        AH = Arena(arena_t[:, H_OFF:H_OFF + HWA], HWA)
        AK = Arena(arena_t[:, K_OFF:K_OFF + KW], KW)
        AC = Arena(arena_t[:, C_OFF:C_OFF + CWORDS], CWORDS)
        AR = Arena(r_t[:, 0:RW], RW)
        PS = [Buf(psum_t[:, i, :], f"ps{i}", psum=True) for i in range(8)]
        cf_ap = AK.f32(lay["_n"])
        cb_ap = AK.bf16(layb["_n"])
        onesR = AR.f32(128)
        bdR = AR.f32(128)
        esink = AK.f32(NSINK)
        eps_t = AK.f32(2)
        CONST = Buf(None, "const")

        def cfc(name, i=0, n=1, rows=128):
            o_, _ = lay[name]
            return cf_ap[0:rows, o_ + i:o_ + i + n]

        def cbc(name, i=0, n=128, rows=128):
            o_, _ = layb[name]
            return cb_ap[0:rows, o_ + i:o_ + i + n]

        P.dma("sp", [(cf_ap, cf_d), (cb_ap, cb_d)], "ld_c", writes=[CONST])
        P.op("dve", lambda e: e.tensor_copy(out=onesR, in_=cfc("ones", 0, 128)), reads=[CONST], writes=[CONST])
        P.op("dve", lambda e: e.tensor_copy(out=bdR, in_=cfc("bd64", 0, 128)), reads=[CONST], writes=[CONST])
        P.op("dve", lambda e: e.memset(eps_t, EPS), writes=[CONST])
        P.op("act", lambda e: e.activation(out=esink, in_=cfc("sinks", 0, NSINK), func=AF.Exp), reads=[CONST], writes=[CONST])
        ident = cbc("ident")
        onesb = cbc("ones")

        TT = C.TT
        NT = len(TT)
        hB = [[Buf(None, f"h{c}_{t}") for t in range(NT)] for c in range(DC)]
        hnB = [[Buf(None, f"hn{c}_{t}") for t in range(NT)] for c in range(DC)]
        allh = [hB[c][t] for c in range(DC) for t in range(NT)]

        def hsl(c, ti):
            s, n = TT[ti]
            return h_ap[:, c, s:s + n]

        def hnsl(c, ti):
            s, n = TT[ti]
            return hn_ap[:, c, s:s + n]

        xv = xT.rearrange("(c p) t -> p c t", p=128)
        mv = metaT.rearrange("(c p) t -> p c t", p=128)
        step = max(1, DC // 4)
        pairs = [(h_ap[:, :, 0:N_META], mv)]
        for c0 in range(0, DC, step):
            pairs.append((h_ap[:, c0:c0 + step, N_META:T], xv[:, c0:c0 + step, :]))
        P.dma("sp", pairs, "ld_x", writes=allh)

        class PSM:
            def __init__(self):
                self.free = list(range(8))
                self.rr = 0

            def reserve(self, n):
                got = self.free[:n]
                self.free = self.free[n:]
                return [PS[i] for i in got]

            def unreserve(self, bufs):
                for b in bufs:
                    self.free.append(PS.index(b))
                self.free.sort()

            def next(self):
                self.rr = (self.rr + 1) % len(self.free)
                return PS[self.free[self.rr]]
        psm = PSM()

        slots = [Buf(slot_aps[i], f"slot{i}") for i in range(NSLOT)]

        class WQ:
            def __init__(self):
                self.plans = []
                self.slot_of = {}
                self.free = list(range(NSLOT))
                self.next_issue = 0

            def plan(self, loader):
                self.plans.append(loader)
                return len(self.plans) - 1

            def _src(self):
                return script if script is not None else self.plans

            def _issue(self):
                idx = self.next_issue
                assert self.free, "weight slots exhausted (too many held)"
                si = self.free.pop(0)
                sb = slots[si]
                P.dma("pool", self._src()[idx](sb.ap), f"w{si}", writes=[sb])
                self.slot_of[idx] = si
                self.next_issue += 1

            def get(self, idx):
                while self.next_issue <= idx:
                    self._issue()
                src = self._src()
                while self.free and self.next_issue < len(src) and self.next_issue <= idx + AHEAD:
                    self._issue()
                return slots[self.slot_of[idx]]

            def done(self, *idxs):
                for idx in idxs:
                    self.free.append(self.slot_of.pop(idx))
        wq = WQ()

        def wcols(wd, l, kc0, nkc, col0, ncol):
            src = wd[l].rearrange("(k p) n -> p k n", p=128)[:, kc0:kc0 + nkc, col0:col0 + ncol]
            return lambda s: [(s[:, 0:nkc, 0:ncol], src)]

        def mm(out, lhsT, rhs, start, stop):
            return lambda e: e.matmul(out, lhsT=lhsT, rhs=rhs, start=start, stop=stop)

        def i_act(out, in_, func, bias=None, scale=None):
            kw = {}
            if bias is not None:
                kw["bias"] = bias
            if scale is not None:
                kw["scale"] = scale
            return lambda e: e.activation(out=out, in_=in_, func=func, **kw)

        def i_stt(out, in0, scalar, in1, op0, op1):
            return lambda e: e.scalar_tensor_tensor(out=out, in0=in0, scalar=scalar, in1=in1, op0=op0, op1=op1)

        def i_tt(out, in0, in1, op):
            return lambda e: e.tensor_tensor(out=out, in0=in0, in1=in1, op=op)

        def i_ts(out, in0, s1, op0):
            return lambda e: e.tensor_scalar(out=out, in0=in0, scalar1=s1, scalar2=None, op0=op0)

        def i_cp(out, in_):
            return lambda e: e.tensor_copy(out=out, in_=in_)

        def i_recip(out, in_):
            return lambda e: e.reciprocal(out=out, in_=in_)

        def rot(n, w, dt=F32, name="r"):
            tiles = [(AR.f32(w) if dt == F32R else AK.f32(w, dt)) for _ in range(n)]
            bufs = [Buf(tiles[i], f"{name}{i}") for i in range(n)]
            st = {"i": 0}

            def nxt():
                st["i"] = (st["i"] + 1) % n
                return bufs[st["i"]]
            return nxt
        nextsq = rot(3, 512, F32R, "sq")
        nextrs = rot(2, 512, F32, "rs")
        nexttmp = rot(3, 512, F32, "tmp")

        def emit_rstd(stat_ps, rows, n, rs, dim):
            P.op("act", i_act(rs.ap[0:rows, 0:n], stat_ps.ap[0:rows, 0:n], AF.Ln, bias=eps_t[0:rows, 0:1], scale=1.0 / dim),
                 reads=[stat_ps, CONST], writes=[rs])
            P.op("act", i_act(rs.ap[0:rows, 0:n], rs.ap[0:rows, 0:n], AF.Exp, scale=-0.5), reads=[rs], writes=[rs])

        def rmsnorm_tile(gname, l, ti):
            s, n = TT[ti]
            st = psm.next()
            for c in range(DC):
                sq = nextsq()
                P.op("act", i_act(sq.ap[:, 0:n], hsl(c, ti), AF.Square), reads=[hB[c][ti]], writes=[sq])
                P.pe([mm(st.ap[:, 0:n], onesR, sq.ap[:, 0:n], c == 0, c == DC - 1)], reads=[sq, CONST], writes=[st])
            rs = nextrs()
            emit_rstd(st, 128, n, rs, D)
            for c in range(DC):
                P.op("dve", i_stt(hnsl(c, ti), hsl(c, ti), cfc(gname, l * DC + c), rs.ap[:, 0:n], ALU.mult, ALU.mult),
                     reads=[hB[c][ti], rs, CONST], writes=[hnB[c][ti]])

        def rmsnorm_h(gname, l):
            for ti in range(NT):
                rmsnorm_tile(gname, l, ti)

        def ffn(l, wg, wu, wd, gname):
            rmsnorm_tile(gname, l, 0)
            lazy = {"todo": True}
            m = AC.mark()
            FG = C.FG
            a_ap = AC.bf16(FG * T).rearrange("p (f t) -> p f t", f=FG)
            aB = [[Buf(None, f"a{f}_{t}") for t in range(NT)] for f in range(FG)]
            plan = []
            for g in range(C.FC // FG):
                for fl in range(FG):
                    fc = g * FG + fl
                    plan.append(("gu", fl, wq.plan(wcols(wg, l, 0, DC, fc * 128, 128)), wq.plan(wcols(wu, l, 0, DC, fc * 128, 128))))
                for dc in range(DC):
                    plan.append(("dn", dc, wq.plan(wcols(wd, l, g * FG, FG, dc * 128, 128)), None))
            for kind, x_, w1, w2 in plan:
                if kind == "gu":
                    fl = x_
                    sg_ = wq.get(w1)
                    su_ = wq.get(w2)
                    for ti, (s, n) in enumerate(TT):
                        if lazy["todo"] and ti > 0:
                            rmsnorm_tile(gname, l, ti)
                        pg = psm.next()
                        pu = psm.next()
                        rb = [hnB[kc][ti] for kc in range(DC)]
                        P.pe([mm(pg.ap[:, 0:n], sg_.ap[:, kc, :], hnsl(kc, ti), kc == 0, kc == DC - 1) for kc in range(DC)], reads=[sg_] + rb, writes=[pg])
                        P.pe([mm(pu.ap[:, 0:n], su_.ap[:, kc, :], hnsl(kc, ti), kc == 0, kc == DC - 1) for kc in range(DC)], reads=[su_] + rb, writes=[pu])
                        tb = nexttmp()
                        P.op("act", i_act(tb.ap[:, 0:n], pg.ap[:, 0:n], AF.Silu), reads=[pg], writes=[tb])
                        P.op("dve", i_tt(a_ap[:, fl, s:s + n], tb.ap[:, 0:n], pu.ap[:, 0:n], ALU.mult), reads=[tb, pu], writes=[aB[fl][ti]])
                    lazy["todo"] = False
                    wq.done(w1, w2)
                else:
                    dc = x_
                    sd_ = wq.get(w1)
                    for ti, (s, n) in enumerate(TT):
                        po = psm.next()
                        P.pe([mm(po.ap[:, 0:n], sd_.ap[:, fl, :], a_ap[:, fl, s:s + n], fl == 0, fl == FG - 1) for fl in range(FG)],
                             reads=[sd_] + [aB[fl][ti] for fl in range(FG)], writes=[po])
                        P.op("dve", i_stt(hsl(dc, ti), po.ap[:, 0:n], 0.5, hsl(dc, ti), ALU.mult, ALU.add), reads=[po, hB[dc][ti]], writes=[hB[dc][ti]])
                    wq.done(w1)
            P.barrier()
            AC.release(m)

        def proj_fm(ps, n, wslot, m0, M, srcs, src_bufs):
            nk = len(srcs)
            P.pe([mm(ps.ap[0:M, 0:n], wslot.ap[:, kc, m0:m0 + M], srcs[kc], kc == 0, kc == nk - 1) for kc in range(nk)],
                 reads=[wslot] + list(src_bufs), writes=[ps])

        def hn_srcs(ti):
            return [hnsl(kc, ti) for kc in range(DC)], [hnB[kc][ti] for kc in range(DC)]

        def headnorm(ps, rows, n, onesM, dim, gain_ap, out_ap, out_buf):
            sq = nextsq()
            P.op("act", i_act(sq.ap[0:rows, 0:n], ps.ap[0:rows, 0:n], AF.Square), reads=[ps], writes=[sq])
            st = psm.next()
            P.pe([mm(st.ap[:, 0:n], onesM[0:rows, :], sq.ap[0:rows, 0:n], True, True)], reads=[sq, CONST], writes=[st])
            rs = nextrs()
            emit_rstd(st, rows, n, rs, dim)
            P.op("dve", i_stt(out_ap, ps.ap[0:rows, 0:n], gain_ap, rs.ap[0:rows, 0:n], ALU.mult, ALU.mult), reads=[ps, rs, CONST], writes=[out_buf])

        def blk(b):
            return N_META + b * BLK

        def tok_range(bi):
            return (0, N_META) if bi == 0 else (blk(bi - 1), BLK)

        def tiles_of(t0, n):
            return [ti for ti, (s, m_) in enumerate(TT) if s < t0 + n and t0 < s + m_]

        def proj_tm(ps, t0, nt, wslot, ncol, col0=0):
            rb = [hnB[kc][ti] for kc in range(DC) for ti in tiles_of(t0, nt)]
            P.pe([mm(ps.ap[0:nt, 0:ncol], hn_ap[:, kc, t0:t0 + nt], wslot.ap[:, kc, col0:col0 + ncol], kc == 0, kc == DC - 1) for kc in range(DC)],
                 reads=[wslot] + rb, writes=[ps])
        Gch = [float(x) for x in np.exp(128.0 * np.log(_gammas(C)))]

        def ck(name):
            if getattr(C, "stop", None) == name:
                raise _Stop()

        def dump(name, ap3, nch, bufs):
            if getattr(C, "stop", None) != name:
                return
            nch = min(nch, DC)
            yv_ = yT.rearrange("(c p) t -> p c t", p=128)
            ev_ = P.dma("pool", [(yv_[:, c, :], ap3[:, c, N_META:T]) for c in range(nch)], "st_y", reads=list(bufs), writes=[])
            P.wait_on("sp", [ev_])
            raise _Stop("dumped")

        def mixer(l):
            Xl = X[l]
            win = W["w_in"]
            off = C.off
            srcw = win[l].rearrange("(k p) n -> p k n", p=128)
            q4 = HW // 4
            P.dma("sp", [(hsp_d[:, q * q4:(q + 1) * q4], h_flat[:, q * q4:(q + 1) * q4]) for q in range(4)], "hsp", reads=allh, writes=[])
            rmsnorm_h("g_mix", l)
            P.barrier()
            ck("spill")
            mH = AH.mark()
            mC = AC.mark()
            merged = AC.bf16(DC * T).rearrange("p (c t) -> p c t", c=DC)
            mgB = [[Buf(None, f"mg{c}_{t}") for t in range(NT)] for c in range(DC)]
            ROPE = Buf(None, "rope")

            def load_rope(tabs):
                t_ = AH.f32(len(tabs) * T).rearrange("p (a t) -> p a t", a=len(tabs))
                P.dma("sp", [(t_[:, i, :], rope_d[:, tb * T:(tb + 1) * T]) for i, tb in enumerate(tabs)], "ld_r", writes=[ROPE])
                return t_

            def proj_rmsnorm(col0, nch, gname, gidx0, out3, outB):
                widx = [wq.plan(wcols(win, l, 0, DC, col0 + oc * 128, 128)) for oc in range(nch)]
                m_ = AH.mark()
                raw = AH.f32(nch * 512).rearrange("p (c n) -> p c n", c=nch)
                rawB = [Buf(None, "raw") for _ in range(nch)]
                for ti, (s, n) in enumerate(TT):
                    srcs, sb = hn_srcs(ti)
                    st = psm.next()
                    for oc in range(nch):
                        ws = wq.get(widx[oc])
                        ps = psm.next()
                        proj_fm(ps, n, ws, 0, 128, srcs, sb)
                        sq = nextsq()
                        P.op("act", i_act(sq.ap[:, 0:n], ps.ap[:, 0:n], AF.Square), reads=[ps], writes=[sq])
                        P.op("dve", i_ts(raw[:, oc, 0:n], ps.ap[:, 0:n], 1.0, ALU.mult), reads=[ps], writes=[rawB[oc]])
                        P.pe([mm(st.ap[:, 0:n], onesR, sq.ap[:, 0:n], oc == 0, oc == nch - 1)], reads=[sq, CONST], writes=[st])
                    rs = nextrs()
                    emit_rstd(st, 128, n, rs, nch * 128)
                    for oc in range(nch):
                        P.op("dve", i_stt(out3[:, oc, s:s + n], raw[:, oc, 0:n], cfc(gname, gidx0 + oc), rs.ap[:, 0:n], ALU.mult, ALU.mult),
                             reads=[rawB[oc], rs, CONST], writes=[outB[oc][ti]])
                wq.done(*widx)
                P.barrier()
                AH.release(m_)

            def rope_fm(wsrc3, nkc, srcs_fn, col0, rows, cos3, sin3, out_fn, out_buf_fn, gain=None, gain_s=None, norm=False):
                half = rows // 2
                wi = wq.plan(lambda s_: [(s_[:, 0:nkc, 0:rows], wsrc3[:, :, col0:col0 + rows])])
                wsi = wq.plan(lambda s_: [(s_[:, 0:nkc, 0:half], wsrc3[:, :, col0 + half:col0 + rows]),
                                          (s_[:, 0:nkc, half:rows], wsrc3[:, :, col0:col0 + half])])
                for ti, (s, n) in enumerate(TT):
                    srcs, sb = srcs_fn(ti)
                    px_ = psm.next()
                    ppx = psm.next()
                    proj_fm(px_, n, wq.get(wi), 0, rows, srcs, sb)
                    proj_fm(ppx, n, wq.get(wsi), 0, rows, srcs, sb)
                    t1 = nexttmp()
                    t2 = nexttmp()
                    cos_ap = cos3[0:rows, s:s + n]
                    sin_ap = sin3[0:rows, s:s + n]
                    if norm:
                        sq = nextsq()
                        P.op("act", i_act(sq.ap[0:rows, 0:n], px_.ap[0:rows, 0:n], AF.Square), reads=[px_], writes=[sq])
                        st = psm.next()
                        P.pe([mm(st.ap[:, 0:n], onesR[0:rows, :], sq.ap[0:rows, 0:n], True, True)], reads=[sq, CONST], writes=[st])
                        rs = nextrs()
                        emit_rstd(st, rows, n, rs, rows)
                        P.op("dve", i_stt(t1.ap[0:rows, 0:n], px_.ap[0:rows, 0:n], gain, cos_ap, ALU.mult, ALU.mult), reads=[px_, ROPE, CONST], writes=[t1])
                        P.op("dve", i_stt(t2.ap[0:rows, 0:n], ppx.ap[0:rows, 0:n], gain_s, sin_ap, ALU.mult, ALU.mult), reads=[ppx, ROPE, CONST], writes=[t2])
                        P.op("dve", i_tt(t1.ap[0:rows, 0:n], t1.ap[0:rows, 0:n], t2.ap[0:rows, 0:n], ALU.add), reads=[t1, t2], writes=[t1])
                        P.op("dve", i_tt(out_fn(s, n), t1.ap[0:rows, 0:n], rs.ap[0:rows, 0:n], ALU.mult), reads=[t1, rs], writes=[out_buf_fn(ti)])
                    else:
                        P.op("dve", i_tt(t1.ap[0:rows, 0:n], px_.ap[0:rows, 0:n], cos_ap, ALU.mult), reads=[px_, ROPE], writes=[t1])
                        P.op("dve", i_tt(t2.ap[0:rows, 0:n], ppx.ap[0:rows, 0:n], sin_ap, ALU.mult), reads=[ppx, ROPE], writes=[t2])
                        P.op("dve", i_tt(out_fn(s, n), t1.ap[0:rows, 0:n], t2.ap[0:rows, 0:n], ALU.add), reads=[t1, t2], writes=[out_buf_fn(ti)])
                wq.done(wi, wsi)

            RH = C.RET_H
            NKV = C.SWA_KVH
            VW = NKV * 64
            ckvn = AH.bf16(C.KVC * T).rearrange("p (c t) -> p c t", c=C.KVC)
            ckvnB = [[Buf(None, "ckvn") for _ in range(NT)] for _ in range(C.KVC)]
            NKALL = T + C.GS * S
            kr_all = AH.bf16(NKALL)
            krB = [Buf(None, "kr") for _ in range(NT)]
            Mst = AH.f32(RH * 256).rearrange("p (h e) -> p h e", h=RH)
            MB = [Buf(None, "M") for _ in range(RH)]
            mSWA = AH.mark()
            KK = AH.bf16(NKV * T).rearrange("p (g t) -> p g t", g=NKV)
            KKB = [[Buf(None, "KK") for _ in range(NT)] for _ in range(NKV)]
            Vs = AH.bf16((C.NB + 1) * VW).rearrange("p (b n) -> p b n", b=C.NB + 1)
            VsB = [Buf(None, "Vs") for _ in range(C.NB + 1)]
            KKp_all = AH.bf16(C.GS * NKV * 128).rearrange("p (i g n) -> p i g n", i=C.GS, g=NKV)
            Vp_all = AH.bf16(C.GS * VW).rearrange("p (i n) -> p i n", i=C.GS)
            PRV = Buf(None, "prev")
            mM1 = AH.mark()
            Lloc = AH.f32(RH * 256).rearrange("p (h e) -> p h e", h=RH)
            LB = [Buf(None, "L") for _ in range(RH)]
            rope_t = load_rope([0, 1, 2, 3])

            dump("hn", hn_ap, DC, [b_ for r_ in hnB for b_ in r_])
            ck("rope")
            proj_rmsnorm(off["ckv"], C.KVC, "g_kva", l * C.KVC, ckvn, ckvnB)
            ck("m1a")
            dump("ckvn", ckvn, C.KVC, [b_ for r_ in ckvnB for b_ in r_])
            rope_fm(srcw, DC, hn_srcs, off["kr"], 64, rope_t[:, 2, :], rope_t[:, 3, :], lambda s, n: kr_all[0:64, s:s + n], lambda ti: krB[ti],
                    gain=cfc("g_kr", l, rows=64), gain_s=cfc("g_krs", l, rows=64), norm=True)
            for g in range(NKV):
                c0 = off["swa_k"] + g * 64
                wi = wq.plan(lambda s_, c0=c0: [(s_[:, 0:DC, 0:64], srcw[:, :, c0:c0 + 64]), (s_[:, 0:DC, 64:128], srcw[:, :, c0:c0 + 64])])
                for ti, (s, n) in enumerate(TT):
                    srcs, sb = hn_srcs(ti)
                    ps = psm.next()
                    proj_fm(ps, n, wq.get(wi), 0, 128, srcs, sb)
                    headnorm(ps, 128, n, bdR, 64, cfc("g_sk", l), KK[:, g, s:s + n], KKB[g][ti])
                wq.done(wi)
            wv = wq.plan(wcols(win, l, 0, DC, off["swa_v"], VW))
            for bi in range(C.NB + 1):
                t0, nt = tok_range(bi)
                ps = psm.next()
                proj_tm(ps, t0, nt, wq.get(wv), VW)
                P.op("act", i_act(Vs[0:nt, bi, :], ps.ap[0:nt, 0:VW], AF.Copy), reads=[ps], writes=[VsB[bi]])
            wq.done(wv)

            ck("m1c")
            def ret_alloc(want_q):
                KT = AH.bf16(T)
                KTB = [Buf(None, "KT") for _ in range(NT)]
                QT = AH.bf16(T) if want_q else None
                QTB = [Buf(None, "QT") for _ in range(NT)] if want_q else None
                Kz = AH.bf16((C.NB + 1) * 128).rearrange("p (b n) -> p b n", b=C.NB + 1)
                KzB = [Buf(None, "Kz") for _ in range(C.NB + 1)]
                Vr = AH.bf16((C.NB + 1) * 256).rearrange("p (b n) -> p b n", b=C.NB + 1)
                VrB = [Buf(None, "Vr") for _ in range(C.NB + 1)]
                return (KT, KTB, QT, QTB, Kz, KzB, Vr, VrB)

            def ret_kv_head(h, want_q, ropeT, weighted=False, pre=None):
                (KT, KTB, QT, QTB, Kz, KzB, Vr, VrB) = pre
                rope_fm(srcw, DC, hn_srcs, off["ret_k"] + h * 128, 128, ropeT[:, 0, :], ropeT[:, 1, :], lambda s, n: KT[:, s:s + n], lambda ti: KTB[ti])
                if want_q:
                    rope_fm(srcw, DC, hn_srcs, off["ret_q"] + h * 128, 128, ropeT[:, 0, :], ropeT[:, 1, :], lambda s, n: QT[:, s:s + n], lambda ti: QTB[ti])
                wvv = wq.plan(wcols(win, l, 0, DC, off["ret_v"] + h * 256, 128))
                wvv2 = wq.plan(wcols(win, l, 0, DC, off["ret_v"] + h * 256 + 128, 128))
                for bi in range(C.NB + 1):
                    t0, nt = tok_range(bi)
                    pt = psm.next()
                    ptb = pt.ap.bitcast(BF16)
                    P.pe([lambda e, ptb=ptb, t0=t0, nt=nt, KT=KT: e.transpose(out=ptb[0:nt, 0:128], in_=KT[:, t0:t0 + nt], identity=ident)],
                         reads=[KTB[ti] for ti in tiles_of(t0, nt)] + [CONST], writes=[pt])
                    zc = cfc("zeta_m", h, rows=nt) if bi == 0 else (cfc("zeta_w", h * C.NB + bi - 1) if weighted else cfc("zeta", h))
                    P.op("dve", i_ts(Kz[0:nt, bi, :], ptb[0:nt, 0:128], zc, ALU.mult), reads=[pt, CONST], writes=[KzB[bi]])
                    for hf, wslot_i in enumerate((wvv, wvv2)):
                        pv = psm.next()
                        proj_tm(pv, t0, nt, wq.get(wslot_i), 128)
                        P.op("act", i_act(Vr[0:nt, bi, hf * 128:(hf + 1) * 128], pv.ap[0:nt, 0:128], AF.Copy), reads=[pv], writes=[VrB[bi]])
                wq.done(wvv, wvv2)
                return KT, KTB, QT, QTB, Kz, KzB, Vr, VrB

            def kv_update(state_ap, state_buf, Kz, KzB, Vr, VrB, bi, gamma_chunk, first):
                t0, nt = tok_range(bi)
                pk = psm.next()
                P.pe([mm(pk.ap[:, 0:256], Kz[0:nt, bi, :], Vr[0:nt, bi, :], True, True)], reads=[KzB[bi], VrB[bi]], writes=[pk])
                if first:
                    P.op("dve", i_cp(state_ap, pk.ap[:, 0:256]), reads=[pk], writes=[state_buf])
                else:
                    P.op("dve", i_stt(state_ap, state_ap, float(gamma_chunk), pk.ap[:, 0:256], ALU.mult, ALU.add), reads=[pk, state_buf], writes=[state_buf])
            m_p1 = AH.mark()
            pre1 = ret_alloc(False)
            for h in range(RH):
                KT, KTB, _, _, Kz, KzB, Vr, VrB = ret_kv_head(h, False, rope_t, weighted=True, pre=pre1)
                kv_update(Mst[:, h, :], MB[h], Kz, KzB, Vr, VrB, 0, 0.0, True)
                pk = psm.next()
                P.pe([mm(pk.ap[:, 0:256], Kz[:, b + 1, :], Vr[:, b + 1, :], b == 0, b == C.NB - 1) for b in range(C.NB)],
                     reads=[KzB[b + 1] for b in range(C.NB)] + [VrB[b + 1] for b in range(C.NB)], writes=[pk])
                P.op("dve", i_cp(Lloc[:, h, :], pk.ap[:, 0:256]), reads=[pk], writes=[LB[h]])
            P.barrier()
            AH.release(m_p1)

            DX1s, DX1d, DX2s, DX2d, DX3s, DX3d = (Buf(None, "dx") for _ in range(6))
            lastb = C.NB
            P.dma("sp", [(Xl["x1s"].ap().rearrange("(c p) t -> p c t", p=128), ckvn[:, :, N_META:T])], "st_x1",
                  reads=[b_ for r_ in ckvnB for b_ in r_], writes=[DX1s])
            x2s = Xl["x2s"].ap()
            x2d = Xl["x2d"].ap()
            kkrows = 128 * 128 // S

            def x2rows(ap2, base, k):
                return ap2[base + k * kkrows:base + (k + 1) * kkrows, :].rearrange("r (q n) -> (r q) n", n=128)
            prs = [(x2s[0:64, :], kr_all[0:64, N_META:T])]
            for g in range(NKV):
                prs.append((x2rows(x2s, 64, g), KK[:, g, blk(C.NB - 1):blk(C.NB - 1) + BLK]))
            prs.append((x2rows(x2s, 64, NKV), Vs[:, lastb, :]))
            P.dma("sp", prs, "st_x2", reads=krB + [b_ for r_ in KKB for b_ in r_] + [VsB[lastb]], writes=[DX2s])
            P.dma("sp", [(Xl["x3s"].ap(), Lloc.rearrange("p h e -> p (h e)"))], "st_x3", reads=LB, writes=[DX3s])
            for ci, (s_, d_, bs, bd) in enumerate(((Xl["x1s"], Xl["x1d"], DX1s, DX1d), (Xl["x2s"], Xl["x2d"], DX2s, DX2d), (Xl["x3s"], Xl["x3d"], DX3s, DX3d))):
                P.collective(lambda e, s_=s_, d_=d_: e.collective_compute("AllGather", ALU.bypass, replica_groups=groups,
                                                                          ins=[s_.ap().opt()], outs=[d_.ap().opt()]),
                             f"cc{ci}", reads=[bs], writes=[bd])
            P.barrier()
            AH.release(mM1)
            KRG = Buf(None, "krg")
            P.dma("sp", [(kr_all[0:64, T + i * S:T + (i + 1) * S], x2d[i * C.X2A_ROWS:i * C.X2A_ROWS + 64, :]) for i in range(C.GS)],
                  "ld_g", reads=[DX2d], writes=[KRG])
            pr = []
            for i in range(C.GS):
                base = i * C.X2A_ROWS + 64
                for g in range(NKV):
                    pr.append((KKp_all[:, i, g, :], x2rows(x2d, base, g)))
                pr.append((Vp_all[:, i, :], x2rows(x2d, base, NKV)))
            P.dma("sp", pr, "ld_g2", reads=[DX2d], writes=[PRV])

            ck("xchg")

            def branch_merge(r, o3, oB, nkc, wbr):
                for dc in range(DC):
                    wg_i = wq.plan(wcols(win, l, 0, DC, off["gate"] + r * D + dc * 128, 128))
                    wb_i = wq.plan(wcols(wbr, l, 0, nkc, dc * 128, 128))
                    for ti, (s, n) in enumerate(TT):
                        srcs, sb = hn_srcs(ti)
                        pg = psm.next()
                        proj_fm(pg, n, wq.get(wg_i), 0, 128, srcs, sb)
                        py = psm.next()
                        proj_fm(py, n, wq.get(wb_i), 0, 128, [o3[:, kc, s:s + n] for kc in range(nkc)], [oB[kc][ti] for kc in range(nkc)])
                        tb = nexttmp()
                        P.op("act", i_act(tb.ap[:, 0:n], pg.ap[:, 0:n], AF.Sigmoid), reads=[pg], writes=[tb])
                        if r == 0:
                            P.op("dve", i_tt(merged[:, dc, s:s + n], tb.ap[:, 0:n], py.ap[:, 0:n], ALU.mult), reads=[tb, py], writes=[mgB[dc][ti]])
                        else:
                            P.op("dve", i_tt(tb.ap[:, 0:n], tb.ap[:, 0:n], py.ap[:, 0:n], ALU.mult), reads=[tb, py], writes=[tb])
                            P.op("dve", i_tt(merged[:, dc, s:s + n], tb.ap[:, 0:n], merged[:, dc, s:s + n], ALU.add),
                                 reads=[tb, mgB[dc][ti]], writes=[mgB[dc][ti]])
                    wq.done(wg_i, wb_i)

            mSC = AC.mark()
            NQC = C.SWA_H // 2
            o_a = AC.bf16(NQC * T).rearrange("p (c t) -> p c t", c=NQC)
            oaB = [[Buf(None, "oa") for _ in range(NT)] for _ in range(NQC)]
            Qs = AH.bf16(NQC * T).rearrange("p (c t) -> p c t", c=NQC)
            QsB = [[Buf(None, "Qs") for _ in range(NT)] for _ in range(NQC)]
            for c in range(NQC):
                wi = wq.plan(wcols(win, l, 0, DC, off["swa_q"] + c * 128, 128))
                for ti, (s, n) in enumerate(TT):
                    srcs, sb = hn_srcs(ti)
                    ps = psm.next()
                    proj_fm(ps, n, wq.get(wi), 0, 128, srcs, sb)
                    headnorm(ps, 128, n, bdR, 64, cfc("g_sq", l), Qs[:, c, s:s + n], QsB[c][ti])
                wq.done(wi)
            ptile = [AH.bf16(512) for _ in range(3)]
            ptB = [Buf(ptile[i], f"pt{i}") for i in range(3)]
            pti = {"i": 0}

            def nextpt():
                pti["i"] = (pti["i"] + 1) % 3
                return ptB[pti["i"]]
            fin_t = AH.f32(128)
            finB = Buf(fin_t, "fin")
            HPG = C.SWA_H // NKV
            quads = []
            for g in range(NKV):
                hs = list(range(g * HPG, (g + 1) * HPG))
                for q0 in range(0, len(hs), 4):
                    quads.append((g, hs[q0:q0 + 4]))
            po, pl = psm.reserve(2)

            spend = {"p": None}

            def swa_flush():
                if spend["p"] is None:
                    return
                (hp, par, vv, nk, nq, rb, pt, first, last) = spend["p"]
                spend["p"] = None
                fns = []
                for ii, (hq, hh) in enumerate(hp):
                    cs = (hq // 2) * 128
                    fns.append(mm(po.ap[par * 64:(par + 1) * 64, cs:cs + nq], vv, pt.ap[0:nk, ii * 128:ii * 128 + nq], first and ii == 0, last))
                P.pe(fns, reads=rb + [pt], writes=[po])
                fns = []
                for ii, (hq, hh) in enumerate(hp):
                    cs = (hq // 2) * 128
                    fns.append(mm(pl.ap[par * 64:(par + 1) * 64, cs:cs + nq], onesb[0:nk, 0:64], pt.ap[0:nk, ii * 128:ii * 128 + nq], first and ii == 0, last))
                P.pe(fns, reads=[pt, CONST], writes=[pl])

            def swa_block(bi):
                t0, nq = tok_range(bi)
                qtiles = tiles_of(t0, nq)
                for (g, hs) in quads:
                    mrd = [KKB[g][ti] for ti in tiles_of(0, N_META)] + [VsB[0]]
                    kts = []
                    if bi == 0:
                        kts.append((KK[:, g, 0:N_META], Vs[0:N_META, 0, g * 64:(g + 1) * 64], N_META, "cmcur", cfc("zero", 0, rows=N_META), mrd))
                    else:
                        kts.append((KK[:, g, 0:N_META], Vs[0:N_META, 0, g * 64:(g + 1) * 64], N_META, None, cfc("zero", 0, rows=N_META), mrd))
                        kts.append((KK[:, g, t0:t0 + BLK], Vs[:, bi, g * 64:(g + 1) * 64], BLK, "cmcur", cfc("zero"),
                                    [KKB[g][ti] for ti in qtiles] + [VsB[bi]]))
                        if bi >= 2:
                            p0 = blk(bi - 2)
                            kts.append((KK[:, g, p0:p0 + BLK], Vs[:, bi - 1, g * 64:(g + 1) * 64], BLK, "cmprev", cfc("zero"),
                                        [KKB[g][ti] for ti in tiles_of(p0, BLK)] + [VsB[bi - 1]]))
                        else:
                            for i in range(C.GS):
                                kts.append((KKp_all[:, i, g, :], Vp_all[:, i, g * 64:(g + 1) * 64], BLK, "cmprev", cfc("selp", i), [PRV]))
                    nh = len(hs)
                    qrd = [QsB[hh // 2][ti] for hh in hs for ti in qtiles]
                    for ki, (kT, vv, nk, mname, bias, rb) in enumerate(kts):
                        for par in range(2):
                            hp = [(hq, hh) for hq, hh in enumerate(hs) if hh % 2 == par]
                            if not hp:
                                continue
                            ncolp = len(hp) * 128
                            pss = psm.next()
                            fns = []
                            if mname is not None:
                                fns.append(mm(pss.ap[0:nk, 0:ncolp], ident[0:nk, 0:nk], cbc(mname, 0, ncolp, rows=nk), True, False))
                            for ii, (hq, hh) in enumerate(hp):
                                c = hh // 2
                                fns.append(mm(pss.ap[0:nk, ii * 128:ii * 128 + nq], kT[par * 64:(par + 1) * 64, :],
                                              Qs[par * 64:(par + 1) * 64, c, t0:t0 + nq], mname is None and ii == 0, ii == len(hp) - 1))
                            P.pe(fns, reads=rb + qrd + [CONST], writes=[pss])
                            pt = nextpt()
                            P.op("act", i_act(pt.ap[0:nk, 0:ncolp], pss.ap[0:nk, 0:ncolp], AF.Exp, bias=bias, scale=0.125), reads=[pss, CONST], writes=[pt])
                            swa_flush()
                            spend["p"] = (hp, par, vv, nk, nq, rb, pt, ki == 0, ki == len(kts) - 1)
                    swa_flush()
                    for cq in range((nh + 1) // 2):
                        c = hs[2 * cq] // 2
                        cs = cq * 128
                        P.op("dve", i_ts(fin_t[:, 0:nq], pl.ap[:, cs:cs + nq], esink[:, l * NQC + c:l * NQC + c + 1], ALU.add), reads=[pl, CONST], writes=[finB])
                        P.op("dve", i_recip(fin_t[:, 0:nq], fin_t[:, 0:nq]), reads=[finB], writes=[finB])
                        P.op("dve", i_tt(o_a[:, c, t0:t0 + nq], po.ap[:, cs:cs + nq], fin_t[:, 0:nq], ALU.mult), reads=[po, finB],
                             writes=[oaB[c][ti] for ti in qtiles])
            for bi in list(range(2, C.NB + 1)) + [0, 1]:
                swa_block(bi)
            psm.unreserve([po, pl])
            dump("o_a", o_a, NQC, [b_ for r_ in oaB for b_ in r_])
            AH.release(mSWA)
            branch_merge(0, o_a, oaB, NQC, W["w_br_swa"])
            P.barrier()
            dump("m0", merged, DC, [b_ for r_ in mgB for b_ in r_])
            AC.release(mSC)

            ck("swa")
            mR = AH.mark()
            mRC = AC.mark()
            NRC = C.RET_OUT // 128
            o_c = AC.bf16(NRC * T).rearrange("p (c t) -> p c t", c=NRC)
            ocB = [[Buf(None, "oc") for _ in range(NT)] for _ in range(NRC)]
            rope2 = load_rope([0, 1])
            st_t = AH.f32(256)
            stB = Buf(st_t, "state")
            sb16 = AH.bf16(256)
            sb16B = Buf(sb16, "state16")
            qx_t = [AH.bf16(128) for _ in range(2)]
            qxB = [Buf(qx_t[i], "qx") for i in range(2)]
            pT_t = [AH.bf16(128) for _ in range(2)]
            pTB = [Buf(pT_t[i], "pT") for i in range(2)]
            mXR = AR.mark()
            Lg = AH.f32(C.GS * 256).rearrange("p (i e) -> p i e", i=C.GS)
            LGB = Buf(None, "Lg")
            x3d = Xl["x3d"].ap()
            pre2 = ret_alloc(True)
            sg = AH.bf16(2 * T).rearrange("p (a t) -> p a t", a=2)
            sgB = [[Buf(None, "sg") for _ in range(NT)] for _ in range(2)]
            xh = AH.bf16(2 * T).rearrange("p (a t) -> p a t", a=2)
            sqh = AH.bf16(2 * T).rearrange("p (a t) -> p a t", a=2)
            xhB = [Buf(None, "xh") for _ in range(NT)]
            sqhB = [Buf(None, "sqh") for _ in range(NT)]
            for h in range(RH):
                P.dma("sp", [(Lg[:, i, :], x3d[i * 128:(i + 1) * 128, h * 256:(h + 1) * 256]) for i in range(C.GS)], "ld_g3", reads=[DX3d], writes=[LGB])
                KT, KTB, QT, QTB, Kz, KzB, Vr, VrB = ret_kv_head(h, True, rope2, pre=pre2)
                for a_ in range(2):
                    wi = wq.plan(wcols(win, l, 0, DC, off["ret_g"] + h * 256 + a_ * 128, 128))
                    for ti, (s, n) in enumerate(TT):
                        srcs, sb = hn_srcs(ti)
                        ps = psm.next()
                        proj_fm(ps, n, wq.get(wi), 0, 128, srcs, sb)
                        P.op("act", i_act(sg[:, a_, s:s + n], ps.ap[:, 0:n], AF.Silu), reads=[ps], writes=[sgB[a_][ti]])
                    wq.done(wi)
                rc0 = h * (1 + C.GS)
                P.op("dve", i_ts(st_t, Mst[:, h, :], cfc("retc", rc0), ALU.mult), reads=[MB[h], CONST], writes=[stB])
                for i in range(C.GS):
                    P.op("dve", i_stt(st_t, Lg[:, i, :], cfc("retc", rc0 + 1 + i), st_t, ALU.mult, ALU.add), reads=[LGB, stB, CONST], writes=[stB])
                def prep(bi):
                    t0, nq = tok_range(bi)
                    qtiles = tiles_of(t0, nq)
                    it = bi % 2
                    if bi == 0:
                        xi_ap = cfc("xi_m", h * 16, 16)
                        dp_ap = cfc("dp_m", h * 16, 16, rows=16)
                    else:
                        xi_ap = cfc("xi", h * 128, 128)
                        dp_ap = cfc("dp", h * 128, 128)
                    P.op("dve", i_tt(qx_t[it][:, 0:nq], QT[:, t0:t0 + nq], xi_ap, ALU.mult), reads=[QTB[ti] for ti in qtiles] + [CONST], writes=[qxB[it]])
                    pss = psm.next()
                    P.pe([mm(pss.ap[0:nq, 0:nq], KT[:, t0:t0 + nq], qx_t[it][:, 0:nq], True, True)], reads=[KTB[ti] for ti in qtiles] + [qxB[it]], writes=[pss])
                    P.op("dve", i_tt(pT_t[it][0:nq, 0:nq], pss.ap[0:nq, 0:nq], dp_ap, ALU.mult), reads=[pss, CONST], writes=[pTB[it]])

                def main(bi):
                    t0, nq = tok_range(bi)
                    qtiles = tiles_of(t0, nq)
                    it = bi % 2
                    if bi > 0:
                        P.op("act", i_act(sb16, st_t, AF.Copy), reads=[stB], writes=[sb16B])
                    pos_ = [psm.next(), psm.next()]
                    for e2 in range(2):
                        fns = [mm(pos_[e2].ap[:, 0:nq], Vr[0:nq, bi, e2 * 128:(e2 + 1) * 128], pT_t[it][0:nq, 0:nq], True, bi == 0)]
                        rd = [VrB[bi], pTB[it]]
                        if bi > 0:
                            fns.append(mm(pos_[e2].ap[:, 0:nq], sb16[:, e2 * 128:(e2 + 1) * 128], qx_t[it][:, 0:nq], False, True))
                            rd += [sb16B, qxB[it]]
                        P.pe(fns, reads=rd, writes=[pos_[e2]])
                    for e2 in range(2):
                        P.op("act", i_act(xh[:, e2, t0:t0 + nq], pos_[e2].ap[:, 0:nq], AF.Copy), reads=[pos_[e2]], writes=[xhB[ti] for ti in qtiles])
                        P.op("act", i_act(sqh[:, e2, t0:t0 + nq], pos_[e2].ap[:, 0:nq], AF.Square), reads=[pos_[e2]], writes=[sqhB[ti] for ti in qtiles])
                    if 0 < bi < C.NB:
                        kv_update(st_t, stB, Kz, KzB, Vr, VrB, bi, Gch[h], False)
                prep(0)
                for bi in range(0, C.NB + 1):
                    if bi + 1 <= C.NB:
                        prep(bi + 1)
                    main(bi)
                for ti, (s, n) in enumerate(TT):
                    s1 = psm.next()
                    s2 = psm.next()
                    P.pe([mm(s1.ap[:, 0:n], onesb, xh[:, e2, s:s + n], e2 == 0, e2 == 1) for e2 in range(2)], reads=[xhB[ti], CONST], writes=[s1])
                    P.pe([mm(s2.ap[:, 0:n], onesb, sqh[:, e2, s:s + n], e2 == 0, e2 == 1) for e2 in range(2)], reads=[sqhB[ti], CONST], writes=[s2])
                    mean = nexttmp()
                    var = nexttmp()
                    P.op("dve", i_ts(mean.ap[:, 0:n], s1.ap[:, 0:n], 1.0 / 256, ALU.mult), reads=[s1], writes=[mean])
                    P.op("dve", i_tt(var.ap[:, 0:n], mean.ap[:, 0:n], mean.ap[:, 0:n], ALU.mult), reads=[mean], writes=[var])
                    P.op("dve", i_stt(var.ap[:, 0:n], s2.ap[:, 0:n], 1.0 / 256, var.ap[:, 0:n], ALU.mult, ALU.subtract), reads=[s2, var], writes=[var])
                    P.op("act", i_act(var.ap[:, 0:n], var.ap[:, 0:n], AF.Ln, bias=eps_t[:, 0:1], scale=1.0), reads=[var, CONST], writes=[var])
                    P.op("act", i_act(var.ap[:, 0:n], var.ap[:, 0:n], AF.Exp, scale=-0.5), reads=[var], writes=[var])
                    for e2 in range(2):
                        tmpa = nexttmp()
                        P.op("dve", i_tt(tmpa.ap[:, 0:n], xh[:, e2, s:s + n], mean.ap[:, 0:n], ALU.subtract), reads=[xhB[ti], mean], writes=[tmpa])
                        P.op("dve", i_tt(tmpa.ap[:, 0:n], tmpa.ap[:, 0:n], var.ap[:, 0:n], ALU.mult), reads=[tmpa, var], writes=[tmpa])
                        P.op("dve", i_stt(o_c[:, 2 * h + e2, s:s + n], tmpa.ap[:, 0:n], cfc("g_gn", l * NRC + 2 * h + e2), sg[:, e2, s:s + n], ALU.mult, ALU.mult),
                             reads=[tmpa, CONST, sgB[e2][ti]], writes=[ocB[2 * h + e2][ti]])
            AH.release(mR)
            AR.release(mXR)
            dump("o_c", o_c, NRC, [b_ for r_ in ocB for b_ in r_])
            branch_merge(2, o_c, ocB, NRC, W["w_br_ret"])
            P.barrier()
            dump("m02", merged, DC, [b_ for r_ in mgB for b_ in r_])
            AC.release(mRC)

            ck("ret")
            mM = AH.mark()
            mMC = AC.mark()
            NMC = C.MLA_OUT // 128
            MH = C.MLA_H
            o_b = AC.bf16(NMC * T).rearrange("p (c t) -> p c t", c=NMC)
            obB = [[Buf(None, "ob") for _ in range(NT)] for _ in range(NMC)]
            cqn = AH.bf16(C.QRC * T).rearrange("p (c t) -> p c t", c=C.QRC)
            cqnB = [[Buf(None, "cqn") for _ in range(NT)] for _ in range(C.QRC)]
            proj_rmsnorm(off["cq"], C.QRC, "g_qa", l * C.QRC, cqn, cqnB)
            rope3 = load_rope([2, 3])
            wuq = W["mla_w_uq"]
            wukv = W["mla_w_ukv"]
            wuq3 = wuq[l].rearrange("(k p) n -> p k n", p=128)
            qn_t = [AH.bf16(T) for _ in range(2)]
            qr_t = [AH.bf16(T) for _ in range(2)]
            qnB = [[Buf(None, "qn") for _ in range(NT)] for _ in range(2)]
            qrB = [[Buf(None, "qr") for _ in range(NT)] for _ in range(2)]
            cg_t = AH.bf16(C.KVC * S).rearrange("p (c t) -> p c t", c=C.KVC)
            cgB = Buf(None, "cg")
            Kh_t = [AH.bf16(S) for _ in range(2)]
            KhB = [Buf(None, "Kh") for _ in range(2)]
            Vh_t = [AH.bf16(S).rearrange("p (b n) -> p b n", n=128) for _ in range(2)]
            VhB = [Buf(None, "Vh") for _ in range(2)]
            Pm_t = [AH.bf16(512) for _ in range(2)]
            PmB = [Buf(Pm_t[i], "Pm") for i in range(2)]
            pmi = {"i": 0}
            wkv_t = [AH.bf16(2 * C.KVC * 128).rearrange("p (a k n) -> p a k n", a=2, k=C.KVC) for _ in range(2)]
            wkvB = [Buf(None, "wkv") for _ in range(2)]
            Khm = AH.bf16(N_META)
            Vhm = AH.bf16(128)
            KhmB = Buf(None, "Khm")
            VhmB = Buf(None, "Vhm")

            def nextpm():
                pmi["i"] = (pmi["i"] + 1) % 2
                return PmB[pmi["i"]]
            acc = psm.reserve(2 * C.NQT)
            scale = (C.NOPE + C.ROPE) ** -0.5
            x1d = Xl["x1d"].ap()
            srcs_list = ["own"] + list(range(C.GS - 1))
            items = [(h, sidx) for h in range(MH) for sidx in range(len(srcs_list))]
            QW = C.QB * BLK
            wukv3 = wukv[l].rearrange("(k p) n -> p k n", p=128)

            def mla_q(h):
                ib = h % 2
                P.dma("pool", [(wkv_t[ib][:, 0, :, :], wukv3[:, :, h * 256:h * 256 + 128]), (wkv_t[ib][:, 1, :, :], wukv3[:, :, h * 256 + 128:h * 256 + 256])],
                      f"wkv{ib}", reads=[cqnB[0][0]], writes=[wkvB[ib]])
                wn = wq.plan(wcols(wuq, l, 0, C.QRC, h * 192, 128))

                def cq_srcs(ti):
                    s, n = TT[ti]
                    return [cqn[:, kc, s:s + n] for kc in range(C.QRC)], [cqnB[kc][ti] for kc in range(C.QRC)]
                for ti, (s, n) in enumerate(TT):
                    srcs, sb = cq_srcs(ti)
                    ps = psm.next()
                    proj_fm(ps, n, wq.get(wn), 0, 128, srcs, sb)
                    headnorm(ps, 128, n, onesR, 128, cfc("g_qn", l), qn_t[ib][:, s:s + n], qnB[ib][ti])
                wq.done(wn)
                rope_fm(wuq3, C.QRC, cq_srcs, h * 192 + 128, 64, rope3[:, 0, :], rope3[:, 1, :], lambda s, n: qr_t[ib][0:64, s:s + n], lambda ti: qrB[ib][ti],
                        gain=cfc("g_qr", l, rows=64), gain_s=cfc("g_qrs", l, rows=64), norm=True)

            def mla_expand(it_idx):
                h, sidx = items[it_idx]
                src = srcs_list[sidx]
                ib = it_idx % 2
                wk = wkv_t[h % 2][:, 0]
                wv_ = wkv_t[h % 2][:, 1]
                wb = wkvB[h % 2]
                if src == "own":
                    def csl(kc, k0, n):
                        return ckvn[:, kc, N_META + k0:N_META + k0 + n]

                    def crd(k0, n):
                        return [ckvnB[kc][ti] for kc in range(C.KVC) for ti in tiles_of(N_META + k0, n)]
                else:
                    P.dma("sp", [(cg_t, x1d[src * C.KVR:(src + 1) * C.KVR, :].rearrange("(c p) t -> p c t", p=128))], "ld_cg", reads=[DX1d], writes=[cgB])

                    def csl(kc, k0, n):
                        return cg_t[:, kc, k0:k0 + n]

                    def crd(k0, n):
                        return [cgB]
                for k0 in range(0, S, 512):
                    n = min(512, S - k0)
                    ps = psm.next()
                    P.pe([mm(ps.ap[:, 0:n], wk[:, kc, :], csl(kc, k0, n), kc == 0, kc == C.KVC - 1) for kc in range(C.KVC)], reads=[wb] + crd(k0, n), writes=[ps])
                    headnorm(ps, 128, n, onesR, 128, cfc("g_kn", l), Kh_t[ib][:, k0:k0 + n], KhB[ib])
                for kb in range(C.NB):
                    ps = psm.next()
                    P.pe([mm(ps.ap[:, 0:128], csl(kc, kb * 128, 128), wv_[:, kc, :], kc == 0, kc == C.KVC - 1) for kc in range(C.KVC)],
                         reads=[wb] + crd(kb * 128, 128), writes=[ps])
                    P.op("act", i_act(Vh_t[ib][:, kb, :], ps.ap[:, 0:128], AF.Copy), reads=[ps], writes=[VhB[ib]])

            def mla_expand_meta(h):
                wk = wkv_t[h % 2][:, 0]
                wv_ = wkv_t[h % 2][:, 1]
                wb = wkvB[h % 2]
                mrd = [ckvnB[kc][ti] for kc in range(C.KVC) for ti in tiles_of(0, N_META)]
                ps = psm.next()
                P.pe([mm(ps.ap[:, 0:N_META], wk[:, kc, :], ckvn[:, kc, 0:N_META], kc == 0, kc == C.KVC - 1) for kc in range(C.KVC)], reads=[wb] + mrd, writes=[ps])
                headnorm(ps, 128, N_META, onesR, 128, cfc("g_kn", l), Khm[:, 0:N_META], KhmB)
                ps = psm.next()
                P.pe([mm(ps.ap[0:N_META, 0:128], ckvn[:, kc, 0:N_META], wv_[:, kc, :], kc == 0, kc == C.KVC - 1) for kc in range(C.KVC)], reads=[wb] + mrd, writes=[ps])
                P.op("act", i_act(Vhm[0:N_META, :], ps.ap[0:N_META, 0:128], AF.Copy), reads=[ps], writes=[VhmB])

            pend = {"p": None}

            def att_flush():
                if pend["p"] is not None:
                    (Ops, Lps, vv, nk, nq, rb, pm, first, last) = pend["p"]
                    P.pe([mm(Ops.ap[:, 0:nq], vv, pm.ap[0:nk, 0:nq], first, last)], reads=rb + [pm], writes=[Ops])
                    P.pe([mm(Lps.ap[:, 0:nq], onesb[0:nk, :], pm.ap[0:nk, 0:nq], first, last)], reads=[pm, CONST], writes=[Lps])
                    pend["p"] = None

            def att_step(h, qsl, nq, qrd, Ops, Lps, kT, krT, vv, nk, bias, mask_ap, rb, first, last):
                ibq = h % 2
                pss = psm.next()
                fns = [mm(pss.ap[0:nk, 0:nq], kT, qn_t[ibq][:, qsl:qsl + nq], True, False),
                       mm(pss.ap[0:nk, 0:nq], krT, qr_t[ibq][0:64, qsl:qsl + nq], False, mask_ap is None)]
                if mask_ap is not None:
                    fns.append(mm(pss.ap[0:nk, 0:nq], ident[0:nk, 0:nk], mask_ap, False, True))
                P.pe(fns, reads=rb + qrd + [CONST, KRG], writes=[pss])
                pm = nextpm()
                P.op("act", i_act(pm.ap[0:nk, 0:nq], pss.ap[0:nk, 0:nq], AF.Exp, bias=bias, scale=scale), reads=[pss, CONST], writes=[pm])
                att_flush()
                pend["p"] = (Ops, Lps, vv, nk, nq, rb, pm, first, last)

            def finish(h, Ops, Lps, qsl, nq):
                rl = nexttmp()
                P.op("dve", i_recip(rl.ap[:, 0:nq], Lps.ap[:, 0:nq]), reads=[Lps], writes=[rl])
                P.op("dve", i_tt(o_b[:, h, qsl:qsl + nq], Ops.ap[:, 0:nq], rl.ap[:, 0:nq], ALU.mult), reads=[Ops, rl],
                     writes=[obB[h][ti] for ti in tiles_of(qsl, nq)])

            def mla_attend(it_idx):
                h, sidx = items[it_idx]
                src = srcs_list[sidx]
                ib = it_idx % 2
                ibq = h % 2
                nsrc = len(srcs_list)
                mkrd = [KhmB, VhmB] + [krB[ti] for ti in tiles_of(0, N_META)]
                for qt in range(C.NQT):
                    qsl = blk(qt * C.QB)
                    qrd = [qnB[ibq][ti] for ti in tiles_of(qsl, QW)] + [qrB[ibq][ti] for ti in tiles_of(qsl, QW)]
                    Ops, Lps = acc[2 * qt], acc[2 * qt + 1]
                    if src == "own":
                        att_step(h, qsl, QW, qrd, Ops, Lps, Khm[:, 0:N_META], kr_all[0:64, 0:N_META], Vhm[0:N_META, :], N_META,
                                 cfc("zero", 0, rows=N_META), None, mkrd, True, False)
                        for kb in range((qt + 1) * C.QB):
                            r = kb - qt * C.QB
                            mask_ap = cbc("mm", r * C.QB * 128, QW) if r >= 0 else None
                            att_step(h, qsl, QW, qrd, Ops, Lps, Kh_t[ib][:, kb * 128:(kb + 1) * 128], kr_all[0:64, blk(kb):blk(kb) + 128],
                                     Vh_t[ib][:, kb, :], 128, cfc("zero"), mask_ap, [KhB[ib], VhB[ib]] + [krB[ti] for ti in tiles_of(blk(kb), 128)], False, False)
                    else:
                        for kb in range(C.NB):
                            lastk = (sidx == nsrc - 1) and (kb == C.NB - 1)
                            att_step(h, qsl, QW, qrd, Ops, Lps, Kh_t[ib][:, kb * 128:(kb + 1) * 128],
                                     kr_all[0:64, T + src * S + kb * 128:T + src * S + (kb + 1) * 128],
                                     Vh_t[ib][:, kb, :], 128, cfc("vis", src), None, [KhB[ib], VhB[ib]], False, lastk)
                if src == "own":
                    Om = psm.next()
                    Lm = psm.next()
                    qrd = [qnB[ibq][ti] for ti in tiles_of(0, N_META)] + [qrB[ibq][ti] for ti in tiles_of(0, N_META)]
                    att_step(h, 0, N_META, qrd, Om, Lm, Khm[:, 0:N_META], kr_all[0:64, 0:N_META], Vhm[0:N_META, :], N_META,
                             cfc("zero", 0, rows=N_META), cbc("cmcur", 0, N_META, rows=N_META), mkrd, True, True)
                    att_flush()
                    finish(h, Om, Lm, 0, N_META)

            nI = len(items)
            mla_q(0)
            mla_expand(0)
            mla_expand_meta(0)
            for it_idx in range(nI):
                h, sidx = items[it_idx]
                if it_idx + 1 < nI:
                    h2, s2 = items[it_idx + 1]
                    if s2 == 0:
                        mla_q(h2)
                    mla_expand(it_idx + 1)
                mla_attend(it_idx)
                att_flush()
                if sidx == len(srcs_list) - 1:
                    for qt in range(C.NQT):
                        finish(h, acc[2 * qt], acc[2 * qt + 1], blk(qt * C.QB), QW)
                    if it_idx + 1 < nI:
                        mla_expand_meta(h + 1)
            psm.unreserve(acc)
            AH.release(mM)
            dump("o_b", o_b, NMC, [b_ for r_ in obB for b_ in r_])
            branch_merge(1, o_b, obB, NMC, W["w_br_mla"])
            P.barrier()
            dump("merged", merged, DC, [b_ for r_ in mgB for b_ in r_])
            AC.release(mMC)

            ck("mla")
            AH.release(mH)
            P.dma("sp", [(h_flat[:, q * q4:(q + 1) * q4], hsp_d[:, q * q4:(q + 1) * q4]) for q in range(4)], "hld", reads=[], writes=allh)
            for dc in range(DC):
                wi = wq.plan(wcols(W["w_o"], l, 0, DC, dc * 128, 128))
                for ti, (s, n) in enumerate(TT):
                    po_ = psm.next()
                    proj_fm(po_, n, wq.get(wi), 0, 128, [merged[:, kc, s:s + n] for kc in range(DC)], [mgB[kc][ti] for kc in range(DC)])
                    P.op("dve", i_tt(hsl(dc, ti), po_.ap[:, 0:n], hsl(dc, ti), ALU.add), reads=[po_, hB[dc][ti]], writes=[hB[dc][ti]])
                wq.done(wi)
            P.barrier()
            AC.release(mC)

        try:
            ck("load")
            for l in range(L):
                ffn(l, W["ffn1_w_gate"], W["ffn1_w_up"], W["ffn1_w_down"], "g_ffn1")
                ck("ffn1")
                mixer(l)
                ck("mixer")
                ffn(l, W["ffn2_w_gate"], W["ffn2_w_up"], W["ffn2_w_down"], "g_ffn2")
        except _Stop as e_:
            P.barrier()
            if e_.args:
                return P, wq.plans, (AH.peak, AK.peak, AC.peak)

        yv = yT.rearrange("(c p) t -> p c t", p=128)
        step = max(1, DC // 4)
        ev = P.dma("sp", [(yv[:, c0:c0 + step, :], h_ap[:, c0:c0 + step, N_META:T]) for c0 in range(0, DC, step)], "st_y", reads=allh, writes=[])
        P.wait_on("sp", [ev])
        return P, wq.plans, (AH.peak, AK.peak, AC.peak)

    P1, script, _ = generate(None)
    P2, script2, peaks = generate(script)
    assert len(script2) == len(script)
    if getattr(cfg, "dbgdump", None):
        with open(cfg.dbgdump, "w") as f:
            for d in P2.dbg:
                f.write(repr(d) + "\n")
    P2.emit()
    return nc, stack, peaks


def weight_shapes(C):
    L, D = C.DEPTH, C.D
    return [("ffn1_w_gate", [L, D, C.DFF]), ("ffn1_w_up", [L, D, C.DFF]), ("ffn1_w_down", [L, C.DFF, D]),
            ("w_in", [L, D, C.IN_W]), ("mla_w_uq", [L, C.QR, C.MLA_H * 192]),
            ("mla_w_ukv", [L, C.KVR, C.MLA_H * 256]), ("w_br_swa", [L, C.SWA_OUT, D]),
            ("w_br_mla", [L, C.MLA_OUT, D]), ("w_br_ret", [L, C.RET_OUT, D]), ("w_o", [L, D, D]),
            ("ffn2_w_gate", [L, D, C.DFF]), ("ffn2_w_up", [L, D, C.DFF]), ("ffn2_w_down", [L, C.DFF, D])]


def run_cfg(cfg, inputs, trace=False):
    x = np.asarray(inputs["x"], np.float32)
    meta = np.asarray(inputs["meta_tokens"], np.float32)
    S = cfg.S
    wnames = [n for n, _ in weight_shapes(cfg)]
    wts = {n: np.ascontiguousarray(np.asarray(inputs[n], np.float32)) for n in wnames}
    metaT = np.ascontiguousarray(meta.T)
    consts = [host_consts(cfg, inputs, j) for j in range(cfg.GS)]
    in_maps = []
    for core in range(cfg.NCORE):
        b, j = divmod(core, cfg.GS)
        cf, cb, rope = consts[j]
        m = {"xT": np.ascontiguousarray(x[b, j * S:(j + 1) * S, :].T), "metaT": metaT, "cf": cf, "cb": cb, "rope": rope}
        m.update(wts)
        in_maps.append(m)
    nc, stack, peaks = build_program(cfg)
    with stack:
        res = run_bass_kernel_spmd(nc, in_maps, core_ids=list(range(cfg.NCORE)), trace=trace)
    out = np.zeros((cfg.NBATCH, cfg.GS * S, cfg.D), np.float32)
    for core in range(cfg.NCORE):
        b, j = divmod(core, cfg.GS)
        out[b, j * S:(j + 1) * S, :] = res.results[core]["yT"].T
    return out, res


def kernel(**inputs):
    out, _ = run_cfg(FULL, inputs)
    return out
```
